# Optimizing a Trainium2 kernel written in Bass

```python
import math
import jax, jax.numpy as jnp
from jax import lax
import numpy as np

D_MODEL = 1024
BATCH = 16
SEQ = 4096
DEPTH = 4

N_MIXERS = 3
CHUNK = 64
D_FF = 2816
LN_EPS = 1e-5
NORM_EPS = 1e-6
DN_ALPHA = (2 * DEPTH) ** 0.25
DN_BETA = (8 * DEPTH) ** -0.25

GLA_HEADS = 4
GLA_DK = D_MODEL // (2 * GLA_HEADS)
GLA_DV = D_MODEL // GLA_HEADS
GLA_KEY = GLA_HEADS * GLA_DK
GLA_VAL = GLA_HEADS * GLA_DV
GLA_RANK = 16
GLA_TAU = 16.0
GLA_SPLITS = (GLA_KEY, GLA_KEY, GLA_VAL, GLA_VAL, GLA_RANK)

RET_HEADS = 4
RET_DK = D_MODEL // RET_HEADS
RET_DV = 2 * D_MODEL // RET_HEADS
RET_KEY = RET_HEADS * RET_DK
RET_VAL = RET_HEADS * RET_DV
RET_SPLITS = (RET_KEY, RET_KEY, RET_VAL, RET_VAL)
ROPE_BASE = 10000.0

SSD_INNER = 2 * D_MODEL
SSD_HEADDIM = 64
SSD_HEADS = SSD_INNER // SSD_HEADDIM
SSD_GROUPS = 4
SSD_HPG = SSD_HEADS // SSD_GROUPS
SSD_STATE = 128
SSD_CONV = 4
SSD_CONV_DIM = SSD_INNER + 2 * SSD_GROUPS * SSD_STATE
SSD_SPLITS = (SSD_INNER, SSD_CONV_DIM, SSD_HEADS)

N_LAYERS_A = len(range(0, DEPTH, N_MIXERS))
N_LAYERS_B = len(range(1, DEPTH, N_MIXERS))
N_LAYERS_C = len(range(2, DEPTH, N_MIXERS))

kernel_name = 'hybrid_gla_retnet_ssd_macaron_deepnorm'


def split_cols(t, sizes):
    return jnp.split(t, np.cumsum(sizes)[:-1].tolist(), axis=-1)


def layer_norm(x, g, b):
    xf = x.astype(jnp.float32)
    mu = xf.mean(-1, keepdims=True)
    var = jnp.square(xf - mu).mean(-1, keepdims=True)
    return ((xf - mu) * lax.rsqrt(var + LN_EPS)).astype(x.dtype) * g + b


def rms_norm(x, g):
    xf = x.astype(jnp.float32)
    ms = jnp.square(xf).mean(-1, keepdims=True)
    return (xf * lax.rsqrt(ms + NORM_EPS)).astype(x.dtype) * g


def head_group_norm(x, g):
    xf = x.astype(jnp.float32)
    mu = xf.mean(-1, keepdims=True)
    var = jnp.square(xf - mu).mean(-1, keepdims=True)
    return ((xf - mu) * lax.rsqrt(var + NORM_EPS)).astype(x.dtype) * g


def swiglu(x, w_in, w_out):
    gate, up = jnp.split(x @ w_in, 2, axis=-1)
    return (jax.nn.silu(gate) * up) @ w_out


def to_chunks(t):
    b, l = t.shape[:2]
    t = t.reshape((b, l // CHUNK, CHUNK) + t.shape[2:])
    return jnp.moveaxis(t, 1, 0)


def from_chunks(t):
    t = jnp.moveaxis(t, 0, 1)
    return t.reshape((t.shape[0], t.shape[1] * t.shape[2]) + t.shape[3:])


def rotary(t):
    l, dh = t.shape[1], t.shape[-1]
    inv = ROPE_BASE ** (-jnp.arange(0, dh, 2, dtype=jnp.float32) / dh)
    ang = jnp.arange(l, dtype=jnp.float32)[:, None] * inv[None, :]
    cos = jnp.cos(ang)[None, :, None, :].astype(t.dtype)
    sin = jnp.sin(ang)[None, :, None, :].astype(t.dtype)
    t1, t2 = jnp.split(t, 2, axis=-1)
    return jnp.concatenate([t1 * cos - t2 * sin, t1 * sin + t2 * cos], axis=-1)


def gla_chunked(q, k, v, log_a):
    bsz = q.shape[0]
    mask = jnp.tril(jnp.ones((CHUNK, CHUNK), dtype=bool))

    def step(state, inp):
        qc, kc, vc, ac = inp
        cum = jnp.cumsum(ac, axis=1)
        tot = cum[:, -1]
        q_dec = qc * jnp.exp(cum)
        k_inv = kc * jnp.exp(-cum)
        k_end = kc * jnp.exp(tot[:, None] - cum)
        s = jnp.where(mask, jnp.einsum('bihd,bjhd->bhij', q_dec, k_inv), 0.0)
        o = jnp.einsum('bhij,bjhv->bihv', s, vc) + jnp.einsum('bihd,bhdv->bihv', q_dec, state)
        state = jnp.exp(tot)[..., None] * state + jnp.einsum('bjhd,bjhv->bhdv', k_end, vc)
        return state, o

    state0 = jnp.zeros((bsz, GLA_HEADS, GLA_DK, GLA_DV), jnp.float32)
    _, o = lax.scan(step, state0, (to_chunks(q), to_chunks(k), to_chunks(v), to_chunks(log_a)))
    return from_chunks(o).astype(v.dtype)


def gla_mixer(x, w_in, w_gate, b_gate, norm_g, w_out):
    bsz, l, _ = x.shape
    q, k, v, r, g_low = split_cols(x @ w_in, GLA_SPLITS)
    q = q.reshape(bsz, l, GLA_HEADS, GLA_DK) * GLA_DK ** -0.5
    k = k.reshape(bsz, l, GLA_HEADS, GLA_DK)
    v = v.reshape(bsz, l, GLA_HEADS, GLA_DV)
    log_a = jax.nn.log_sigmoid((g_low @ w_gate + b_gate).astype(jnp.float32)) / GLA_TAU
    log_a = log_a.reshape(bsz, l, GLA_HEADS, GLA_DK)
    o = rms_norm(gla_chunked(q, k, v, log_a), norm_g)
    return (o.reshape(bsz, l, GLA_VAL) * jax.nn.silu(r)) @ w_out


def retention_chunked(q, k, v):
    bsz = q.shape[0]
    log_g = jnp.log(1.0 - 2.0 ** (-5.0 - jnp.arange(RET_HEADS, dtype=jnp.float32)))
    idx = jnp.arange(CHUNK, dtype=jnp.float32)
    mask = jnp.tril(jnp.ones((CHUNK, CHUNK), dtype=bool))
    decay_intra = jnp.where(mask[None], jnp.exp((idx[:, None] - idx[None, :])[None] * log_g[:, None, None]), 0.0)
    decay_q = jnp.exp((idx + 1.0)[:, None] * log_g[None, :])[None, :, :, None]
    decay_k = jnp.exp((CHUNK - 1.0 - idx)[:, None] * log_g[None, :])[None, :, :, None]
    decay_chunk = jnp.exp(CHUNK * log_g)[None, :, None, None]

    def step(state, inp):
        qc, kc, vc = inp
        s = jnp.einsum('bihd,bjhd->bhij', qc, kc) * decay_intra
        o = jnp.einsum('bhij,bjhv->bihv', s, vc) + jnp.einsum('bihd,bhdv->bihv', qc, state) * decay_q
        state = decay_chunk * state + jnp.einsum('bjhd,bjhv->bhdv', kc * decay_k, vc)
        return state, o

    state0 = jnp.zeros((bsz, RET_HEADS, RET_DK, RET_DV), jnp.float32)
    _, o = lax.scan(step, state0, (to_chunks(q), to_chunks(k), to_chunks(v)))
    return from_chunks(o).astype(v.dtype)


def retention_mixer(x, w_in, norm_g, w_out):
    bsz, l, _ = x.shape
    q, k, v, g = split_cols(x @ w_in, RET_SPLITS)
    q = rotary(q.reshape(bsz, l, RET_HEADS, RET_DK))
    k = rotary(k.reshape(bsz, l, RET_HEADS, RET_DK)) * RET_DK ** -0.5
    v = v.reshape(bsz, l, RET_HEADS, RET_DV)
    o = head_group_norm(retention_chunked(q, k, v), norm_g)
    return (o.reshape(bsz, l, RET_VAL) * jax.nn.silu(g)) @ w_out


def causal_dwconv(t, w, b):
    ch = t.shape[-1]
    y = lax.conv_general_dilated(t, w[:, None, :].astype(t.dtype), window_strides=(1,),
                                 padding=[(SSD_CONV - 1, 0)],
                                 dimension_numbers=('NWC', 'WIO', 'NWC'),
                                 feature_group_count=ch)
    return y + b


def ssd_chunked(xs, dt, a, bm, cm):
    bsz = xs.shape[0]
    mask = jnp.tril(jnp.ones((CHUNK, CHUNK), dtype=bool))[None, :, :, None, None]

    def step(state, inp):
        xc, dtc, bc, cc = inp
        cum = jnp.cumsum(dtc * a, axis=1)
        seg = jnp.exp(jnp.where(mask, cum[:, :, None] - cum[:, None, :], -jnp.inf))
        cb = jnp.einsum('bign,bjgn->bijg', cc, bc)
        w = seg * cb[..., None] * dtc[:, None]
        y = jnp.einsum('bijge,bjgep->bigep', w, xc)
        y = y + jnp.einsum('bign,bgepn->bigep', cc, state) * jnp.exp(cum)[..., None]
        to_end = jnp.exp(cum[:, -1:] - cum) * dtc
        state = jnp.exp(cum[:, -1])[..., None, None] * state + jnp.einsum('bjgn,bjge,bjgep->bgepn', bc, to_end, xc)
        return state, y

    state0 = jnp.zeros((bsz, SSD_GROUPS, SSD_HPG, SSD_HEADDIM, SSD_STATE), jnp.float32)
    _, y = lax.scan(step, state0, (to_chunks(xs), to_chunks(dt), to_chunks(bm), to_chunks(cm)))
    return from_chunks(y).astype(xs.dtype)


def ssd_mixer(x, w_in, conv_w, conv_b, dt_bias, a_log, d_skip, norm_g, w_out):
    bsz, l, _ = x.shape
    z, xbc, dt_raw = split_cols(x @ w_in, SSD_SPLITS)
    xbc = jax.nn.silu(causal_dwconv(xbc, conv_w, conv_b))
    xs, bm, cm = split_cols(xbc, (SSD_INNER, SSD_GROUPS * SSD_STATE, SSD_GROUPS * SSD_STATE))
    xs = xs.reshape(bsz, l, SSD_GROUPS, SSD_HPG, SSD_HEADDIM)
    bm = bm.reshape(bsz, l, SSD_GROUPS, SSD_STATE)
    cm = cm.reshape(bsz, l, SSD_GROUPS, SSD_STATE)
    dt = jax.nn.softplus((dt_raw + dt_bias).astype(jnp.float32)).reshape(bsz, l, SSD_GROUPS, SSD_HPG)
    a = -jnp.exp(a_log.astype(jnp.float32)).reshape(SSD_GROUPS, SSD_HPG)
    y = ssd_chunked(xs, dt, a, bm, cm) + xs * d_skip.reshape(SSD_GROUPS, SSD_HPG)[..., None]
    y = y.reshape(bsz, l, SSD_INNER) * jax.nn.silu(z)
    y = rms_norm(y.reshape(bsz, l, SSD_GROUPS, SSD_INNER // SSD_GROUPS), norm_g.reshape(SSD_GROUPS, -1))
    return y.reshape(bsz, l, SSD_INNER) @ w_out


def _col_scale(sizes, scaled):
    return jnp.concatenate([jnp.full((s,), DN_BETA if i in scaled else 1.0, jnp.float32)
                            for i, s in enumerate(sizes)])


def setup_inputs(seed: int = 0) -> dict:
    key = jax.random.key(seed)
    ks = list(jax.random.split(key, 24))

    def nrm(i, shape, scale):
        return jax.random.normal(ks[i], shape, jnp.float32) * scale

    x = nrm(0, (BATCH, SEQ, D_MODEL), 1.0)
    ffn_w_in = nrm(1, (DEPTH, 2, D_MODEL, 2 * D_FF), DN_BETA * D_MODEL ** -0.5)
    ffn_w_out = nrm(2, (DEPTH, 2, D_FF, D_MODEL), DN_BETA * D_FF ** -0.5)
    ln_g = 1.0 + nrm(3, (DEPTH, 3, D_MODEL), 0.02)
    ln_b = nrm(4, (DEPTH, 3, D_MODEL), 0.02)
    gla_w_in = nrm(5, (N_LAYERS_A, D_MODEL, sum(GLA_SPLITS)), D_MODEL ** -0.5) * _col_scale(GLA_SPLITS, (2,))
    gla_w_gate = nrm(6, (N_LAYERS_A, GLA_RANK, GLA_KEY), GLA_RANK ** -0.5)
    gla_b_gate = nrm(7, (N_LAYERS_A, GLA_KEY), 0.1)
    gla_norm_g = 1.0 + nrm(8, (N_LAYERS_A, GLA_DV), 0.02)
    gla_w_out = nrm(9, (N_LAYERS_A, GLA_VAL, D_MODEL), DN_BETA * GLA_VAL ** -0.5)
    ret_w_in = nrm(10, (N_LAYERS_B, D_MODEL, sum(RET_SPLITS)), D_MODEL ** -0.5) * _col_scale(RET_SPLITS, (2,))
    ret_norm_g = 1.0 + nrm(11, (N_LAYERS_B, RET_DV), 0.02)
    ret_w_out = nrm(12, (N_LAYERS_B, RET_VAL, D_MODEL), DN_BETA * RET_VAL ** -0.5)
    ssd_cols = (SSD_INNER, SSD_INNER, SSD_CONV_DIM - SSD_INNER, SSD_HEADS)
    ssd_w_in = nrm(13, (N_LAYERS_C, D_MODEL, sum(SSD_SPLITS)), D_MODEL ** -0.5) * _col_scale(ssd_cols, (1,))
    ssd_conv_w = nrm(14, (N_LAYERS_C, SSD_CONV, SSD_CONV_DIM), SSD_CONV ** -0.5)
    ssd_conv_b = nrm(15, (N_LAYERS_C, SSD_CONV_DIM), 0.02)
    dt0 = jnp.exp(jax.random.uniform(ks[16], (N_LAYERS_C, SSD_HEADS), jnp.float32,
                                     math.log(1e-3), math.log(1e-1)))
    ssd_dt_bias = dt0 + jnp.log(-jnp.expm1(-dt0))
    ssd_a_log = jnp.log(jax.random.uniform(ks[17], (N_LAYERS_C, SSD_HEADS), jnp.float32, 1.0, 16.0))
    ssd_d = 1.0 + nrm(18, (N_LAYERS_C, SSD_HEADS), 0.02)
    ssd_norm_g = 1.0 + nrm(19, (N_LAYERS_C, SSD_INNER), 0.02)
    ssd_w_out = nrm(20, (N_LAYERS_C, SSD_INNER, D_MODEL), DN_BETA * SSD_INNER ** -0.5)
    return {'x': x, 'ffn_w_in': ffn_w_in, 'ffn_w_out': ffn_w_out, 'ln_g': ln_g, 'ln_b': ln_b,
            'gla_w_in': gla_w_in, 'gla_w_gate': gla_w_gate, 'gla_b_gate': gla_b_gate,
            'gla_norm_g': gla_norm_g, 'gla_w_out': gla_w_out,
            'ret_w_in': ret_w_in, 'ret_norm_g': ret_norm_g, 'ret_w_out': ret_w_out,
            'ssd_w_in': ssd_w_in, 'ssd_conv_w': ssd_conv_w, 'ssd_conv_b': ssd_conv_b,
            'ssd_dt_bias': ssd_dt_bias, 'ssd_a_log': ssd_a_log, 'ssd_d': ssd_d,
            'ssd_norm_g': ssd_norm_g, 'ssd_w_out': ssd_w_out}


def reference(x, ffn_w_in, ffn_w_out, ln_g, ln_b,
              gla_w_in, gla_w_gate, gla_b_gate, gla_norm_g, gla_w_out,
              ret_w_in, ret_norm_g, ret_w_out,
              ssd_w_in, ssd_conv_w, ssd_conv_b, ssd_dt_bias, ssd_a_log, ssd_d, ssd_norm_g, ssd_w_out):
    h = x
    for i in range(DEPTH):
        h = layer_norm(DN_ALPHA * h + 0.5 * swiglu(h, ffn_w_in[i, 0], ffn_w_out[i, 0]), ln_g[i, 0], ln_b[i, 0])
        kind, j = i % N_MIXERS, i // N_MIXERS
        if kind == 0:
            m = gla_mixer(h, gla_w_in[j], gla_w_gate[j], gla_b_gate[j], gla_norm_g[j], gla_w_out[j])
        elif kind == 1:
            m = retention_mixer(h, ret_w_in[j], ret_norm_g[j], ret_w_out[j])
        else:
            m = ssd_mixer(h, ssd_w_in[j], ssd_conv_w[j], ssd_conv_b[j], ssd_dt_bias[j],
                          ssd_a_log[j], ssd_d[j], ssd_norm_g[j], ssd_w_out[j])
        h = layer_norm(DN_ALPHA * h + m, ln_g[i, 1], ln_b[i, 1])
        h = layer_norm(DN_ALPHA * h + 0.5 * swiglu(h, ffn_w_in[i, 1], ffn_w_out[i, 1]), ln_g[i, 2], ln_b[i, 2])
    return h
```

```python
import math
import numpy as np
import concourse.bass as bass
import concourse.mybir as mybir
from concourse.bass_utils import run_bass_kernel_spmd

F32 = mybir.dt.float32
BF16 = mybir.dt.bfloat16
AF = mybir.ActivationFunctionType
ALU = mybir.AluOpType

D = 1024
DEPTH = 4
DFF = 2816
NFC = DFF // 128
T = 512
J = T // 128
ALPHA = (2 * DEPTH) ** 0.25
LN_EPS_EFF = 1e-5 / (ALPHA * ALPHA)
NCORES = 8

COMPUTE = ('pe', 'act', 'dve', 'pool')


class Prog:
    def __init__(self):
        self.ops = []
        self.last_w = {}
        self.readers = {}
        self.chan_count = {}
        self.fence_last = {}
        self.fence_pending = set()
        self.last_on = {}
        self.out_chans = set()

    def add(self, eng, fn, reads=(), writes=(), chan=None, after=()):
        idx = len(self.ops)
        deps = {}
        def dep(i):
            o = self.ops[i]
            if o['chan'] is not None:
                deps[i] = self.chan_count[o['chan']]
            else:
                deps[i] = None
        for i in after:
            dep(i)
        for r in reads:
            if r in self.last_w:
                dep(self.last_w[r])
            if r.startswith("pb") or r.startswith("ptr"):
                for i in self.readers.get(r, ()):
                    if self.ops[i]['eng'] != eng:
                        dep(i)
        for w in writes:
            if w in self.last_w:
                dep(self.last_w[w])
            for i in self.readers.get(w, ()):
                dep(i)
        if eng in self.fence_pending:
            self.fence_pending.discard(eng)
            for e, i in self.fence_last.items():
                if e != eng:
                    dep(i)
        for r in reads:
            self.readers.setdefault(r, []).append(idx)
        for w in writes:
            self.last_w[w] = idx
            self.readers[w] = []
        op = dict(eng=eng, fn=fn, deps=deps, chan=chan, mark=False, cnt=0)
        if chan is not None:
            self.chan_count[chan] = self.chan_count.get(chan, 0) + 16
        self.ops.append(op)
        if chan is None:
            self.last_on[eng] = idx
        return idx

    def fence(self):
        self.fence_last = dict(self.last_on)
        self.fence_pending = set(COMPUTE)

    def emit(self, nc):
        ops = self.ops
        for X in ops:
            for i in X['deps']:
                Dp = ops[i]
                if Dp['chan'] is None and not (Dp['eng'] == 'pe' and X['eng'] == 'pe' and X['chan'] is None):
                    Dp['mark'] = True
        cnt = {}
        for X in ops:
            if X['chan'] is None and X['mark']:
                cnt[X['eng']] = cnt.get(X['eng'], 0) + 1
                X['cnt'] = cnt[X['eng']]
        chans = sorted(self.chan_count)
        import contextlib
        with contextlib.ExitStack() as es:
            sems = {}
            for e in ('pe', 'act', 'dve', 'pool'):
                sems[('eng', e)] = es.enter_context(nc.semaphore("s_" + e))
            for c in chans:
                sems[('chan', c)] = es.enter_context(nc.semaphore("c_" + c))
            block = es.enter_context(nc.Block())
            by_eng = {}
            for X in ops:
                by_eng.setdefault(X['eng'], []).append(X)

            def run(E, eng):
                waited = {}
                for X in by_eng.get(E, []):
                    waits = {}
                    for i, cv in X['deps'].items():
                        Dp = ops[i]
                        if Dp['chan'] is not None:
                            key, val = ('chan', Dp['chan']), cv
                        elif Dp['eng'] == 'pe' and E == 'pe' and X['chan'] is None:
                            continue
                        else:
                            key, val = ('eng', Dp['eng']), Dp['cnt']
                        if val > waits.get(key, 0):
                            waits[key] = val
                    for key, val in waits.items():
                        if val > waited.get(key, 0):
                            eng.wait_ge(sems[key], val)
                            waited[key] = val
                    ins = X['fn'](eng)
                    if X['chan'] is not None:
                        ins.then_inc(sems[('chan', X['chan'])], 16)
                    elif X['mark']:
                        ins.then_inc(sems[('eng', E)], 1)
                if E == 'sp':
                    for c in sorted(self.out_chans):
                        eng.wait_ge(sems[('chan', c)], self.chan_count[c])

            @block.tensor
            def _(e):
                run('pe', e)

            @block.scalar
            def _(e):
                run('act', e)

            @block.vector
            def _(e):
                run('dve', e)

            @block.gpsimd
            def _(e):
                run('pool', e)

            @block.sync
            def _(e):
                run('sp', e)


class Rot:
    def __init__(self, items):
        self.items = items
        self.i = 0

    def next(self):
        it = self.items[self.i % len(self.items)]
        self.i += 1
        return it


class WStream:
    def __init__(self, P, name, slots, plan):
        self.P, self.name, self.slots, self.plan = P, name, slots, plan
        self.issued = 0
        self.used = 0
        for _ in range(len(slots)):
            self._issue()

    def _issue(self):
        if self.issued >= len(self.plan):
            return
        n = self.issued
        self.issued += 1
        s = n % len(self.slots)
        buf = self.slots[s]
        res = "%s%d" % (self.name, s)
        g = self.plan[n]
        dst, src = g['load'](buf)
        self.P.add('sp', (lambda e, d=dst, s_=src: e.dma_start(out=d, in_=s_)),
                   writes=[res], chan=res, after=g['after'])

    def get(self, tag):
        n = self.used
        assert self.plan[n]['tag'] == tag, (self.plan[n]['tag'], tag)
        s = n % len(self.slots)
        return self.slots[s], "%s%d" % (self.name, s)

    def done(self):
        self.used += 1
        self._issue()


GLA_H, GLA_DK, GLA_DV = 4, 128, 256
RET_H, RET_DK, RET_DV = 4, 256, 512
SSD_G, SSD_HPG, SSD_P, SSD_N = 4, 8, 64, 128
MIX = ['gla', 'ret', 'ssd', 'gla']
MIXIDX = [0, 0, 0, 1]
RET_GAMMA = [1.0 - 2.0 ** (-5.0 - hh) for hh in range(RET_H)]
C_ID, C_TRI, C_U, C_NEG4, C_ONES, C_DEC, C_RC, C_RC2, C_RDK, C_END = 0, 128, 256, 384, 896, 1024, 1536, 1540, 1544, 1548
NEG = -30000.0


def make_consts():
    c = np.zeros((128, C_END), np.float32)
    r = np.arange(128)[:, None].astype(np.float64)
    i = np.arange(128)[None, :].astype(np.float64)
    c[:, C_ID:C_ID + 128] = np.eye(128)
    c[:, C_TRI:C_TRI + 128] = (r <= i)
    c[:, C_U:C_U + 128] = (r > i)
    for q in range(4):
        c[:, C_NEG4 + q * 128:C_NEG4 + (q + 1) * 128] = np.where(i < r, NEG, 0.0)
    c[:, C_ONES:C_ONES + 128] = 1.0
    for hh, g in enumerate(RET_GAMMA):
        c[:, C_DEC + hh * 128:C_DEC + (hh + 1) * 128] = np.where(i >= r, g ** (-(r + 1.0)), 0.0) / 16.0
        c[:, C_RC + hh] = g ** (np.arange(128) + 1.0)
        c[:, C_RC2 + hh] = (g ** (np.arange(128) + 1.0)) ** 2
        c[:, C_RDK + hh] = g ** (127.0 - np.arange(128)) / 16.0
    return c


def make_rope(seq_len):
    dh = RET_DK
    inv = (np.float32(10000.0) ** (-np.arange(0, dh, 2, dtype=np.float32) / np.float32(dh))).astype(np.float32)
    ang = (np.arange(seq_len, dtype=np.float32)[None, :] * inv[:, None]).astype(np.float32)
    return np.cos(ang).astype(np.float32), np.sin(ang).astype(np.float32)


def build_program(n_seq, seq_len, layers=None, stages=None):
    ntok = n_seq * seq_len
    ntile_seq = seq_len // T
    layers = list(range(DEPTH)) if layers is None else layers
    stages = stages or ['ffn1', 'mix', 'ffn2']
    nc = bass.Bass("TRN2", target_bir_lowering=False)
    P = Prog()

    def din(name, shape, dt=F32):
        return nc.dram_tensor(name, list(shape), dt, kind="ExternalInput").ap()

    x_d = din("x", [ntok, D])
    y_d = nc.dram_tensor("y", [ntok, D], F32, kind="ExternalOutput").ap()
    ffn_w_in = din("ffn_w_in", [DEPTH, 2, D, 2 * DFF])
    ffn_w_out = din("ffn_w_out", [DEPTH, 2, DFF, D])
    ln_gb = din("ln_gb", [DEPTH * 3, 2, 128, D])
    cst_d = din("cst", [128, C_END])
    gla_w_in = din("gla_w_in", [2, D, 3088])
    gla_wg = din("gla_wg", [2, 17, 512])
    gla_ng = din("gla_ng", [2, 128, 256])
    gla_w_out = din("gla_w_out", [2, 1024, D])
    ret_w_in = din("ret_w_in", [1, D, 6144])
    ret_ng = din("ret_ng", [128, 512])
    ret_w_out = din("ret_w_out", [1, 2048, D])
    rope_c = din("rope_cos", [128, seq_len])
    rope_s = din("rope_sin", [128, seq_len])
    ssd_w_in = din("ssd_w_in", [1, D, 5152])
    ssd_cw = din("ssd_cw", [128, 24, 4])
    ssd_cb = din("ssd_cb", [128, 24])
    ssd_vec = din("ssd_vec", [128, 3, 32])
    ssd_ngd = din("ssd_ngd", [4, 128, 2, 512])
    ssd_w_out = din("ssd_w_out", [1, 2048, D])

    import contextlib
    es = contextlib.ExitStack()

    def sb(name, shape, dt):
        return es.enter_context(nc.sbuf_tensor(name, list(shape), dt))

    def ps(name, shape, dt):
        return es.enter_context(nc.psum_tensor(name, list(shape), dt))

    h = sb("h", [128, J, D], F32)
    xT = sb("xT", [128, 8, T], BF16)
    actT = sb("actT", [128, NFC, T], BF16)
    win_slots = [sb("win%d" % i, [128, 8, 768], BF16) for i in range(2)]
    wout_slots = [sb("wout%d" % i, [128, 8, 512], BF16) for i in range(2)]
    lng = sb("lng", [128, 2, D], F32)
    hb = [sb("hb%d" % i, [128, D], BF16) for i in range(2)]
    silu_s = [sb("silu%d" % i, [128, T], F32) for i in range(2)]
    stats = [sb("stats%d" % i, [128, 16], F32) for i in range(2)]
    ep_st = sb("ep_st", [128, J, 12], F32)
    ep_mv = sb("ep_mv", [128, J, 2], F32)
    ep_rs = sb("ep_rs", [128, J], F32)
    cst = sb("cst_sb", [128, C_END], F32)
    ident = sb("ident_b", [128, 128], BF16)
    gla_state = [sb("gla_st%d" % i, [128, GLA_H, GLA_DV], F32) for i in range(2)]
    gla_state_bf = [sb("gla_stb%d" % i, [128, GLA_H, GLA_DV], BF16) for i in range(2)]
    gla_wg_sb = [sb("gla_wg%d" % i, [17, 512], BF16) for i in range(2)]
    gla_ng_sb = [sb("gla_ng%d" % i, [128, 256], F32) for i in range(2)]
    ret_state = sb("ret_st", [128, 8, RET_DV], F32)
    ret_state_bf = sb("ret_stb", [128, 8, RET_DV], BF16)
    ret_ng_sb = sb("ret_ng_sb", [128, 512], F32)
    rope_sb = sb("rope_sb", [128, 2, T], F32)
    ssd_state = sb("ssd_st", [128, SSD_G, 512], F32)
    ssd_state_bf = sb("ssd_stb", [128, SSD_G, 512], BF16)
    ssd_carry = sb("ssd_carry", [128, 24, 4], F32)
    ssd_cw_sb = sb("ssd_cw_sb", [128, 24, 4], F32)
    ssd_cb_sb = sb("ssd_cb_sb", [128, 24], F32)
    ssd_vec_sb = sb("ssd_vec_sb", [128, 3, 32], F32)
    ssd_nega = sb("ssd_nega", [128, 32], F32)
    ssd_ngd_sb = sb("ssd_ngd_sb", [128, 2, 512], F32)
    AR_WORDS = 7808
    arena_t = sb("arena", [128, AR_WORDS], F32)

    class Arena:
        def __init__(self):
            self.off = 0

        def take(self, free, dt, parts=128):
            n = 1
            for f in free:
                n *= f
            words = n if dt == F32 else (n + 1) // 2
            a = arena_t[0:parts, self.off:self.off + words]
            self.off += words
            assert self.off <= AR_WORDS, self.off
            if dt != F32:
                a = a.bitcast(BF16)
            if len(free) == 2:
                a = a.rearrange("p (a b) -> p a b", a=free[0])
            elif len(free) == 3:
                a = a.rearrange("p (a b c) -> p a b c", a=free[0], b=free[1])
            return a

    pbank = [ps("pb%d" % i, [128, 512], F32) for i in range(6)]
    ptr_all = [ps("ptr_all%d" % i, [128, 8, 128], BF16) for i in range(2)]
    ptr = [ptr_all[0][:, 0:4, :], ptr_all[1][:, 0:4, :]]
    proj_ps = Rot([(pbank[i], "pb%d" % i) for i in range(4)])
    acc_ps = [(pbank[i], "pb%d" % i) for i in (4, 5, 2, 3)]
    tr_ps = Rot([(ptr[i], "ptr%d" % i) for i in range(2)])
    hb_r = Rot([(hb[i], "hb%d" % i) for i in range(2)])
    silu_r = Rot([(silu_s[i], "silu%d" % i) for i in range(2)])
    stats_r = Rot([(stats[i], "stats%d" % i) for i in range(2)])

    def cs(c0, n=128):
        return cst[:, c0:c0 + n]

    def colgrp(w2d, pieces, tag):
        dmas = []
        for (d0, s0, n) in pieces:
            src = w2d[:, s0:s0 + n].rearrange("(kc p) n -> p kc n", p=128)
            dmas.append((lambda b, d0=d0, n=n: b[:, :, d0:d0 + n], src))
        return dict(tag=tag, dmas=dmas, ncols=max(d0 + n for (d0, s0, n) in pieces))

    def ffn_in_groups(l, s):
        groups = []
        for g0 in range(0, NFC, 3):
            n = min(3, NFC - g0)
            groups.append(colgrp(ffn_w_in[l, s], [(0, g0 * 128, n * 128), (n * 128, DFF + g0 * 128, n * 128)], ("ffn_in", l, s, g0)))
        return groups

    def out_groups(w2d, nk, tag):
        groups = []
        for hf in range(2):
            src = w2d[:, hf * 512:(hf + 1) * 512].rearrange("(fc p) n -> p fc n", p=128)
            for k0 in range(0, nk, 8):
                n = min(8, nk - k0)
                groups.append(dict(tag=(tag, hf, k0), dmas=[(lambda b, n=n: b[:, 0:n, :], src[:, k0:k0 + n, :])], nk=n))
        return groups

    def mix_in_groups(l):
        kind, mi = MIX[l], MIXIDX[l]
        gs = []
        if kind == 'gla':
            w = gla_w_in[mi]
            gs.append(colgrp(w, [(0, 3072, 16)], ("gla_g", l)))
            for hh in range(GLA_H):
                gs.append(colgrp(w, [(0, hh * 128, 128), (128, 512 + hh * 128, 128), (256, 1024 + hh * 256, 256),
                                     (512, 2048 + hh * 256, 256)], ("gla_h", l, hh)))
        elif kind == 'ret':
            w = ret_w_in[0]
            for hh in range(RET_H):
                gs.append(colgrp(w, [(0, 4096 + hh * 512, 512)], ("ret_g", l, hh)))
                gs.append(colgrp(w, [(0, hh * 256, 256), (256, 1024 + hh * 256, 256)], ("ret_qk", l, hh)))
                gs.append(colgrp(w, [(0, 2048 + hh * 512, 512)], ("ret_v", l, hh)))
        else:
            w = ssd_w_in[0]
            gs.append(colgrp(w, [(0, 5120, 32)], ("ssd_dt", l)))
            for g in range(SSD_G):
                gs.append(colgrp(w, [(0, g * 512, 512)], ("ssd_z", l, g)))
                gs.append(colgrp(w, [(0, 2048 + g * 512, 512), (512, 4096 + g * 128, 128), (640, 4608 + g * 128, 128)], ("ssd_x", l, g)))
        return gs

    def mix_out_groups(l):
        kind, mi = MIX[l], MIXIDX[l]
        if kind == 'gla':
            return out_groups(gla_w_out[mi], 8, ("mix_out", l))
        if kind == 'ret':
            return out_groups(ret_w_out[0], 16, ("mix_out", l))
        return out_groups(ssd_w_out[0], 16, ("mix_out", l))

    STG = {'ffn1': 0, 'mix': 1, 'ffn2': 2}

    def stage_groups(l, st):
        if st == 'ffn1':
            return ffn_in_groups(l, 0), out_groups(ffn_w_out[l, 0], NFC, ("ffn_out", l, 0))
        if st == 'ffn2':
            return ffn_in_groups(l, 1), out_groups(ffn_w_out[l, 1], NFC, ("ffn_out", l, 1))
        return mix_in_groups(l), mix_out_groups(l)

    stage_list = [(l, st) for l in layers for st in stages]
    n_in = sum(len(stage_groups(l, st)[0]) for l, st in stage_list)
    n_out = sum(len(stage_groups(l, st)[1]) for l, st in stage_list)
    scr_in = nc.dram_tensor("scr_in", [n_in, 128, 8 * 768], BF16).ap()
    scr_out = nc.dram_tensor("scr_out", [n_out, 128, 8 * 512], BF16).ap()
    stage_plans = {}
    ii = oi = 0
    for (l, st) in stage_list:
        gi, go = stage_groups(l, st)
        chan = "prep%d" % (l * 3 + STG[st])
        last = None
        pin, pout = [], []
        for g in gi:
            img = scr_in[ii].rearrange("p (kc n) -> p kc n", kc=8)
            ncols = 0
            for (dst_fn, src) in g['dmas']:
                last = P.add('pool', (lambda e, d=dst_fn(img), s_=src: e.dma_start(out=d, in_=s_)), chan=chan)
            ncols = g['ncols']
            pin.append(dict(tag=g['tag'], img=img, ncols=ncols))
            ii += 1
        for g in go:
            img = scr_out[oi].rearrange("p (kc n) -> p kc n", kc=8)
            for (dst_fn, src) in g['dmas']:
                last = P.add('pool', (lambda e, d=dst_fn(img), s_=src: e.dma_start(out=d, in_=s_)), chan=chan)
            pout.append(dict(tag=g['tag'], img=img, nk=g['nk']))
            oi += 1
        for g in pin:
            g['after'] = [last]
            g['load'] = (lambda buf, g=g: (buf[:, :, 0:g['ncols']], g['img'][:, :, 0:g['ncols']]))
        for g in pout:
            g['after'] = [last]
            g['load'] = (lambda buf, g=g: (buf[:, 0:g['nk'], :], g['img'][:, 0:g['nk'], :]))
        stage_plans[(l, st)] = (pin, pout)

    in_plan, out_plan = [], []
    for sq in range(n_seq):
        for tt in range(ntile_seq):
            for (l, st) in stage_list:
                in_plan += stage_plans[(l, st)][0]
                out_plan += stage_plans[(l, st)][1]

    P.add('sp', lambda e: e.dma_start(out=cst[:], in_=cst_d[:, :]), writes=["cst"], chan="k_cst")
    P.add('dve', lambda e: e.tensor_copy(out=ident[:], in_=cst[:, C_ID:C_ID + 128]), reads=["cst"], writes=["ident"])
    for i in range(2):
        P.add('pool', lambda e, i=i: e.dma_start(out=gla_wg_sb[i][:], in_=gla_wg[i]), writes=["gla_wg%d" % i], chan="k_wg%d" % i)
        P.add('sp', lambda e, i=i: e.dma_start(out=gla_ng_sb[i][:], in_=gla_ng[i]), writes=["gla_ng%d" % i], chan="k_ng%d" % i)
    P.add('sp', lambda e: e.dma_start(out=ret_ng_sb[:], in_=ret_ng[:, :]), writes=["ret_ng"], chan="k_rng")
    P.add('sp', lambda e: e.dma_start(out=ssd_cw_sb[:], in_=ssd_cw[:, :, :]), writes=["ssd_cw"], chan="k_cw")
    P.add('sp', lambda e: e.dma_start(out=ssd_cb_sb[:], in_=ssd_cb[:, :]), writes=["ssd_cb"], chan="k_cb")
    P.add('sp', lambda e: e.dma_start(out=ssd_vec_sb[:], in_=ssd_vec[:, :, :]), writes=["ssd_vec"], chan="k_vec")
    P.add('act', lambda e: e.activation(out=ssd_nega[:], in_=ssd_vec_sb[:, 1, :], func=AF.Exp), reads=["ssd_vec"], writes=["ssd_nega"])
    P.add('dve', lambda e: e.tensor_scalar_mul(out=ssd_nega[:], in0=ssd_nega[:], scalar1=-1.0), reads=["ssd_nega"], writes=["ssd_nega"])

    win = WStream(P, "win", win_slots, in_plan)
    wout = WStream(P, "wout", wout_slots, out_plan)

    def transposes(src_bf, src_res, nblk, dst_fn, dst_res):
        for b0 in range(0, nblk, 4):
            n = min(4, nblk - b0)
            pt, pres = tr_ps.next()
            for q in range(n):
                P.add('pe', lambda e, pt=pt, q=q, b=b0 + q: e.transpose(pt[:, q, :], src_bf[:, b * 128:(b + 1) * 128], ident[:]),
                      reads=[src_res, "ident"], writes=[pres])
            P.add('act', lambda e, pt=pt, b0=b0, n=n: e.copy(out=dst_fn(b0, n), in_=pt[:, 0:n, :]),
                  reads=[pres], writes=[dst_res])

    def transpose_to_xT(src_bf, src_res, j):
        transposes(src_bf, src_res, 8, lambda b0, n: xT[:, b0:b0 + n, j * 128:(j + 1) * 128], "xT%d" % j)

    def load_ln(idx):
        P.add('sp', lambda e: e.dma_start(out=lng[:], in_=ln_gb[idx].rearrange("a p d -> p a d")),
              writes=["lng"], chan="lng")

    def rstd_from_var(var_ap, out_ap, res, eps):
        P.add('dve', lambda e: e.tensor_scalar_add(out=out_ap, in0=var_ap, scalar1=eps), reads=[res], writes=[res])
        P.add('act', lambda e: e.sqrt(out=out_ap, in_=out_ap), reads=[res], writes=[res])
        P.add('dve', lambda e: e.reciprocal(out=out_ap, in_=out_ap), reads=[res], writes=[res])

    def epi_stats(j):
        hres = "h%d" % j
        for hf in range(2):
            P.add('dve', lambda e, hf=hf: e.bn_stats(out=ep_st[:, j, hf * 6:(hf + 1) * 6], in_=h[:, j, hf * 512:(hf + 1) * 512]),
                  reads=[hres], writes=["ep_st%d" % j])
        P.add('dve', lambda e: e.bn_aggr(out=ep_mv[:, j, :], in_=ep_st[:, j, :]), reads=["ep_st%d" % j], writes=["ep_mv"])

    def epi_finish():
        P.add('dve', lambda e: e.tensor_scalar_add(out=ep_rs[:], in0=ep_mv[:, :, 1], scalar1=float(LN_EPS_EFF)), reads=["ep_mv"], writes=["ep_rs"])
        P.add('act', lambda e: e.sqrt(out=ep_rs[:], in_=ep_rs[:]), reads=["ep_rs"], writes=["ep_rs"])
        P.add('dve', lambda e: e.reciprocal(out=ep_rs[:], in_=ep_rs[:]), reads=["ep_rs"], writes=["ep_rs"])
        for j in range(J):
            hres = "h%d" % j
            P.add('dve', lambda e, j=j: e.tensor_scalar(out=h[:, j, :], in0=h[:, j, :], scalar1=ep_mv[:, j, 0:1], scalar2=ep_rs[:, j:j + 1],
                                                       op0=ALU.subtract, op1=ALU.mult), reads=["ep_mv", "ep_rs", hres], writes=[hres])
            P.add('pool', lambda e, j=j: e.tensor_tensor(out=h[:, j, :], in0=h[:, j, :], in1=lng[:, 0, :], op=ALU.mult),
                  reads=[hres, "lng"], writes=[hres])
            P.add('pool', lambda e, j=j: e.tensor_tensor(out=h[:, j, :], in0=h[:, j, :], in1=lng[:, 1, :], op=ALU.add),
                  reads=[hres, "lng"], writes=[hres])
            hbt, hbres = hb_r.next()
            P.add('act', lambda e, hbt=hbt, j=j: e.copy(out=hbt[:], in_=h[:, j, :]), reads=[hres], writes=[hbres])
            transpose_to_xT(hbt, hbres, j)

    XT_ALL = ["xT%d" % j for j in range(J)]

    def out_proj(nk, tag, lnidx, coef):
        load_ln(lnidx)
        for hf in range(2):
            k0s = list(range(0, nk, 8))
            for k0 in k0s:
                n = min(8, nk - k0)
                wb, wres = wout.get((tag, hf, k0))
                for j in range(J):
                    mp, mres = acc_ps[j]
                    for kk in range(n):
                        kc = k0 + kk
                        P.add('pe', lambda e, mp=mp, kc=kc, kk=kk, j=j, wb=wb: e.matmul(
                            mp[:], lhsT=actT[:, kc, j * 128:(j + 1) * 128], rhs=wb[:, kk, :], start=(kc == 0), stop=(kc == nk - 1)),
                            reads=["actT", wres], writes=[mres])
                wout.done()
            for j in range(J):
                mp, mres = acc_ps[j]
                P.add('dve', lambda e, mp=mp, j=j, hf=hf: e.scalar_tensor_tensor(
                    out=h[:, j, hf * 512:(hf + 1) * 512], in0=mp[:], scalar=float(coef / ALPHA),
                    in1=h[:, j, hf * 512:(hf + 1) * 512], op0=ALU.mult, op1=ALU.add),
                    reads=[mres, "h%d" % j], writes=["h%d" % j])
                if hf == 1:
                    epi_stats(j)
        epi_finish()

    def proj_fm(pp, pres, wb, wres, c0, m=128):
        for kc in range(8):
            P.add('pe', lambda e, kc=kc: e.matmul(pp[0:m, :], lhsT=wb[:, kc, c0:c0 + m], rhs=xT[:, kc, :], start=(kc == 0), stop=(kc == 7)),
                  reads=[wres] + XT_ALL, writes=[pres])

    def proj_tm(pp, pres, wb, wres, j, c0, n):
        for kc in range(8):
            P.add('pe', lambda e, kc=kc: e.matmul(pp[:, 0:n], lhsT=xT[:, kc, j * 128:(j + 1) * 128], rhs=wb[:, kc, c0:c0 + n],
                                                  start=(kc == 0), stop=(kc == 7)),
                  reads=[wres, "xT%d" % j], writes=[pres])

    def ffn(l, s):
        for g0 in range(0, NFC, 3):
            n = min(3, NFC - g0)
            wb, wres = win.get(("ffn_in", l, s, g0))
            for c in range(n):
                pg, gres = proj_ps.next()
                pu, ures = proj_ps.next()
                proj_fm(pg, gres, wb, wres, c * 128)
                proj_fm(pu, ures, wb, wres, n * 128 + c * 128)
                sl, slres = silu_r.next()
                P.add('act', lambda e, sl=sl, pg=pg: e.activation(out=sl[:], in_=pg[:], func=AF.Silu), reads=[gres], writes=[slres])
                fc = g0 + c
                P.add('dve', lambda e, sl=sl, pu=pu, fc=fc: e.tensor_tensor(out=actT[:, fc, :], in0=sl[:], in1=pu[:], op=ALU.mult),
                      reads=[slres, ures], writes=["actT"])
            win.done()
        out_proj(NFC, ("ffn_out", l, s), l * 3 + (0 if s == 0 else 2), 0.5)

    def A(eng, fn, reads, writes):
        P.add(eng, fn, reads=reads, writes=writes)

    def gla(l, first_tile):
        mi = MIXIDX[l]
        P.fence()
        ar = Arena()
        Lb = ar.take([J, 512], F32)
        E3 = ar.take([J, 512], F32)
        E1 = ar.take([T], F32)
        E2 = ar.take([T], F32)
        gl_aug = ar.take([T], BF16, parts=17)
        qd = ar.take([T], BF16)
        ki = ar.take([T], BF16)
        kend = [ar.take([128], BF16) for _ in range(1)]
        vb = [ar.take([256], BF16) for _ in range(1)]
        rs = [ar.take([256], F32) for _ in range(1)]
        sTm = [ar.take([128], BF16) for _ in range(1)]
        ytmp = [ar.take([256], F32) for _ in range(1)]
        yb = [ar.take([256], BF16) for _ in range(1)]
        st = [ar.take([16], F32) for _ in range(2)]
        state, state_bf = gla_state[mi], gla_state_bf[mi]
        sres = "gla_st%d" % mi
        if first_tile:
            A('dve', lambda e: e.memset(state[:], 0.0), [], [sres])
            A('dve', lambda e: e.memset(state_bf[:], 0.0), [], [sres + "b"])
        A('dve', lambda e: e.memset(gl_aug[:], 1.0), [], ["g.gl"])
        wb, wres = win.get(("gla_g", l))
        pp, pres = proj_ps.next()
        proj_fm(pp, pres, wb, wres, 0, m=16)
        A('act', lambda e, pp=pp: e.copy(out=gl_aug[0:16, :], in_=pp[0:16, :]), [pres], ["g.gl"])
        win.done()
        for j in range(J):
            pp, pres = proj_ps.next()
            A('pe', lambda e, pp=pp, j=j: e.matmul(pp[:], lhsT=gl_aug[0:17, j * 128:(j + 1) * 128], rhs=gla_wg_sb[mi][0:17, :], start=True, stop=True),
              ["g.gl", "gla_wg%d" % mi], [pres])
            A('act', lambda e, pp=pp, j=j: e.activation(out=Lb[:, j, :], in_=pp[:], func=AF.Exp, scale=-1.0), [pres], ["g.L%d" % j])
            A('act', lambda e, j=j: e.activation(out=Lb[:, j, :], in_=Lb[:, j, :], func=AF.Ln, bias=cst[:, C_ONES:C_ONES + 1]), ["g.L%d" % j, "cst"], ["g.L%d" % j])
        for j in range(J):
            pp, pres = proj_ps.next()
            A('pe', lambda e, pp=pp, j=j: e.matmul(pp[:], lhsT=cs(C_U), rhs=Lb[:, j, :], start=True, stop=True), ["cst", "g.L%d" % j], [pres])
            A('act', lambda e, pp=pp, j=j: e.activation(out=E3[:, j, :], in_=pp[:], func=AF.Exp, scale=-1.0 / 16.0), [pres], ["g.E3%d" % j])
        LALL = ["g.L%d" % j for j in range(J)]
        for hh in range(GLA_H):
            wb, wres = win.get(("gla_h", l, hh))
            pc, pcres = proj_ps.next()
            for j in range(J):
                A('pe', lambda e, pc=pc, j=j, hh=hh: e.matmul(pc[:, j * 128:(j + 1) * 128], lhsT=Lb[:, j, hh * 128:(hh + 1) * 128], rhs=cs(C_TRI),
                                                         start=True, stop=True), ["cst"] + LALL, [pcres])
            A('act', lambda e, pc=pc: e.activation(out=E1[:], in_=pc[:], func=AF.Exp, scale=-1.0 / 16.0), [pcres], ["g.E1"])
            A('act', lambda e, pc=pc: e.activation(out=E2[:], in_=pc[:], func=AF.Exp, scale=1.0 / 16.0), [pcres], ["g.E2"])
            pq, pqres = proj_ps.next()
            proj_fm(pq, pqres, wb, wres, 0)
            A('dve', lambda e, pq=pq: e.scalar_tensor_tensor(out=qd[:], in0=pq[:], scalar=float(GLA_DK ** -0.5), in1=E1[:], op0=ALU.mult, op1=ALU.mult),
              [pqres, "g.E1"], ["g.qd"])
            pk, pkres = proj_ps.next()
            proj_fm(pk, pkres, wb, wres, 128)
            A('dve', lambda e, pk=pk: e.tensor_tensor(out=ki[:], in0=pk[:], in1=E2[:], op=ALU.mult), [pkres, "g.E2"], ["g.ki"])
            for j in range(J):
                q = 0
                js = slice(j * 128, (j + 1) * 128)
                pkv, pkvres = proj_ps.next()
                proj_tm(pkv, pkvres, wb, wres, j, 128, 384)
                A('dve', lambda e, pkv=pkv, q=q, j=j, hh=hh: e.tensor_tensor(out=kend[q][:], in0=pkv[:, 0:128], in1=E3[:, j, hh * 128:(hh + 1) * 128], op=ALU.mult),
                  [pkvres, "g.E3%d" % j], ["g.kend%d" % q])
                A('act', lambda e, pkv=pkv, q=q: e.copy(out=vb[q][:], in_=pkv[:, 128:384]), [pkvres], ["g.vb%d" % q])
                pr, prres = proj_ps.next()
                proj_tm(pr, prres, wb, wres, j, 512, 256)
                A('act', lambda e, pr=pr, q=q: e.activation(out=rs[q][:], in_=pr[:, 0:256], func=AF.Silu), [prres], ["g.rs%d" % q])
                psT, psTres = proj_ps.next()
                A('pe', lambda e, psT=psT, js=js: e.matmul(psT[:, 0:128], lhsT=ki[:, js], rhs=qd[:, js], start=True, stop=True), ["g.ki", "g.qd"], [psTres])
                A('dve', lambda e, psT=psT, q=q: e.tensor_tensor(out=sTm[q][:], in0=psT[:, 0:128], in1=cs(C_TRI), op=ALU.mult), [psTres, "cst"], ["g.sTm%d" % q])
                po, pores = proj_ps.next()
                A('pe', lambda e, po=po, q=q: e.matmul(po[:, 0:256], lhsT=sTm[q][:], rhs=vb[q][:], start=True, stop=False), ["g.sTm%d" % q, "g.vb%d" % q], [pores])
                A('pe', lambda e, po=po, js=js, hh=hh: e.matmul(po[:, 0:256], lhsT=qd[:, js], rhs=state_bf[:, hh, :], start=False, stop=True), ["g.qd", sres + "b"], [pores])
                psu, psures = proj_ps.next()
                A('pe', lambda e, psu=psu, q=q: e.matmul(psu[:, 0:256], lhsT=kend[q][:], rhs=vb[q][:], start=True, stop=True), ["g.kend%d" % q, "g.vb%d" % q], [psures])
                A('dve', lambda e, psu=psu, j=j, hh=hh: e.scalar_tensor_tensor(out=state[:, hh, :], in0=state[:, hh, :], scalar=E1[:, j * 128 + 127:j * 128 + 128],
                                                                           in1=psu[:, 0:256], op0=ALU.mult, op1=ALU.add), [psures, "g.E1", sres], [sres])
                A('act', lambda e, hh=hh: e.copy(out=state_bf[:, hh, :], in_=state[:, hh, :]), [sres], [sres + "b"])
                s_, s_res = st[q], "g.st%d" % q
                A('dve', lambda e, po=po, s_=s_: e.bn_stats(out=s_[:, 0:6], in_=po[:, 0:256]), [pores], [s_res])
                A('dve', lambda e, s_=s_: e.bn_aggr(out=s_[:, 6:8], in_=s_[:, 0:6]), [s_res], [s_res])
                A('dve', lambda e, s_=s_: e.scalar_tensor_tensor(out=s_[:, 8:9], in0=s_[:, 6:7], scalar=s_[:, 6:7], in1=s_[:, 7:8], op0=ALU.mult, op1=ALU.add), [s_res], [s_res])
                rstd_from_var(s_[:, 8:9], s_[:, 9:10], s_res, 1e-6)
                A('dve', lambda e, po=po, s_=s_, q=q: e.scalar_tensor_tensor(out=ytmp[q][:], in0=po[:, 0:256], scalar=s_[:, 9:10], in1=gla_ng_sb[mi][:],
                                                                      op0=ALU.mult, op1=ALU.mult), [pores, s_res, "gla_ng%d" % mi], ["g.ytmp%d" % q])
                A('dve', lambda e, q=q: e.tensor_tensor(out=yb[q][:], in0=ytmp[q][:], in1=rs[q][:], op=ALU.mult), ["g.ytmp%d" % q, "g.rs%d" % q], ["g.yb%d" % q])
                transposes(yb[q], "g.yb%d" % q, 2, lambda b0, n, j=j, hh=hh: actT[:, 2 * hh + b0:2 * hh + b0 + n, j * 128:(j + 1) * 128], "actT")
            win.done()
        P.fence()
        out_proj(8, ("mix_out", l), l * 3 + 1, 1.0)

    def ret(l, first_tile, pos0):
        P.fence()
        ar = Arena()
        sg = ar.take([J, 512], BF16)
        qr = ar.take([2, T], BF16)
        kr = ar.take([2, T], BF16)
        kd = ar.take([J, 256], BF16)
        vb = ar.take([J, 512], BF16)
        t1 = ar.take([T], F32)
        t2 = ar.take([T], F32)
        sTm = [ar.take([128], BF16) for _ in range(2)]
        ytmp = [ar.take([512], F32) for _ in range(2)]
        yb = [ar.take([512], BF16) for _ in range(2)]
        st = [ar.take([16], F32) for _ in range(2)]
        state, state_bf, sres = ret_state, ret_state_bf, "ret_st"
        if first_tile:
            A('dve', lambda e: e.memset(state[:], 0.0), [], [sres])
            A('dve', lambda e: e.memset(state_bf[:], 0.0), [], [sres + "b"])
        P.add('sp', lambda e: e.dma_start(out=rope_sb[:, 0, :], in_=rope_c[:, pos0:pos0 + T]), writes=["rope"], chan="rope")
        P.add('sp', lambda e: e.dma_start(out=rope_sb[:, 1, :], in_=rope_s[:, pos0:pos0 + T]), writes=["rope"], chan="rope")
        cosb, sinb = rope_sb[:, 0, :], rope_sb[:, 1, :]

        def rotary(pa, pares, pb_, pbres, dst, dres):
            A('dve', lambda e: e.tensor_tensor(out=t1[:], in0=pa[:], in1=cosb, op=ALU.mult), [pares, "rope"], ["r.t1"])
            A('dve', lambda e: e.tensor_tensor(out=t2[:], in0=pb_[:], in1=sinb, op=ALU.mult), [pbres, "rope"], ["r.t2"])
            A('dve', lambda e: e.tensor_tensor(out=dst[:, 0, :], in0=t1[:], in1=t2[:], op=ALU.subtract), ["r.t1", "r.t2"], [dres])
            A('dve', lambda e: e.tensor_tensor(out=t1[:], in0=pa[:], in1=sinb, op=ALU.mult), [pares, "rope", dres], ["r.t1"])
            A('dve', lambda e: e.tensor_tensor(out=t2[:], in0=pb_[:], in1=cosb, op=ALU.mult), [pbres, "rope", dres], ["r.t2"])
            A('dve', lambda e: e.tensor_tensor(out=dst[:, 1, :], in0=t1[:], in1=t2[:], op=ALU.add), ["r.t1", "r.t2"], [dres])

        for hh in range(RET_H):
            wb, wres = win.get(("ret_g", l, hh))
            for j in range(J):
                pp, pres = proj_ps.next()
                proj_tm(pp, pres, wb, wres, j, 0, 512)
                A('act', lambda e, pp=pp, j=j: e.activation(out=sg[:, j, :], in_=pp[:], func=AF.Silu), [pres], ["r.sg"])
            win.done()
            wb, wres = win.get(("ret_qk", l, hh))
            pa, pares = proj_ps.next(); proj_fm(pa, pares, wb, wres, 0)
            pb_, pbres = proj_ps.next(); proj_fm(pb_, pbres, wb, wres, 128)
            rotary(pa, pares, pb_, pbres, qr, "r.qr")
            pa, pares = proj_ps.next(); proj_fm(pa, pares, wb, wres, 256)
            pb_, pbres = proj_ps.next(); proj_fm(pb_, pbres, wb, wres, 384)
            rotary(pa, pares, pb_, pbres, kr, "r.kr")
            win.done()
            wb, wres = win.get(("ret_v", l, hh))
            for j in range(J):
                pt, ptres = tr_ps.next()
                for dc in range(2):
                    A('pe', lambda e, pt=pt, dc=dc, j=j: e.transpose(pt[:, dc, :], kr[:, dc, j * 128:(j + 1) * 128], ident[:]), ["r.kr", "ident"], [ptres])
                A('dve', lambda e, pt=pt, j=j, hh=hh: e.tensor_scalar_mul(out=kd[:, j, :].rearrange("p (a b) -> p a b", a=2), in0=pt[:, 0:2, :],
                                                                    scalar1=cst[:, C_RDK + hh:C_RDK + hh + 1]), [ptres, "cst"], ["r.kd"])
                pp, pres = proj_ps.next()
                proj_tm(pp, pres, wb, wres, j, 0, 512)
                A('act', lambda e, pp=pp, j=j: e.copy(out=vb[:, j, :], in_=pp[:]), [pres], ["r.vb"])
            for j in range(J):
                q = j % 2
                js = slice(j * 128, (j + 1) * 128)
                psT, psTres = proj_ps.next()
                for dc in range(2):
                    A('pe', lambda e, psT=psT, dc=dc, js=js: e.matmul(psT[:, 0:128], lhsT=kr[:, dc, js], rhs=qr[:, dc, js], start=(dc == 0), stop=(dc == 1)),
                      ["r.kr", "r.qr"], [psTres])
                A('dve', lambda e, psT=psT, q=q, hh=hh: e.tensor_tensor(out=sTm[q][:], in0=psT[:, 0:128], in1=cs(C_DEC + hh * 128), op=ALU.mult),
                  [psTres, "cst"], ["r.sTm%d" % q])
                po, pores = proj_ps.next()
                A('pe', lambda e, po=po, q=q, j=j: e.matmul(po[:], lhsT=sTm[q][:], rhs=vb[:, j, :], start=True, stop=False), ["r.sTm%d" % q, "r.vb"], [pores])
                for dc in range(2):
                    A('pe', lambda e, po=po, dc=dc, js=js, hh=hh: e.matmul(po[:], lhsT=qr[:, dc, js], rhs=state_bf[:, hh * 2 + dc, :], start=False, stop=(dc == 1)),
                      ["r.qr", sres + "b"], [pores])
                for dc in range(2):
                    psu, psures = proj_ps.next()
                    A('pe', lambda e, psu=psu, dc=dc, j=j: e.matmul(psu[:], lhsT=kd[:, j, dc * 128:(dc + 1) * 128], rhs=vb[:, j, :], start=True, stop=True),
                      ["r.kd", "r.vb"], [psures])
                    A('dve', lambda e, psu=psu, dc=dc, hh=hh: e.scalar_tensor_tensor(out=state[:, hh * 2 + dc, :], in0=state[:, hh * 2 + dc, :],
                                                                               scalar=float(RET_GAMMA[hh] ** 128), in1=psu[:], op0=ALU.mult, op1=ALU.add),
                      [psures, sres], [sres])
                    A('act', lambda e, dc=dc, hh=hh: e.copy(out=state_bf[:, hh * 2 + dc, :], in_=state[:, hh * 2 + dc, :]), [sres], [sres + "b"])
                s_, s_res = st[q], "r.st%d" % q
                A('dve', lambda e, po=po, s_=s_: e.bn_stats(out=s_[:, 0:6], in_=po[:]), [pores], [s_res])
                A('dve', lambda e, s_=s_: e.bn_aggr(out=s_[:, 6:8], in_=s_[:, 0:6]), [s_res], [s_res])
                A('dve', lambda e, s_=s_, hh=hh: e.tensor_tensor(out=s_[:, 8:9], in0=s_[:, 7:8], in1=cst[:, C_RC2 + hh:C_RC2 + hh + 1], op=ALU.mult), [s_res, "cst"], [s_res])
                rstd_from_var(s_[:, 8:9], s_[:, 9:10], s_res, 1e-6)
                A('dve', lambda e, s_=s_, hh=hh: e.tensor_tensor(out=s_[:, 10:11], in0=s_[:, 9:10], in1=cst[:, C_RC + hh:C_RC + hh + 1], op=ALU.mult), [s_res, "cst"], [s_res])
                A('dve', lambda e, po=po, s_=s_, q=q: e.tensor_scalar(out=ytmp[q][:], in0=po[:], scalar1=s_[:, 6:7], scalar2=s_[:, 10:11],
                                                               op0=ALU.subtract, op1=ALU.mult), [pores, s_res], ["r.ytmp%d" % q])
                A('dve', lambda e, q=q: e.tensor_tensor(out=ytmp[q][:], in0=ytmp[q][:], in1=ret_ng_sb[:], op=ALU.mult), ["r.ytmp%d" % q, "ret_ng"], ["r.ytmp%d" % q])
                A('dve', lambda e, q=q, j=j: e.tensor_tensor(out=yb[q][:], in0=ytmp[q][:], in1=sg[:, j, :], op=ALU.mult), ["r.ytmp%d" % q, "r.sg"], ["r.yb%d" % q])
                transposes(yb[q], "r.yb%d" % q, 4, lambda b0, n, j=j, hh=hh: actT[:, 4 * hh + b0:4 * hh + b0 + n, j * 128:(j + 1) * 128], "actT")
            win.done()
        P.fence()
        out_proj(16, ("mix_out", l), l * 3 + 1, 1.0)

    def ssd(l, first_tile):
        P.fence()
        ar = Arena()
        dt = ar.take([J, 32], F32)
        dtA = ar.take([J, 32], F32)
        negcum = ar.take([J, 32], F32)
        ecum = ar.take([J, 32], F32)
        declast = ar.take([J, 32], F32)
        toend = ar.take([J, 32], F32)
        sz = ar.take([J, 512], BF16)
        stage = [ar.take([T + 4], F32) for _ in range(1)]
        acc = [ar.take([T], F32) for _ in range(1)]
        xfm = [ar.take([T], BF16) for _ in range(2)]
        x_tok = ar.take([J, 512], BF16)
        BT = ar.take([T], BF16)
        CT = ar.take([T], BF16)
        B_tok = ar.take([J, 128], BF16)
        cbm = [ar.take([128], F32) for _ in range(2)]
        wseg = [ar.take([128], F32) for _ in range(2)]
        wT = [ar.take([128], BF16) for _ in range(2)]
        dbc = [ar.take([128], F32) for _ in range(2)]
        ta = [ar.take([512], F32) for _ in range(1)]
        tb = [ar.take([512], F32) for _ in range(1)]
        xs = [ar.take([512], BF16) for _ in range(1)]
        yb = [ar.take([512], BF16) for _ in range(1)]
        st = [ar.take([16], F32) for _ in range(2)]
        state, state_bf, sres = ssd_state, ssd_state_bf, "ssd_st"
        if first_tile:
            A('dve', lambda e: e.memset(state[:], 0.0), [], [sres])
            A('dve', lambda e: e.memset(state_bf[:], 0.0), [], [sres + "b"])
            A('dve', lambda e: e.memset(ssd_carry[:], 0.0), [], ["ssd_carry"])
        wb, wres = win.get(("ssd_dt", l))
        for j in range(J):
            pp, pres = proj_ps.next()
            proj_tm(pp, pres, wb, wres, j, 0, 32)
            A('dve', lambda e, pp=pp, j=j: e.tensor_tensor(out=dt[:, j, :], in0=pp[:, 0:32], in1=ssd_vec_sb[:, 0, :], op=ALU.add), [pres, "ssd_vec"], ["s.dt"])
        win.done()
        A('act', lambda e: e.activation(out=dt[:], in_=dt[:], func=AF.Exp), ["s.dt"], ["s.dt"])
        A('act', lambda e: e.activation(out=dt[:], in_=dt[:], func=AF.Ln, bias=cst[:, C_ONES:C_ONES + 1]), ["s.dt", "cst"], ["s.dt"])
        for j in range(J):
            A('dve', lambda e, j=j: e.tensor_tensor(out=dtA[:, j, :], in0=dt[:, j, :], in1=ssd_nega[:], op=ALU.mult), ["s.dt", "ssd_nega"], ["s.dtA"])
        for j in range(J):
            pp, pres = proj_ps.next()
            A('pe', lambda e, pp=pp, j=j: e.matmul(pp[:, 0:32], lhsT=cs(C_TRI), rhs=dtA[:, j, :], start=True, stop=True), ["cst", "s.dtA"], [pres])
            A('pe', lambda e, pp=pp, j=j: e.matmul(pp[:, 32:64], lhsT=cs(C_ONES), rhs=dtA[:, j, :], start=True, stop=True), ["cst", "s.dtA"], [pres])
            A('pe', lambda e, pp=pp, j=j: e.matmul(pp[:, 64:96], lhsT=cs(C_U), rhs=dtA[:, j, :], start=True, stop=True), ["cst", "s.dtA"], [pres])
            A('act', lambda e, pp=pp, j=j: e.activation(out=ecum[:, j, :], in_=pp[:, 0:32], func=AF.Exp), [pres], ["s.ecum"])
            A('dve', lambda e, pp=pp, j=j: e.tensor_scalar_mul(out=negcum[:, j, :], in0=pp[:, 0:32], scalar1=-1.0), [pres], ["s.negcum"])
            A('act', lambda e, pp=pp, j=j: e.activation(out=declast[:, j, :], in_=pp[:, 32:64], func=AF.Exp), [pres], ["s.declast"])
            A('act', lambda e, pp=pp, j=j: e.activation(out=toend[:, j, :], in_=pp[:, 64:96], func=AF.Exp), [pres], ["s.toend"])
            A('dve', lambda e, j=j: e.tensor_tensor(out=toend[:, j, :], in0=toend[:, j, :], in1=dt[:, j, :], op=ALU.mult), ["s.toend", "s.dt"], ["s.toend"])
        for g in range(SSD_G):
            P.add('sp', lambda e, g=g: e.dma_start(out=ssd_ngd_sb[:], in_=ssd_ngd[g]), writes=["ssd_ngd"], chan="ssd_ngd")
            wb, wres = win.get(("ssd_z", l, g))
            for j in range(J):
                pp, pres = proj_ps.next()
                proj_tm(pp, pres, wb, wres, j, 0, 512)
                A('act', lambda e, pp=pp, j=j: e.activation(out=sz[:, j, :], in_=pp[:], func=AF.Silu), [pres], ["s.sz"])
            win.done()
            wb, wres = win.get(("ssd_x", l, g))
            for ci_loc in range(6):
                ci = (g * 4 + ci_loc) if ci_loc < 4 else (16 + g if ci_loc == 4 else 20 + g)
                q = ci_loc % 2
                sg_, sgres = stage[0], "s.stage0"
                ac, acres = acc[0], "s.acc0"
                pp, pres = proj_ps.next()
                proj_fm(pp, pres, wb, wres, ci_loc * 128)
                A('dve', lambda e, sg_=sg_, ci=ci: e.tensor_copy(out=sg_[:, 0:3], in_=ssd_carry[:, ci, 0:3]), ["ssd_carry"], [sgres])
                A('act', lambda e, sg_=sg_, pp=pp: e.copy(out=sg_[:, 3:3 + T], in_=pp[:]), [pres], [sgres])
                A('dve', lambda e, sg_=sg_, ci=ci: e.tensor_copy(out=ssd_carry[:, ci, 0:3], in_=sg_[:, T:T + 3]), [sgres], ["ssd_carry"])
                A('dve', lambda e, sg_=sg_, ac=ac, ci=ci: e.tensor_scalar(out=ac[:], in0=sg_[:, 3:3 + T], scalar1=ssd_cw_sb[:, ci, 3:4], scalar2=ssd_cb_sb[:, ci:ci + 1],
                                                                  op0=ALU.mult, op1=ALU.add), [sgres, "ssd_cw", "ssd_cb"], [acres])
                for k in range(3):
                    A('dve', lambda e, sg_=sg_, ac=ac, ci=ci, k=k: e.scalar_tensor_tensor(out=ac[:], in0=sg_[:, k:k + T], scalar=ssd_cw_sb[:, ci, k:k + 1], in1=ac[:],
                                                                                   op0=ALU.mult, op1=ALU.add), [sgres, "ssd_cw", acres], [acres])
                if ci_loc < 4:
                    xf, xfres = xfm[q], "s.xfm%d" % q
                    A('act', lambda e, xf=xf, ac=ac: e.activation(out=xf[:], in_=ac[:], func=AF.Silu), [acres], [xfres])
                    for j in range(J):
                        pt, ptres = tr_ps.next()
                        A('pe', lambda e, pt=pt, xf=xf, j=j: e.transpose(pt[:, 0, :], xf[:, j * 128:(j + 1) * 128], ident[:]), [xfres, "ident"], [ptres])
                        A('act', lambda e, pt=pt, j=j, c=ci_loc: e.copy(out=x_tok[:, j, c * 128:(c + 1) * 128], in_=pt[:, 0, :]), [ptres], ["s.xtok"])
                elif ci_loc == 4:
                    A('act', lambda e, ac=ac: e.activation(out=BT[:], in_=ac[:], func=AF.Silu), [acres], ["s.BT"])
                    for j in range(J):
                        pt, ptres = tr_ps.next()
                        A('pe', lambda e, pt=pt, j=j: e.transpose(pt[:, 0, :], BT[:, j * 128:(j + 1) * 128], ident[:]), ["s.BT", "ident"], [ptres])
                        A('act', lambda e, pt=pt, j=j: e.copy(out=B_tok[:, j, :], in_=pt[:, 0, :]), [ptres], ["s.Btok"])
                else:
                    A('act', lambda e, ac=ac: e.activation(out=CT[:], in_=ac[:], func=AF.Silu), [acres], ["s.CT"])
            win.done()
            for j in range(J):
                q = j % 2
                js = slice(j * 128, (j + 1) * 128)
                pcb, pcbres = proj_ps.next()
                A('pe', lambda e, pcb=pcb, js=js: e.matmul(pcb[:, 0:128], lhsT=BT[:, js], rhs=CT[:, js], start=True, stop=True), ["s.BT", "s.CT"], [pcbres])
                A('dve', lambda e, pcb=pcb, q=q: e.tensor_tensor(out=cbm[q][:], in0=pcb[:, 0:128], in1=cs(C_TRI), op=ALU.mult), [pcbres, "cst"], ["s.cbm%d" % q])
                py, pyres = proj_ps.next()
                for half in range(2):
                    pseg, psegres = proj_ps.next()
                    for e4 in range(4):
                        eh = g * 8 + half * 4 + e4
                        w_, wres_ = dbc[e4 % 2], "s.dbc%d" % (e4 % 2)
                        A('dve', lambda e, w_=w_, j=j, eh=eh: e.tensor_scalar_mul(out=w_[:], in0=cs(C_ONES), scalar1=dtA[:, j, eh:eh + 1]), ["cst", "s.dtA"], [wres_])
                        A('pe', lambda e, pseg=pseg, e4=e4, w_=w_: e.matmul(pseg[:, e4 * 128:(e4 + 1) * 128], lhsT=w_[:], rhs=cs(C_TRI), start=True, stop=False),
                          [wres_, "cst"], [psegres])
                        A('pe', lambda e, pseg=pseg, e4=e4: e.matmul(pseg[:, e4 * 128:(e4 + 1) * 128], lhsT=cs(C_ID), rhs=cs(C_NEG4), start=False, stop=True),
                          ["cst"], [psegres])
                    for e4 in range(4):
                        eh = g * 8 + half * 4 + e4
                        el = half * 4 + e4
                        ws_, wsres = wseg[e4 % 2], "s.wseg%d" % (e4 % 2)
                        wt_, wtres = wT[e4 % 2], "s.wT%d" % (e4 % 2)
                        A('act', lambda e, pseg=pseg, e4=e4, ws_=ws_, j=j, eh=eh: e.activation(out=ws_[:], in_=pseg[:, e4 * 128:(e4 + 1) * 128], func=AF.Exp,
                                                                                       bias=negcum[:, j, eh:eh + 1], scale=1.0), [psegres, "s.negcum"], [wsres])
                        A('dve', lambda e, ws_=ws_, wt_=wt_, j=j, eh=eh, q=q: e.scalar_tensor_tensor(out=wt_[:], in0=ws_[:], scalar=dt[:, j, eh:eh + 1], in1=cbm[q][:],
                                                                                             op0=ALU.mult, op1=ALU.mult), [wsres, "s.dt", "s.cbm%d" % q], [wtres])
                        A('pe', lambda e, py=py, wt_=wt_, j=j, el=el: e.matmul(py[:, el * 64:(el + 1) * 64], lhsT=wt_[:], rhs=x_tok[:, j, el * 64:(el + 1) * 64], start=True, stop=True),
                          [wtres, "s.xtok"], [pyres])
                pint, pintres = proj_ps.next()
                A('pe', lambda e, pint=pint, js=js, g=g: e.matmul(pint[:], lhsT=CT[:, js], rhs=state_bf[:, g, :], start=True, stop=True), ["s.CT", sres + "b"], [pintres])
                ta_, tares = ta[0], "s.ta0"
                tb_, tbres = tb[0], "s.tb0"
                for el in range(8):
                    eh = g * 8 + el
                    A('dve', lambda e, pint=pint, ta_=ta_, el=el, j=j, eh=eh: e.tensor_scalar_mul(out=ta_[:, el * 64:(el + 1) * 64], in0=pint[:, el * 64:(el + 1) * 64],
                                                                                          scalar1=ecum[:, j, eh:eh + 1]), [pintres, "s.ecum"], [tares])
                A('dve', lambda e, py=py, ta_=ta_: e.tensor_tensor(out=ta_[:], in0=ta_[:], in1=py[:], op=ALU.add), [tares, pyres], [tares])
                A('dve', lambda e, tb_=tb_, j=j: e.tensor_tensor(out=tb_[:], in0=x_tok[:, j, :], in1=ssd_ngd_sb[:, 1, :], op=ALU.mult), ["s.xtok", "ssd_ngd"], [tbres])
                A('dve', lambda e, ta_=ta_, tb_=tb_: e.tensor_tensor(out=ta_[:], in0=ta_[:], in1=tb_[:], op=ALU.add), [tares, tbres], [tares])
                A('dve', lambda e, ta_=ta_, j=j: e.tensor_tensor(out=ta_[:], in0=ta_[:], in1=sz[:, j, :], op=ALU.mult), [tares, "s.sz"], [tares])
                s_, s_res = st[q], "s.st%d" % q
                A('dve', lambda e, ta_=ta_, s_=s_: e.bn_stats(out=s_[:, 0:6], in_=ta_[:]), [tares], [s_res])
                A('dve', lambda e, s_=s_: e.bn_aggr(out=s_[:, 6:8], in_=s_[:, 0:6]), [s_res], [s_res])
                A('dve', lambda e, s_=s_: e.scalar_tensor_tensor(out=s_[:, 8:9], in0=s_[:, 6:7], scalar=s_[:, 6:7], in1=s_[:, 7:8], op0=ALU.mult, op1=ALU.add), [s_res], [s_res])
                rstd_from_var(s_[:, 8:9], s_[:, 9:10], s_res, 1e-6)
                A('dve', lambda e, ta_=ta_, s_=s_, q=q: e.scalar_tensor_tensor(out=yb[0][:], in0=ta_[:], scalar=s_[:, 9:10], in1=ssd_ngd_sb[:, 0, :], op0=ALU.mult, op1=ALU.mult),
                  [tares, s_res, "ssd_ngd"], ["s.yb0"])
                transposes(yb[0], "s.yb0", 4, lambda b0, n, j=j, g=g: actT[:, 4 * g + b0:4 * g + b0 + n, j * 128:(j + 1) * 128], "actT")
                xs_, xsres = xs[0], "s.xs0"
                for el in range(8):
                    eh = g * 8 + el
                    A('dve', lambda e, xs_=xs_, el=el, j=j, eh=eh: e.tensor_scalar_mul(out=xs_[:, el * 64:(el + 1) * 64], in0=x_tok[:, j, el * 64:(el + 1) * 64],
                                                                                scalar1=toend[:, j, eh:eh + 1]), ["s.xtok", "s.toend"], [xsres])
                psu, psures = proj_ps.next()
                A('pe', lambda e, psu=psu, xs_=xs_, j=j: e.matmul(psu[:], lhsT=B_tok[:, j, :], rhs=xs_[:], start=True, stop=True), ["s.Btok", xsres], [psures])
                for el in range(8):
                    eh = g * 8 + el
                    A('dve', lambda e, psu=psu, el=el, j=j, eh=eh, g=g: e.scalar_tensor_tensor(out=state[:, g, el * 64:(el + 1) * 64], in0=state[:, g, el * 64:(el + 1) * 64],
                                                                                        scalar=declast[:, j, eh:eh + 1], in1=psu[:, el * 64:(el + 1) * 64],
                                                                                        op0=ALU.mult, op1=ALU.add), [psures, "s.declast", sres], [sres])
                A('act', lambda e, g=g: e.copy(out=state_bf[:, g, :], in_=state[:, g, :]), [sres], [sres + "b"])
        P.fence()
        out_proj(16, ("mix_out", l), l * 3 + 1, 1.0)

    for sq in range(n_seq):
        for tt in range(ntile_seq):
            t0 = sq * seq_len + tt * T
            for j in range(J):
                P.add('sp', lambda e, j=j, t0=t0: e.dma_start(out=h[:, j, :], in_=x_d[t0 + j * 128:t0 + (j + 1) * 128, :]),
                      writes=["h%d" % j], chan="xin%d" % j)
                hbt, hbres = hb_r.next()
                P.add('act', lambda e, hbt=hbt, j=j: e.copy(out=hbt[:], in_=h[:, j, :]), reads=["h%d" % j], writes=[hbres])
                transpose_to_xT(hbt, hbres, j)
            for l in layers:
                for stg in stages:
                    if stg == 'ffn1':
                        ffn(l, 0)
                    elif stg == 'ffn2':
                        ffn(l, 1)
                    elif MIX[l] == 'gla':
                        gla(l, tt == 0)
                    elif MIX[l] == 'ret':
                        ret(l, tt == 0, tt * T)
                    else:
                        ssd(l, tt == 0)
            for j in range(J):
                P.add('sp', lambda e, j=j, t0=t0: e.dma_start(out=y_d[t0 + j * 128:t0 + (j + 1) * 128, :], in_=h[:, j, :]),
                      reads=["h%d" % j], chan="yout%d" % j)
                P.out_chans.add("yout%d" % j)

    P.emit(nc)
    es.close()
    return nc, P


def host_inputs(inputs, n_seq, seq_len):
    f = lambda a: np.ascontiguousarray(np.asarray(a, dtype=np.float32))
    ln_g, ln_b = f(inputs['ln_g']), f(inputs['ln_b'])
    ln_gb = np.empty((DEPTH * 3, 2, 128, D), np.float32)
    for i in range(DEPTH):
        for k in range(3):
            ln_gb[i * 3 + k, 0] = ln_g[i, k][None, :]
            ln_gb[i * 3 + k, 1] = ln_b[i, k][None, :]
    gla_wg = np.concatenate([f(inputs['gla_w_gate']), f(inputs['gla_b_gate'])[:, None, :]], axis=1)
    gla_ng = np.ascontiguousarray(np.broadcast_to(f(inputs['gla_norm_g'])[:, None, :], (2, 128, 256)))
    ret_ng = np.ascontiguousarray(np.broadcast_to(f(inputs['ret_norm_g'])[0][None, :], (128, 512)))
    cw = f(inputs['ssd_conv_w'])[0]
    ssd_cw = np.ascontiguousarray(cw.T.reshape(24, 128, 4).transpose(1, 0, 2))
    ssd_cb = np.ascontiguousarray(f(inputs['ssd_conv_b'])[0].reshape(24, 128).T)
    vec = np.stack([f(inputs['ssd_dt_bias'])[0], f(inputs['ssd_a_log'])[0], f(inputs['ssd_d'])[0]], 0)
    ssd_vec = np.ascontiguousarray(np.broadcast_to(vec[None], (128, 3, 32)))
    ng = f(inputs['ssd_norm_g'])[0].reshape(4, 512)
    dsk = np.repeat(f(inputs['ssd_d'])[0], 64).reshape(4, 512)
    ngd = np.stack([ng, dsk], 1)
    ssd_ngd = np.ascontiguousarray(np.broadcast_to(ngd[:, None], (4, 128, 2, 512)))
    rc, rs_ = make_rope(seq_len)
    return dict(
        ffn_w_in=f(inputs['ffn_w_in']), ffn_w_out=f(inputs['ffn_w_out']), ln_gb=ln_gb, cst=make_consts(),
        gla_w_in=f(inputs['gla_w_in']), gla_wg=np.ascontiguousarray(gla_wg), gla_ng=gla_ng, gla_w_out=f(inputs['gla_w_out']),
        ret_w_in=f(inputs['ret_w_in']), ret_ng=ret_ng, ret_w_out=f(inputs['ret_w_out']), rope_cos=rc, rope_sin=rs_,
        ssd_w_in=f(inputs['ssd_w_in']), ssd_cw=ssd_cw, ssd_cb=ssd_cb, ssd_vec=ssd_vec, ssd_ngd=ssd_ngd, ssd_w_out=f(inputs['ssd_w_out']))


def kernel(**inputs):
    x = np.asarray(inputs['x'], dtype=np.float32)
    B, L, _ = x.shape
    n_seq = B // NCORES
    nc, _ = build_program(n_seq, L)
    shared = host_inputs(inputs, n_seq, L)
    in_maps = []
    for c in range(NCORES):
        m = dict(shared)
        m['x'] = np.ascontiguousarray(x[c * n_seq:(c + 1) * n_seq].reshape(n_seq * L, D))
        in_maps.append(m)
    res = run_bass_kernel_spmd(nc, in_maps, core_ids=list(range(NCORES)))
    out = np.concatenate([np.asarray(r["y"]).reshape(n_seq, L, D) for r in res.results], axis=0)
    return out.astype(np.float32)
```

```python
import math
import numpy as np
import concourse.bass as bass
import concourse.mybir as mybir
from concourse.bass_utils import run_bass_kernel_spmd

F32 = mybir.dt.float32
BF16 = mybir.dt.bfloat16
AF = mybir.ActivationFunctionType
ALU = mybir.AluOpType

D = 1024
DEPTH = 4
DFF = 2816
NFC = DFF // 128
T = 512
J = T // 128
ALPHA = (2 * DEPTH) ** 0.25
LN_EPS_EFF = 1e-5 / (ALPHA * ALPHA)
NCORES = 8

COMPUTE = ('pe', 'act', 'dve', 'pool')


class Prog:
    def __init__(self):
        self.ops = []
        self.last_w = {}
        self.readers = {}
        self.chan_count = {}
        self.fence_last = {}
        self.fence_pending = set()
        self.last_on = {}
        self.out_chans = set()

    def add(self, eng, fn, reads=(), writes=(), chan=None, after=()):
        idx = len(self.ops)
        deps = {}
        def dep(i):
            o = self.ops[i]
            if o['chan'] is not None:
                deps[i] = self.chan_count[o['chan']]
            else:
                deps[i] = None
        for i in after:
            dep(i)
        for r in reads:
            if r in self.last_w:
                dep(self.last_w[r])
            if r.startswith("pb") or r.startswith("ptr"):
                for i in self.readers.get(r, ()):
                    if self.ops[i]['eng'] != eng:
                        dep(i)
        for w in writes:
            if w in self.last_w:
                dep(self.last_w[w])
            for i in self.readers.get(w, ()):
                dep(i)
        if eng in self.fence_pending:
            self.fence_pending.discard(eng)
            for e, i in self.fence_last.items():
                if e != eng:
                    dep(i)
        for r in reads:
            self.readers.setdefault(r, []).append(idx)
        for w in writes:
            self.last_w[w] = idx
            self.readers[w] = []
        op = dict(eng=eng, fn=fn, deps=deps, chan=chan, mark=False, cnt=0)
        if chan is not None:
            self.chan_count[chan] = self.chan_count.get(chan, 0) + 16
        self.ops.append(op)
        if chan is None:
            self.last_on[eng] = idx
        return idx

    def fence(self):
        self.fence_last = dict(self.last_on)
        self.fence_pending = set(COMPUTE)

    def emit(self, nc):
        ops = self.ops
        for X in ops:
            for i in X['deps']:
                Dp = ops[i]
                if Dp['chan'] is None and not (Dp['eng'] == 'pe' and X['eng'] == 'pe' and X['chan'] is None):
                    Dp['mark'] = True
        cnt = {}
        for X in ops:
            if X['chan'] is None and X['mark']:
                cnt[X['eng']] = cnt.get(X['eng'], 0) + 1
                X['cnt'] = cnt[X['eng']]
        chans = sorted(self.chan_count)
        import contextlib
        with contextlib.ExitStack() as es:
            sems = {}
            for e in ('pe', 'act', 'dve', 'pool'):
                sems[('eng', e)] = es.enter_context(nc.semaphore("s_" + e))
            for c in chans:
                sems[('chan', c)] = es.enter_context(nc.semaphore("c_" + c))
            block = es.enter_context(nc.Block())
            by_eng = {}
            for X in ops:
                by_eng.setdefault(X['eng'], []).append(X)

            def run(E, eng):
                waited = {}
                for X in by_eng.get(E, []):
                    waits = {}
                    for i, cv in X['deps'].items():
                        Dp = ops[i]
                        if Dp['chan'] is not None:
                            key, val = ('chan', Dp['chan']), cv
                        elif Dp['eng'] == 'pe' and E == 'pe' and X['chan'] is None:
                            continue
                        else:
                            key, val = ('eng', Dp['eng']), Dp['cnt']
                        if val > waits.get(key, 0):
                            waits[key] = val
                    for key, val in waits.items():
                        if val > waited.get(key, 0):
                            eng.wait_ge(sems[key], val)
                            waited[key] = val
                    ins = X['fn'](eng)
                    if X['chan'] is not None:
                        ins.then_inc(sems[('chan', X['chan'])], 16)
                    elif X['mark']:
                        ins.then_inc(sems[('eng', E)], 1)
                if E == 'sp':
                    for c in sorted(self.out_chans):
                        eng.wait_ge(sems[('chan', c)], self.chan_count[c])

            @block.tensor
            def _(e):
                run('pe', e)

            @block.scalar
            def _(e):
                run('act', e)

            @block.vector
            def _(e):
                run('dve', e)

            @block.gpsimd
            def _(e):
                run('pool', e)

            @block.sync
            def _(e):
                run('sp', e)


class Rot:
    def __init__(self, items):
        self.items = items
        self.i = 0

    def next(self):
        it = self.items[self.i % len(self.items)]
        self.i += 1
        return it


class WStream:
    def __init__(self, P, name, slots, plan):
        self.P, self.name, self.slots, self.plan = P, name, slots, plan
        self.issued = 0
        self.used = 0
        for _ in range(len(slots)):
            self._issue()

    def _issue(self):
        if self.issued >= len(self.plan):
            return
        n = self.issued
        self.issued += 1
        s = n % len(self.slots)
        buf = self.slots[s]
        res = "%s%d" % (self.name, s)
        g = self.plan[n]
        dst, src = g['load'](buf)
        self.P.add('sp', (lambda e, d=dst, s_=src: e.dma_start(out=d, in_=s_)),
                   writes=[res], chan=res, after=g['after'])

    def get(self, tag):
        n = self.used
        assert self.plan[n]['tag'] == tag, (self.plan[n]['tag'], tag)
        s = n % len(self.slots)
        return self.slots[s], "%s%d" % (self.name, s)

    def done(self):
        self.used += 1
        self._issue()


GLA_H, GLA_DK, GLA_DV = 4, 128, 256
RET_H, RET_DK, RET_DV = 4, 256, 512
SSD_G, SSD_HPG, SSD_P, SSD_N = 4, 8, 64, 128
MIX = ['gla', 'ret', 'ssd', 'gla']
MIXIDX = [0, 0, 0, 1]
RET_GAMMA = [1.0 - 2.0 ** (-5.0 - hh) for hh in range(RET_H)]
C_ID, C_TRI, C_U, C_NEG4, C_ONES, C_DEC, C_RC, C_RC2, C_RDK, C_END = 0, 128, 256, 384, 896, 1024, 1536, 1540, 1544, 1548
NEG = -30000.0


def make_consts():
    c = np.zeros((128, C_END), np.float32)
    r = np.arange(128)[:, None].astype(np.float64)
    i = np.arange(128)[None, :].astype(np.float64)
    c[:, C_ID:C_ID + 128] = np.eye(128)
    c[:, C_TRI:C_TRI + 128] = (r <= i)
    c[:, C_U:C_U + 128] = (r > i)
    for q in range(4):
        c[:, C_NEG4 + q * 128:C_NEG4 + (q + 1) * 128] = np.where(i < r, NEG, 0.0)
    c[:, C_ONES:C_ONES + 128] = 1.0
    for hh, g in enumerate(RET_GAMMA):
        c[:, C_DEC + hh * 128:C_DEC + (hh + 1) * 128] = np.where(i >= r, g ** (-(r + 1.0)), 0.0) / 16.0
        c[:, C_RC + hh] = g ** (np.arange(128) + 1.0)
        c[:, C_RC2 + hh] = (g ** (np.arange(128) + 1.0)) ** 2
        c[:, C_RDK + hh] = g ** (127.0 - np.arange(128)) / 16.0
    return c


def make_rope(seq_len):
    dh = RET_DK
    inv = (np.float32(10000.0) ** (-np.arange(0, dh, 2, dtype=np.float32) / np.float32(dh))).astype(np.float32)
    ang = (np.arange(seq_len, dtype=np.float32)[None, :] * inv[:, None]).astype(np.float32)
    return np.cos(ang).astype(np.float32), np.sin(ang).astype(np.float32)


def build_program(n_seq, seq_len, layers=None, stages=None):
    ntok = n_seq * seq_len
    ntile_seq = seq_len // T
    layers = list(range(DEPTH)) if layers is None else layers
    stages = stages or ['ffn1', 'mix', 'ffn2']
    nc = bass.Bass("TRN2", target_bir_lowering=False)
    P = Prog()

    def din(name, shape, dt=F32):
        return nc.dram_tensor(name, list(shape), dt, kind="ExternalInput").ap()

    x_d = din("x", [ntok, D])
    y_d = nc.dram_tensor("y", [ntok, D], F32, kind="ExternalOutput").ap()
    ffn_w_in = din("ffn_w_in", [DEPTH, 2, D, 2 * DFF])
    ffn_w_out = din("ffn_w_out", [DEPTH, 2, DFF, D])
    ln_gb = din("ln_gb", [DEPTH * 3, 2, 128, D])
    cst_d = din("cst", [128, C_END])
    gla_w_in = din("gla_w_in", [2, D, 3088])
    gla_wg = din("gla_wg", [2, 17, 512])
    gla_ng = din("gla_ng", [2, 128, 256])
    gla_w_out = din("gla_w_out", [2, 1024, D])
    ret_w_in = din("ret_w_in", [1, D, 6144])
    ret_ng = din("ret_ng", [128, 512])
    ret_w_out = din("ret_w_out", [1, 2048, D])
    rope_c = din("rope_cos", [128, seq_len])
    rope_s = din("rope_sin", [128, seq_len])
    ssd_w_in = din("ssd_w_in", [1, D, 5152])
    ssd_cw = din("ssd_cw", [128, 24, 4])
    ssd_cb = din("ssd_cb", [128, 24])
    ssd_vec = din("ssd_vec", [128, 3, 32])
    ssd_ngd = din("ssd_ngd", [4, 128, 2, 512])
    ssd_w_out = din("ssd_w_out", [1, 2048, D])

    import contextlib
    es = contextlib.ExitStack()

    def sb(name, shape, dt):
        return es.enter_context(nc.sbuf_tensor(name, list(shape), dt))

    def ps(name, shape, dt):
        return es.enter_context(nc.psum_tensor(name, list(shape), dt))

    h = sb("h", [128, J, D], F32)
    xT = sb("xT", [128, 8, T], BF16)
    actT = sb("actT", [128, NFC, T], BF16)
    win_slots = [sb("win%d" % i, [128, 8, 768], BF16) for i in range(2)]
    wout_slots = [sb("wout%d" % i, [128, 8, 512], BF16) for i in range(2)]
    lng = sb("lng", [128, 2, D], F32)
    hb = [sb("hb%d" % i, [128, D], BF16) for i in range(2)]
    silu_s = [sb("silu%d" % i, [128, T], F32) for i in range(2)]
    stats = [sb("stats%d" % i, [128, 16], F32) for i in range(2)]
    ep_st = sb("ep_st", [128, J, 12], F32)
    ep_mv = sb("ep_mv", [128, J, 2], F32)
    ep_rs = sb("ep_rs", [128, J], F32)
    cst = sb("cst_sb", [128, C_END], F32)
    ident = sb("ident_b", [128, 128], BF16)
    gla_state = [sb("gla_st%d" % i, [128, GLA_H, GLA_DV], F32) for i in range(2)]
    gla_state_bf = [sb("gla_stb%d" % i, [128, GLA_H, GLA_DV], BF16) for i in range(2)]
    gla_wg_sb = [sb("gla_wg%d" % i, [17, 512], BF16) for i in range(2)]
    gla_ng_sb = [sb("gla_ng%d" % i, [128, 256], F32) for i in range(2)]
    ret_state = sb("ret_st", [128, 8, RET_DV], F32)
    ret_state_bf = sb("ret_stb", [128, 8, RET_DV], BF16)
    ret_ng_sb = sb("ret_ng_sb", [128, 512], F32)
    rope_sb = sb("rope_sb", [128, 2, T], F32)
    ssd_state = sb("ssd_st", [128, SSD_G, 512], F32)
    ssd_state_bf = sb("ssd_stb", [128, SSD_G, 512], BF16)
    ssd_carry = sb("ssd_carry", [128, 24, 4], F32)
    ssd_cw_sb = sb("ssd_cw_sb", [128, 24, 4], F32)
    ssd_cb_sb = sb("ssd_cb_sb", [128, 24], F32)
    ssd_vec_sb = sb("ssd_vec_sb", [128, 3, 32], F32)
    ssd_nega = sb("ssd_nega", [128, 32], F32)
    ssd_ngd_sb = sb("ssd_ngd_sb", [128, 2, 512], F32)
    AR_WORDS = 7808
    arena_t = sb("arena", [128, AR_WORDS], F32)

    class Arena:
        def __init__(self):
            self.off = 0

        def take(self, free, dt, parts=128):
            n = 1
            for f in free:
                n *= f
            words = n if dt == F32 else (n + 1) // 2
            a = arena_t[0:parts, self.off:self.off + words]
            self.off += words
            assert self.off <= AR_WORDS, self.off
            if dt != F32:
                a = a.bitcast(BF16)
            if len(free) == 2:
                a = a.rearrange("p (a b) -> p a b", a=free[0])
            elif len(free) == 3:
                a = a.rearrange("p (a b c) -> p a b c", a=free[0], b=free[1])
            return a

    pbank = [ps("pb%d" % i, [128, 512], F32) for i in range(6)]
    ptr_all = [ps("ptr_all%d" % i, [128, 8, 128], BF16) for i in range(2)]
    ptr = [ptr_all[0][:, 0:4, :], ptr_all[1][:, 0:4, :]]
    proj_ps = Rot([(pbank[i], "pb%d" % i) for i in range(4)])
    acc_ps = [(pbank[i], "pb%d" % i) for i in (4, 5, 2, 3)]
    tr_ps = Rot([(ptr[i], "ptr%d" % i) for i in range(2)])
    hb_r = Rot([(hb[i], "hb%d" % i) for i in range(2)])
    silu_r = Rot([(silu_s[i], "silu%d" % i) for i in range(2)])
    stats_r = Rot([(stats[i], "stats%d" % i) for i in range(2)])

    def cs(c0, n=128):
        return cst[:, c0:c0 + n]

    def colgrp(w2d, pieces, tag):
        dmas = []
        for (d0, s0, n) in pieces:
            src = w2d[:, s0:s0 + n].rearrange("(kc p) n -> p kc n", p=128)
            dmas.append((lambda b, d0=d0, n=n: b[:, :, d0:d0 + n], src))
        return dict(tag=tag, dmas=dmas, ncols=max(d0 + n for (d0, s0, n) in pieces))

    def ffn_in_groups(l, s):
        groups = []
        for g0 in range(0, NFC, 3):
            n = min(3, NFC - g0)
            groups.append(colgrp(ffn_w_in[l, s], [(0, g0 * 128, n * 128), (n * 128, DFF + g0 * 128, n * 128)], ("ffn_in", l, s, g0)))
        return groups

    def out_groups(w2d, nk, tag):
        groups = []
        for hf in range(2):
            src = w2d[:, hf * 512:(hf + 1) * 512].rearrange("(fc p) n -> p fc n", p=128)
            for k0 in range(0, nk, 8):
                n = min(8, nk - k0)
                groups.append(dict(tag=(tag, hf, k0), dmas=[(lambda b, n=n: b[:, 0:n, :], src[:, k0:k0 + n, :])], nk=n))
        return groups

    def mix_in_groups(l):
        kind, mi = MIX[l], MIXIDX[l]
        gs = []
        if kind == 'gla':
            w = gla_w_in[mi]
            gs.append(colgrp(w, [(0, 3072, 16)], ("gla_g", l)))
            for hh in range(GLA_H):
                gs.append(colgrp(w, [(0, hh * 128, 128), (128, 512 + hh * 128, 128), (256, 1024 + hh * 256, 256),
                                     (512, 2048 + hh * 256, 256)], ("gla_h", l, hh)))
        elif kind == 'ret':
            w = ret_w_in[0]
            for hh in range(RET_H):
                gs.append(colgrp(w, [(0, 4096 + hh * 512, 512)], ("ret_g", l, hh)))
                gs.append(colgrp(w, [(0, hh * 256, 256), (256, 1024 + hh * 256, 256)], ("ret_qk", l, hh)))
                gs.append(colgrp(w, [(0, 2048 + hh * 512, 512)], ("ret_v", l, hh)))
        else:
            w = ssd_w_in[0]
            gs.append(colgrp(w, [(0, 5120, 32)], ("ssd_dt", l)))
            for g in range(SSD_G):
                gs.append(colgrp(w, [(0, g * 512, 512)], ("ssd_z", l, g)))
                gs.append(colgrp(w, [(0, 2048 + g * 512, 512), (512, 4096 + g * 128, 128), (640, 4608 + g * 128, 128)], ("ssd_x", l, g)))
        return gs

    def mix_out_groups(l):
        kind, mi = MIX[l], MIXIDX[l]
        if kind == 'gla':
            return out_groups(gla_w_out[mi], 8, ("mix_out", l))
        if kind == 'ret':
            return out_groups(ret_w_out[0], 16, ("mix_out", l))
        return out_groups(ssd_w_out[0], 16, ("mix_out", l))

    STG = {'ffn1': 0, 'mix': 1, 'ffn2': 2}

    def stage_groups(l, st):
        if st == 'ffn1':
            return ffn_in_groups(l, 0), out_groups(ffn_w_out[l, 0], NFC, ("ffn_out", l, 0))
        if st == 'ffn2':
            return ffn_in_groups(l, 1), out_groups(ffn_w_out[l, 1], NFC, ("ffn_out", l, 1))
        return mix_in_groups(l), mix_out_groups(l)

    stage_list = [(l, st) for l in layers for st in stages]
    n_in = sum(len(stage_groups(l, st)[0]) for l, st in stage_list)
    n_out = sum(len(stage_groups(l, st)[1]) for l, st in stage_list)
    scr_in = nc.dram_tensor("scr_in", [n_in, 128, 8 * 768], BF16).ap()
    scr_out = nc.dram_tensor("scr_out", [n_out, 128, 8 * 512], BF16).ap()
    stage_plans = {}
    ii = oi = 0
    for (l, st) in stage_list:
        gi, go = stage_groups(l, st)
        chan = "prep%d" % (l * 3 + STG[st])
        last = None
        pin, pout = [], []
        for g in gi:
            img = scr_in[ii].rearrange("p (kc n) -> p kc n", kc=8)
            ncols = 0
            for (dst_fn, src) in g['dmas']:
                last = P.add('pool', (lambda e, d=dst_fn(img), s_=src: e.dma_start(out=d, in_=s_)), chan=chan)
            ncols = g['ncols']
            pin.append(dict(tag=g['tag'], img=img, ncols=ncols))
            ii += 1
        for g in go:
            img = scr_out[oi].rearrange("p (kc n) -> p kc n", kc=8)
            for (dst_fn, src) in g['dmas']:
                last = P.add('pool', (lambda e, d=dst_fn(img), s_=src: e.dma_start(out=d, in_=s_)), chan=chan)
            pout.append(dict(tag=g['tag'], img=img, nk=g['nk']))
            oi += 1
        for g in pin:
            g['after'] = [last]
            g['load'] = (lambda buf, g=g: (buf[:, :, 0:g['ncols']], g['img'][:, :, 0:g['ncols']]))
        for g in pout:
            g['after'] = [last]
            g['load'] = (lambda buf, g=g: (buf[:, 0:g['nk'], :], g['img'][:, 0:g['nk'], :]))
        stage_plans[(l, st)] = (pin, pout)

    in_plan, out_plan = [], []
    for sq in range(n_seq):
        for tt in range(ntile_seq):
            for (l, st) in stage_list:
                in_plan += stage_plans[(l, st)][0]
                out_plan += stage_plans[(l, st)][1]

    P.add('sp', lambda e: e.dma_start(out=cst[:], in_=cst_d[:, :]), writes=["cst"], chan="k_cst")
    P.add('dve', lambda e: e.tensor_copy(out=ident[:], in_=cst[:, C_ID:C_ID + 128]), reads=["cst"], writes=["ident"])
    for i in range(2):
        P.add('pool', lambda e, i=i: e.dma_start(out=gla_wg_sb[i][:], in_=gla_wg[i]), writes=["gla_wg%d" % i], chan="k_wg%d" % i)
        P.add('sp', lambda e, i=i: e.dma_start(out=gla_ng_sb[i][:], in_=gla_ng[i]), writes=["gla_ng%d" % i], chan="k_ng%d" % i)
    P.add('sp', lambda e: e.dma_start(out=ret_ng_sb[:], in_=ret_ng[:, :]), writes=["ret_ng"], chan="k_rng")
    P.add('sp', lambda e: e.dma_start(out=ssd_cw_sb[:], in_=ssd_cw[:, :, :]), writes=["ssd_cw"], chan="k_cw")
    P.add('sp', lambda e: e.dma_start(out=ssd_cb_sb[:], in_=ssd_cb[:, :]), writes=["ssd_cb"], chan="k_cb")
    P.add('sp', lambda e: e.dma_start(out=ssd_vec_sb[:], in_=ssd_vec[:, :, :]), writes=["ssd_vec"], chan="k_vec")
    P.add('act', lambda e: e.activation(out=ssd_nega[:], in_=ssd_vec_sb[:, 1, :], func=AF.Exp), reads=["ssd_vec"], writes=["ssd_nega"])
    P.add('dve', lambda e: e.tensor_scalar_mul(out=ssd_nega[:], in0=ssd_nega[:], scalar1=-1.0), reads=["ssd_nega"], writes=["ssd_nega"])

    win = WStream(P, "win", win_slots, in_plan)
    wout = WStream(P, "wout", wout_slots, out_plan)

    def transposes(src_bf, src_res, nblk, dst_fn, dst_res):
        for b0 in range(0, nblk, 4):
            n = min(4, nblk - b0)
            pt, pres = tr_ps.next()
            for q in range(n):
                P.add('pe', lambda e, pt=pt, q=q, b=b0 + q: e.transpose(pt[:, q, :], src_bf[:, b * 128:(b + 1) * 128], ident[:]),
                      reads=[src_res, "ident"], writes=[pres])
            P.add('act', lambda e, pt=pt, b0=b0, n=n: e.copy(out=dst_fn(b0, n), in_=pt[:, 0:n, :]),
                  reads=[pres], writes=[dst_res])

    def transpose_to_xT(src_bf, src_res, j):
        transposes(src_bf, src_res, 8, lambda b0, n: xT[:, b0:b0 + n, j * 128:(j + 1) * 128], "xT%d" % j)

    def load_ln(idx):
        P.add('sp', lambda e: e.dma_start(out=lng[:], in_=ln_gb[idx].rearrange("a p d -> p a d")),
              writes=["lng"], chan="lng")

    def rstd_from_var(var_ap, out_ap, res, eps):
        P.add('dve', lambda e: e.tensor_scalar_add(out=out_ap, in0=var_ap, scalar1=eps), reads=[res], writes=[res])
        P.add('act', lambda e: e.sqrt(out=out_ap, in_=out_ap), reads=[res], writes=[res])
        P.add('dve', lambda e: e.reciprocal(out=out_ap, in_=out_ap), reads=[res], writes=[res])

    def epi_stats(j):
        hres = "h%d" % j
        for hf in range(2):
            P.add('dve', lambda e, hf=hf: e.bn_stats(out=ep_st[:, j, hf * 6:(hf + 1) * 6], in_=h[:, j, hf * 512:(hf + 1) * 512]),
                  reads=[hres], writes=["ep_st%d" % j])
        P.add('dve', lambda e: e.bn_aggr(out=ep_mv[:, j, :], in_=ep_st[:, j, :]), reads=["ep_st%d" % j], writes=["ep_mv"])

    def epi_finish():
        P.add('dve', lambda e: e.tensor_scalar_add(out=ep_rs[:], in0=ep_mv[:, :, 1], scalar1=float(LN_EPS_EFF)), reads=["ep_mv"], writes=["ep_rs"])
        P.add('act', lambda e: e.sqrt(out=ep_rs[:], in_=ep_rs[:]), reads=["ep_rs"], writes=["ep_rs"])
        P.add('dve', lambda e: e.reciprocal(out=ep_rs[:], in_=ep_rs[:]), reads=["ep_rs"], writes=["ep_rs"])
        for jp in range(0, J, 2):
            js_ = (jp, jp + 1)
            for j in js_:
                P.add('dve', lambda e, j=j: e.tensor_scalar(out=h[:, j, :], in0=h[:, j, :], scalar1=ep_mv[:, j, 0:1], scalar2=ep_rs[:, j:j + 1],
                                                           op0=ALU.subtract, op1=ALU.mult), reads=["ep_mv", "ep_rs", "h%d" % j], writes=["h%d" % j])
            for j in js_:
                P.add('dve', lambda e, j=j: e.tensor_tensor(out=h[:, j, :], in0=h[:, j, :], in1=lng[:, 0, :], op=ALU.mult),
                      reads=["h%d" % j, "lng"], writes=["h%d" % j])
            for j in js_:
                P.add('dve', lambda e, j=j: e.tensor_tensor(out=h[:, j, :], in0=h[:, j, :], in1=lng[:, 1, :], op=ALU.add),
                      reads=["h%d" % j, "lng"], writes=["h%d" % j])
            for j in js_:
                hbt, hbres = hb_r.next()
                P.add('act', lambda e, hbt=hbt, j=j: e.copy(out=hbt[:], in_=h[:, j, :]), reads=["h%d" % j], writes=[hbres])
                transpose_to_xT(hbt, hbres, j)

    XT_ALL = ["xT%d" % j for j in range(J)]

    def out_proj(nk, tag, lnidx, coef):
        load_ln(lnidx)
        for hf in range(2):
            k0s = list(range(0, nk, 8))
            for k0 in k0s:
                n = min(8, nk - k0)
                wb, wres = wout.get((tag, hf, k0))
                for j in range(J):
                    mp, mres = acc_ps[j]
                    for kk in range(n):
                        kc = k0 + kk
                        P.add('pe', lambda e, mp=mp, kc=kc, kk=kk, j=j, wb=wb: e.matmul(
                            mp[:], lhsT=actT[:, kc, j * 128:(j + 1) * 128], rhs=wb[:, kk, :], start=(kc == 0), stop=(kc == nk - 1)),
                            reads=["actT", wres], writes=[mres])
                wout.done()
            for j in range(J):
                mp, mres = acc_ps[j]
                P.add('dve', lambda e, mp=mp, j=j, hf=hf: e.scalar_tensor_tensor(
                    out=h[:, j, hf * 512:(hf + 1) * 512], in0=mp[:], scalar=float(coef / ALPHA),
                    in1=h[:, j, hf * 512:(hf + 1) * 512], op0=ALU.mult, op1=ALU.add),
                    reads=[mres, "h%d" % j], writes=["h%d" % j])
                if hf == 1:
                    epi_stats(j)
        epi_finish()

    def proj_fm(pp, pres, wb, wres, c0, m=128):
        for kc in range(8):
            P.add('pe', lambda e, kc=kc: e.matmul(pp[0:m, :], lhsT=wb[:, kc, c0:c0 + m], rhs=xT[:, kc, :], start=(kc == 0), stop=(kc == 7)),
                  reads=[wres] + XT_ALL, writes=[pres])

    def proj_tm(pp, pres, wb, wres, j, c0, n):
        for kc in range(8):
            P.add('pe', lambda e, kc=kc: e.matmul(pp[:, 0:n], lhsT=xT[:, kc, j * 128:(j + 1) * 128], rhs=wb[:, kc, c0:c0 + n],
                                                  start=(kc == 0), stop=(kc == 7)),
                  reads=[wres, "xT%d" % j], writes=[pres])

    def ffn(l, s):
        for g0 in range(0, NFC, 3):
            n = min(3, NFC - g0)
            wb, wres = win.get(("ffn_in", l, s, g0))
            for c in range(n):
                pg, gres = proj_ps.next()
                pu, ures = proj_ps.next()
                proj_fm(pg, gres, wb, wres, c * 128)
                proj_fm(pu, ures, wb, wres, n * 128 + c * 128)
                sl, slres = silu_r.next()
                P.add('act', lambda e, sl=sl, pg=pg: e.activation(out=sl[:], in_=pg[:], func=AF.Silu), reads=[gres], writes=[slres])
                fc = g0 + c
                P.add('dve', lambda e, sl=sl, pu=pu, fc=fc: e.tensor_tensor(out=actT[:, fc, :], in0=sl[:], in1=pu[:], op=ALU.mult),
                      reads=[slres, ures], writes=["actT"])
            win.done()
        out_proj(NFC, ("ffn_out", l, s), l * 3 + (0 if s == 0 else 2), 0.5)

    def A(eng, fn, reads, writes):
        P.add(eng, fn, reads=reads, writes=writes)

    def gla(l, first_tile):
        mi = MIXIDX[l]
        P.fence()
        ar = Arena()
        Lb = ar.take([J, 512], F32)
        E3 = ar.take([J, 512], F32)
        E1 = ar.take([T], F32)
        E2 = ar.take([T], F32)
        gl_aug = ar.take([T], BF16, parts=17)
        qd = ar.take([T], BF16)
        ki = ar.take([T], BF16)
        kend = [ar.take([128], BF16) for _ in range(1)]
        vb = [ar.take([256], BF16) for _ in range(1)]
        rs = [ar.take([256], F32) for _ in range(1)]
        sTm = [ar.take([128], BF16) for _ in range(1)]
        ytmp = [ar.take([256], F32) for _ in range(1)]
        yb = [ar.take([256], BF16) for _ in range(1)]
        st = [ar.take([16], F32) for _ in range(2)]
        state, state_bf = gla_state[mi], gla_state_bf[mi]
        sres = "gla_st%d" % mi
        if first_tile:
            A('dve', lambda e: e.memset(state[:], 0.0), [], [sres])
            A('dve', lambda e: e.memset(state_bf[:], 0.0), [], [sres + "b"])
        A('dve', lambda e: e.memset(gl_aug[:], 1.0), [], ["g.gl"])
        wb, wres = win.get(("gla_g", l))
        pp, pres = proj_ps.next()
        proj_fm(pp, pres, wb, wres, 0, m=16)
        A('act', lambda e, pp=pp: e.copy(out=gl_aug[0:16, :], in_=pp[0:16, :]), [pres], ["g.gl"])
        win.done()
        for j in range(J):
            pp, pres = proj_ps.next()
            A('pe', lambda e, pp=pp, j=j: e.matmul(pp[:], lhsT=gl_aug[0:17, j * 128:(j + 1) * 128], rhs=gla_wg_sb[mi][0:17, :], start=True, stop=True),
              ["g.gl", "gla_wg%d" % mi], [pres])
            A('act', lambda e, pp=pp, j=j: e.activation(out=Lb[:, j, :], in_=pp[:], func=AF.Exp, scale=-1.0), [pres], ["g.L%d" % j])
            A('act', lambda e, j=j: e.activation(out=Lb[:, j, :], in_=Lb[:, j, :], func=AF.Ln, bias=cst[:, C_ONES:C_ONES + 1]), ["g.L%d" % j, "cst"], ["g.L%d" % j])
        for j in range(J):
            pp, pres = proj_ps.next()
            A('pe', lambda e, pp=pp, j=j: e.matmul(pp[:], lhsT=cs(C_U), rhs=Lb[:, j, :], start=True, stop=True), ["cst", "g.L%d" % j], [pres])
            A('act', lambda e, pp=pp, j=j: e.activation(out=E3[:, j, :], in_=pp[:], func=AF.Exp, scale=-1.0 / 16.0), [pres], ["g.E3%d" % j])
        LALL = ["g.L%d" % j for j in range(J)]
        for hh in range(GLA_H):
            wb, wres = win.get(("gla_h", l, hh))
            pc, pcres = proj_ps.next()
            for j in range(J):
                A('pe', lambda e, pc=pc, j=j, hh=hh: e.matmul(pc[:, j * 128:(j + 1) * 128], lhsT=Lb[:, j, hh * 128:(hh + 1) * 128], rhs=cs(C_TRI),
                                                         start=True, stop=True), ["cst"] + LALL, [pcres])
            A('act', lambda e, pc=pc: e.activation(out=E1[:], in_=pc[:], func=AF.Exp, scale=-1.0 / 16.0), [pcres], ["g.E1"])
            A('act', lambda e, pc=pc: e.activation(out=E2[:], in_=pc[:], func=AF.Exp, scale=1.0 / 16.0), [pcres], ["g.E2"])
            pq, pqres = proj_ps.next()
            proj_fm(pq, pqres, wb, wres, 0)
            A('dve', lambda e, pq=pq: e.scalar_tensor_tensor(out=qd[:], in0=pq[:], scalar=float(GLA_DK ** -0.5), in1=E1[:], op0=ALU.mult, op1=ALU.mult),
              [pqres, "g.E1"], ["g.qd"])
            pk, pkres = proj_ps.next()
            proj_fm(pk, pkres, wb, wres, 128)
            A('dve', lambda e, pk=pk: e.tensor_tensor(out=ki[:], in0=pk[:], in1=E2[:], op=ALU.mult), [pkres, "g.E2"], ["g.ki"])
            for j in range(J):
                q = 0
                js = slice(j * 128, (j + 1) * 128)
                pkv, pkvres = proj_ps.next()
                proj_tm(pkv, pkvres, wb, wres, j, 128, 384)
                A('dve', lambda e, pkv=pkv, q=q, j=j, hh=hh: e.tensor_tensor(out=kend[q][:], in0=pkv[:, 0:128], in1=E3[:, j, hh * 128:(hh + 1) * 128], op=ALU.mult),
                  [pkvres, "g.E3%d" % j], ["g.kend%d" % q])
                A('act', lambda e, pkv=pkv, q=q: e.copy(out=vb[q][:], in_=pkv[:, 128:384]), [pkvres], ["g.vb%d" % q])
                pr, prres = proj_ps.next()
                proj_tm(pr, prres, wb, wres, j, 512, 256)
                A('act', lambda e, pr=pr, q=q: e.activation(out=rs[q][:], in_=pr[:, 0:256], func=AF.Silu), [prres], ["g.rs%d" % q])
                psT, psTres = proj_ps.next()
                A('pe', lambda e, psT=psT, js=js: e.matmul(psT[:, 0:128], lhsT=ki[:, js], rhs=qd[:, js], start=True, stop=True), ["g.ki", "g.qd"], [psTres])
                A('dve', lambda e, psT=psT, q=q: e.tensor_tensor(out=sTm[q][:], in0=psT[:, 0:128], in1=cs(C_TRI), op=ALU.mult), [psTres, "cst"], ["g.sTm%d" % q])
                po, pores = proj_ps.next()
                A('pe', lambda e, po=po, q=q: e.matmul(po[:, 0:256], lhsT=sTm[q][:], rhs=vb[q][:], start=True, stop=False), ["g.sTm%d" % q, "g.vb%d" % q], [pores])
                A('pe', lambda e, po=po, js=js, hh=hh: e.matmul(po[:, 0:256], lhsT=qd[:, js], rhs=state_bf[:, hh, :], start=False, stop=True), ["g.qd", sres + "b"], [pores])
                psu, psures = proj_ps.next()
                A('pe', lambda e, psu=psu, q=q: e.matmul(psu[:, 0:256], lhsT=kend[q][:], rhs=vb[q][:], start=True, stop=True), ["g.kend%d" % q, "g.vb%d" % q], [psures])
                A('dve', lambda e, psu=psu, j=j, hh=hh: e.scalar_tensor_tensor(out=state[:, hh, :], in0=state[:, hh, :], scalar=E1[:, j * 128 + 127:j * 128 + 128],
                                                                           in1=psu[:, 0:256], op0=ALU.mult, op1=ALU.add), [psures, "g.E1", sres], [sres])
                A('act', lambda e, hh=hh: e.copy(out=state_bf[:, hh, :], in_=state[:, hh, :]), [sres], [sres + "b"])
                s_, s_res = st[q], "g.st%d" % q
                A('dve', lambda e, po=po, s_=s_: e.bn_stats(out=s_[:, 0:6], in_=po[:, 0:256]), [pores], [s_res])
                A('dve', lambda e, s_=s_: e.bn_aggr(out=s_[:, 6:8], in_=s_[:, 0:6]), [s_res], [s_res])
                A('dve', lambda e, s_=s_: e.scalar_tensor_tensor(out=s_[:, 8:9], in0=s_[:, 6:7], scalar=s_[:, 6:7], in1=s_[:, 7:8], op0=ALU.mult, op1=ALU.add), [s_res], [s_res])
                rstd_from_var(s_[:, 8:9], s_[:, 9:10], s_res, 1e-6)
                A('dve', lambda e, po=po, s_=s_, q=q: e.scalar_tensor_tensor(out=ytmp[q][:], in0=po[:, 0:256], scalar=s_[:, 9:10], in1=gla_ng_sb[mi][:],
                                                                      op0=ALU.mult, op1=ALU.mult), [pores, s_res, "gla_ng%d" % mi], ["g.ytmp%d" % q])
                A('dve', lambda e, q=q: e.tensor_tensor(out=yb[q][:], in0=ytmp[q][:], in1=rs[q][:], op=ALU.mult), ["g.ytmp%d" % q, "g.rs%d" % q], ["g.yb%d" % q])
                transposes(yb[q], "g.yb%d" % q, 2, lambda b0, n, j=j, hh=hh: actT[:, 2 * hh + b0:2 * hh + b0 + n, j * 128:(j + 1) * 128], "actT")
            win.done()
        P.fence()
        out_proj(8, ("mix_out", l), l * 3 + 1, 1.0)

    def ret(l, first_tile, pos0):
        P.fence()
        ar = Arena()
        sg = ar.take([J, 512], BF16)
        qr = ar.take([2, T], BF16)
        kr = ar.take([2, T], BF16)
        kd = ar.take([J, 256], BF16)
        vb = ar.take([J, 512], BF16)
        t1 = ar.take([T], F32)
        t2 = ar.take([T], F32)
        sTm = [ar.take([128], BF16) for _ in range(2)]
        ytmp = [ar.take([512], F32) for _ in range(2)]
        yb = [ar.take([512], BF16) for _ in range(2)]
        st = [ar.take([16], F32) for _ in range(2)]
        state, state_bf, sres = ret_state, ret_state_bf, "ret_st"
        if first_tile:
            A('dve', lambda e: e.memset(state[:], 0.0), [], [sres])
            A('dve', lambda e: e.memset(state_bf[:], 0.0), [], [sres + "b"])
        P.add('sp', lambda e: e.dma_start(out=rope_sb[:, 0, :], in_=rope_c[:, pos0:pos0 + T]), writes=["rope"], chan="rope")
        P.add('sp', lambda e: e.dma_start(out=rope_sb[:, 1, :], in_=rope_s[:, pos0:pos0 + T]), writes=["rope"], chan="rope")
        cosb, sinb = rope_sb[:, 0, :], rope_sb[:, 1, :]

        def rotary(pa, pares, pb_, pbres, dst, dres):
            A('dve', lambda e: e.tensor_tensor(out=t1[:], in0=pa[:], in1=cosb, op=ALU.mult), [pares, "rope"], ["r.t1"])
            A('dve', lambda e: e.tensor_tensor(out=t2[:], in0=pb_[:], in1=sinb, op=ALU.mult), [pbres, "rope"], ["r.t2"])
            A('dve', lambda e: e.tensor_tensor(out=dst[:, 0, :], in0=t1[:], in1=t2[:], op=ALU.subtract), ["r.t1", "r.t2"], [dres])
            A('dve', lambda e: e.tensor_tensor(out=t1[:], in0=pa[:], in1=sinb, op=ALU.mult), [pares, "rope", dres], ["r.t1"])
            A('dve', lambda e: e.tensor_tensor(out=t2[:], in0=pb_[:], in1=cosb, op=ALU.mult), [pbres, "rope", dres], ["r.t2"])
            A('dve', lambda e: e.tensor_tensor(out=dst[:, 1, :], in0=t1[:], in1=t2[:], op=ALU.add), ["r.t1", "r.t2"], [dres])

        for hh in range(RET_H):
            wb, wres = win.get(("ret_g", l, hh))
            for j in range(J):
                pp, pres = proj_ps.next()
                proj_tm(pp, pres, wb, wres, j, 0, 512)
                A('act', lambda e, pp=pp, j=j: e.activation(out=sg[:, j, :], in_=pp[:], func=AF.Silu), [pres], ["r.sg"])
            win.done()
            wb, wres = win.get(("ret_qk", l, hh))
            pa, pares = proj_ps.next(); proj_fm(pa, pares, wb, wres, 0)
            pb_, pbres = proj_ps.next(); proj_fm(pb_, pbres, wb, wres, 128)
            rotary(pa, pares, pb_, pbres, qr, "r.qr")
            pa, pares = proj_ps.next(); proj_fm(pa, pares, wb, wres, 256)
            pb_, pbres = proj_ps.next(); proj_fm(pb_, pbres, wb, wres, 384)
            rotary(pa, pares, pb_, pbres, kr, "r.kr")
            win.done()
            wb, wres = win.get(("ret_v", l, hh))
            for j in range(J):
                pt, ptres = tr_ps.next()
                for dc in range(2):
                    A('pe', lambda e, pt=pt, dc=dc, j=j: e.transpose(pt[:, dc, :], kr[:, dc, j * 128:(j + 1) * 128], ident[:]), ["r.kr", "ident"], [ptres])
                A('dve', lambda e, pt=pt, j=j, hh=hh: e.tensor_scalar_mul(out=kd[:, j, :].rearrange("p (a b) -> p a b", a=2), in0=pt[:, 0:2, :],
                                                                    scalar1=cst[:, C_RDK + hh:C_RDK + hh + 1]), [ptres, "cst"], ["r.kd"])
                pp, pres = proj_ps.next()
                proj_tm(pp, pres, wb, wres, j, 0, 512)
                A('act', lambda e, pp=pp, j=j: e.copy(out=vb[:, j, :], in_=pp[:]), [pres], ["r.vb"])
            for j in range(J):
                q = j % 2
                js = slice(j * 128, (j + 1) * 128)
                psT, psTres = proj_ps.next()
                for dc in range(2):
                    A('pe', lambda e, psT=psT, dc=dc, js=js: e.matmul(psT[:, 0:128], lhsT=kr[:, dc, js], rhs=qr[:, dc, js], start=(dc == 0), stop=(dc == 1)),
                      ["r.kr", "r.qr"], [psTres])
                A('dve', lambda e, psT=psT, q=q, hh=hh: e.tensor_tensor(out=sTm[q][:], in0=psT[:, 0:128], in1=cs(C_DEC + hh * 128), op=ALU.mult),
                  [psTres, "cst"], ["r.sTm%d" % q])
                po, pores = proj_ps.next()
                A('pe', lambda e, po=po, q=q, j=j: e.matmul(po[:], lhsT=sTm[q][:], rhs=vb[:, j, :], start=True, stop=False), ["r.sTm%d" % q, "r.vb"], [pores])
                for dc in range(2):
                    A('pe', lambda e, po=po, dc=dc, js=js, hh=hh: e.matmul(po[:], lhsT=qr[:, dc, js], rhs=state_bf[:, hh * 2 + dc, :], start=False, stop=(dc == 1)),
                      ["r.qr", sres + "b"], [pores])
                for dc in range(2):
                    psu, psures = proj_ps.next()
                    A('pe', lambda e, psu=psu, dc=dc, j=j: e.matmul(psu[:], lhsT=kd[:, j, dc * 128:(dc + 1) * 128], rhs=vb[:, j, :], start=True, stop=True),
                      ["r.kd", "r.vb"], [psures])
                    A('dve', lambda e, psu=psu, dc=dc, hh=hh: e.scalar_tensor_tensor(out=state[:, hh * 2 + dc, :], in0=state[:, hh * 2 + dc, :],
                                                                               scalar=float(RET_GAMMA[hh] ** 128), in1=psu[:], op0=ALU.mult, op1=ALU.add),
                      [psures, sres], [sres])
                    A('act', lambda e, dc=dc, hh=hh: e.copy(out=state_bf[:, hh * 2 + dc, :], in_=state[:, hh * 2 + dc, :]), [sres], [sres + "b"])
                s_, s_res = st[q], "r.st%d" % q
                A('dve', lambda e, po=po, s_=s_: e.bn_stats(out=s_[:, 0:6], in_=po[:]), [pores], [s_res])
                A('dve', lambda e, s_=s_: e.bn_aggr(out=s_[:, 6:8], in_=s_[:, 0:6]), [s_res], [s_res])
                A('dve', lambda e, s_=s_, hh=hh: e.tensor_tensor(out=s_[:, 8:9], in0=s_[:, 7:8], in1=cst[:, C_RC2 + hh:C_RC2 + hh + 1], op=ALU.mult), [s_res, "cst"], [s_res])
                rstd_from_var(s_[:, 8:9], s_[:, 9:10], s_res, 1e-6)
                A('dve', lambda e, s_=s_, hh=hh: e.tensor_tensor(out=s_[:, 10:11], in0=s_[:, 9:10], in1=cst[:, C_RC + hh:C_RC + hh + 1], op=ALU.mult), [s_res, "cst"], [s_res])
                A('dve', lambda e, po=po, s_=s_, q=q: e.tensor_scalar(out=ytmp[q][:], in0=po[:], scalar1=s_[:, 6:7], scalar2=s_[:, 10:11],
                                                               op0=ALU.subtract, op1=ALU.mult), [pores, s_res], ["r.ytmp%d" % q])
                A('dve', lambda e, q=q: e.tensor_tensor(out=ytmp[q][:], in0=ytmp[q][:], in1=ret_ng_sb[:], op=ALU.mult), ["r.ytmp%d" % q, "ret_ng"], ["r.ytmp%d" % q])
                A('dve', lambda e, q=q, j=j: e.tensor_tensor(out=yb[q][:], in0=ytmp[q][:], in1=sg[:, j, :], op=ALU.mult), ["r.ytmp%d" % q, "r.sg"], ["r.yb%d" % q])
                transposes(yb[q], "r.yb%d" % q, 4, lambda b0, n, j=j, hh=hh: actT[:, 4 * hh + b0:4 * hh + b0 + n, j * 128:(j + 1) * 128], "actT")
            win.done()
        P.fence()
        out_proj(16, ("mix_out", l), l * 3 + 1, 1.0)

    def ssd(l, first_tile):
        P.fence()
        ar = Arena()
        dt = ar.take([J, 32], F32)
        dtA = ar.take([J, 32], F32)
        negcum = ar.take([J, 32], F32)
        ecum = ar.take([J, 32], F32)
        declast = ar.take([J, 32], F32)
        toend = ar.take([J, 32], F32)
        sz = ar.take([J, 512], BF16)
        stage = [ar.take([T + 4], F32) for _ in range(1)]
        acc = [ar.take([T], F32) for _ in range(1)]
        xfm = [ar.take([T], BF16) for _ in range(2)]
        x_tok = ar.take([J, 512], BF16)
        BT = ar.take([T], BF16)
        CT = ar.take([T], BF16)
        B_tok = ar.take([J, 128], BF16)
        cbm = [ar.take([128], F32) for _ in range(2)]
        wseg = [ar.take([128], F32) for _ in range(2)]
        wT = [ar.take([128], BF16) for _ in range(2)]
        dbc = [ar.take([128], F32) for _ in range(2)]
        ta = [ar.take([512], F32) for _ in range(1)]
        tb = [ar.take([512], F32) for _ in range(1)]
        xs = [ar.take([512], BF16) for _ in range(1)]
        yb = [ar.take([512], BF16) for _ in range(1)]
        st = [ar.take([16], F32) for _ in range(2)]
        state, state_bf, sres = ssd_state, ssd_state_bf, "ssd_st"
        if first_tile:
            A('dve', lambda e: e.memset(state[:], 0.0), [], [sres])
            A('dve', lambda e: e.memset(state_bf[:], 0.0), [], [sres + "b"])
            A('dve', lambda e: e.memset(ssd_carry[:], 0.0), [], ["ssd_carry"])
        wb, wres = win.get(("ssd_dt", l))
        for j in range(J):
            pp, pres = proj_ps.next()
            proj_tm(pp, pres, wb, wres, j, 0, 32)
            A('dve', lambda e, pp=pp, j=j: e.tensor_tensor(out=dt[:, j, :], in0=pp[:, 0:32], in1=ssd_vec_sb[:, 0, :], op=ALU.add), [pres, "ssd_vec"], ["s.dt"])
        win.done()
        A('act', lambda e: e.activation(out=dt[:], in_=dt[:], func=AF.Exp), ["s.dt"], ["s.dt"])
        A('act', lambda e: e.activation(out=dt[:], in_=dt[:], func=AF.Ln, bias=cst[:, C_ONES:C_ONES + 1]), ["s.dt", "cst"], ["s.dt"])
        for j in range(J):
            A('dve', lambda e, j=j: e.tensor_tensor(out=dtA[:, j, :], in0=dt[:, j, :], in1=ssd_nega[:], op=ALU.mult), ["s.dt", "ssd_nega"], ["s.dtA"])
        for j in range(J):
            pp, pres = proj_ps.next()
            A('pe', lambda e, pp=pp, j=j: e.matmul(pp[:, 0:32], lhsT=cs(C_TRI), rhs=dtA[:, j, :], start=True, stop=True), ["cst", "s.dtA"], [pres])
            A('pe', lambda e, pp=pp, j=j: e.matmul(pp[:, 32:64], lhsT=cs(C_ONES), rhs=dtA[:, j, :], start=True, stop=True), ["cst", "s.dtA"], [pres])
            A('pe', lambda e, pp=pp, j=j: e.matmul(pp[:, 64:96], lhsT=cs(C_U), rhs=dtA[:, j, :], start=True, stop=True), ["cst", "s.dtA"], [pres])
            A('act', lambda e, pp=pp, j=j: e.activation(out=ecum[:, j, :], in_=pp[:, 0:32], func=AF.Exp), [pres], ["s.ecum"])
            A('dve', lambda e, pp=pp, j=j: e.tensor_scalar_mul(out=negcum[:, j, :], in0=pp[:, 0:32], scalar1=-1.0), [pres], ["s.negcum"])
            A('act', lambda e, pp=pp, j=j: e.activation(out=declast[:, j, :], in_=pp[:, 32:64], func=AF.Exp), [pres], ["s.declast"])
            A('act', lambda e, pp=pp, j=j: e.activation(out=toend[:, j, :], in_=pp[:, 64:96], func=AF.Exp), [pres], ["s.toend"])
            A('dve', lambda e, j=j: e.tensor_tensor(out=toend[:, j, :], in0=toend[:, j, :], in1=dt[:, j, :], op=ALU.mult), ["s.toend", "s.dt"], ["s.toend"])
        for g in range(SSD_G):
            P.add('sp', lambda e, g=g: e.dma_start(out=ssd_ngd_sb[:], in_=ssd_ngd[g]), writes=["ssd_ngd"], chan="ssd_ngd")
            wb, wres = win.get(("ssd_z", l, g))
            for j in range(J):
                pp, pres = proj_ps.next()
                proj_tm(pp, pres, wb, wres, j, 0, 512)
                A('act', lambda e, pp=pp, j=j: e.activation(out=sz[:, j, :], in_=pp[:], func=AF.Silu), [pres], ["s.sz"])
            win.done()
            wb, wres = win.get(("ssd_x", l, g))
            for ci_loc in range(6):
                ci = (g * 4 + ci_loc) if ci_loc < 4 else (16 + g if ci_loc == 4 else 20 + g)
                q = ci_loc % 2
                sg_, sgres = stage[0], "s.stage0"
                ac, acres = acc[0], "s.acc0"
                pp, pres = proj_ps.next()
                proj_fm(pp, pres, wb, wres, ci_loc * 128)
                A('dve', lambda e, sg_=sg_, ci=ci: e.tensor_copy(out=sg_[:, 0:3], in_=ssd_carry[:, ci, 0:3]), ["ssd_carry"], [sgres])
                A('act', lambda e, sg_=sg_, pp=pp: e.copy(out=sg_[:, 3:3 + T], in_=pp[:]), [pres], [sgres])
                A('dve', lambda e, sg_=sg_, ci=ci: e.tensor_copy(out=ssd_carry[:, ci, 0:3], in_=sg_[:, T:T + 3]), [sgres], ["ssd_carry"])
                A('dve', lambda e, sg_=sg_, ac=ac, ci=ci: e.tensor_scalar(out=ac[:], in0=sg_[:, 3:3 + T], scalar1=ssd_cw_sb[:, ci, 3:4], scalar2=ssd_cb_sb[:, ci:ci + 1],
                                                                  op0=ALU.mult, op1=ALU.add), [sgres, "ssd_cw", "ssd_cb"], [acres])
                for k in range(3):
                    A('dve', lambda e, sg_=sg_, ac=ac, ci=ci, k=k: e.scalar_tensor_tensor(out=ac[:], in0=sg_[:, k:k + T], scalar=ssd_cw_sb[:, ci, k:k + 1], in1=ac[:],
                                                                                   op0=ALU.mult, op1=ALU.add), [sgres, "ssd_cw", acres], [acres])
                if ci_loc < 4:
                    xf, xfres = xfm[q], "s.xfm%d" % q
                    A('act', lambda e, xf=xf, ac=ac: e.activation(out=xf[:], in_=ac[:], func=AF.Silu), [acres], [xfres])
                    for j in range(J):
                        pt, ptres = tr_ps.next()
                        A('pe', lambda e, pt=pt, xf=xf, j=j: e.transpose(pt[:, 0, :], xf[:, j * 128:(j + 1) * 128], ident[:]), [xfres, "ident"], [ptres])
                        A('act', lambda e, pt=pt, j=j, c=ci_loc: e.copy(out=x_tok[:, j, c * 128:(c + 1) * 128], in_=pt[:, 0, :]), [ptres], ["s.xtok"])
                elif ci_loc == 4:
                    A('act', lambda e, ac=ac: e.activation(out=BT[:], in_=ac[:], func=AF.Silu), [acres], ["s.BT"])
                    for j in range(J):
                        pt, ptres = tr_ps.next()
                        A('pe', lambda e, pt=pt, j=j: e.transpose(pt[:, 0, :], BT[:, j * 128:(j + 1) * 128], ident[:]), ["s.BT", "ident"], [ptres])
                        A('act', lambda e, pt=pt, j=j: e.copy(out=B_tok[:, j, :], in_=pt[:, 0, :]), [ptres], ["s.Btok"])
                else:
                    A('act', lambda e, ac=ac: e.activation(out=CT[:], in_=ac[:], func=AF.Silu), [acres], ["s.CT"])
            win.done()
            for j in range(J):
                q = j % 2
                js = slice(j * 128, (j + 1) * 128)
                pcb, pcbres = proj_ps.next()
                A('pe', lambda e, pcb=pcb, js=js: e.matmul(pcb[:, 0:128], lhsT=BT[:, js], rhs=CT[:, js], start=True, stop=True), ["s.BT", "s.CT"], [pcbres])
                A('dve', lambda e, pcb=pcb, q=q: e.tensor_tensor(out=cbm[q][:], in0=pcb[:, 0:128], in1=cs(C_TRI), op=ALU.mult), [pcbres, "cst"], ["s.cbm%d" % q])
                py, pyres = proj_ps.next()
                for half in range(2):
                    pseg, psegres = proj_ps.next()
                    for e4 in range(4):
                        eh = g * 8 + half * 4 + e4
                        w_, wres_ = dbc[e4 % 2], "s.dbc%d" % (e4 % 2)
                        A('dve', lambda e, w_=w_, j=j, eh=eh: e.tensor_scalar_mul(out=w_[:], in0=cs(C_ONES), scalar1=dtA[:, j, eh:eh + 1]), ["cst", "s.dtA"], [wres_])
                        A('pe', lambda e, pseg=pseg, e4=e4, w_=w_: e.matmul(pseg[:, e4 * 128:(e4 + 1) * 128], lhsT=w_[:], rhs=cs(C_TRI), start=True, stop=False),
                          [wres_, "cst"], [psegres])
                        A('pe', lambda e, pseg=pseg, e4=e4: e.matmul(pseg[:, e4 * 128:(e4 + 1) * 128], lhsT=cs(C_ID), rhs=cs(C_NEG4), start=False, stop=True),
                          ["cst"], [psegres])
                    for e4 in range(4):
                        eh = g * 8 + half * 4 + e4
                        el = half * 4 + e4
                        ws_, wsres = wseg[e4 % 2], "s.wseg%d" % (e4 % 2)
                        wt_, wtres = wT[e4 % 2], "s.wT%d" % (e4 % 2)
                        A('act', lambda e, pseg=pseg, e4=e4, ws_=ws_, j=j, eh=eh: e.activation(out=ws_[:], in_=pseg[:, e4 * 128:(e4 + 1) * 128], func=AF.Exp,
                                                                                       bias=negcum[:, j, eh:eh + 1], scale=1.0), [psegres, "s.negcum"], [wsres])
                        A('dve', lambda e, ws_=ws_, wt_=wt_, j=j, eh=eh, q=q: e.scalar_tensor_tensor(out=wt_[:], in0=ws_[:], scalar=dt[:, j, eh:eh + 1], in1=cbm[q][:],
                                                                                             op0=ALU.mult, op1=ALU.mult), [wsres, "s.dt", "s.cbm%d" % q], [wtres])
                        A('pe', lambda e, py=py, wt_=wt_, j=j, el=el: e.matmul(py[:, el * 64:(el + 1) * 64], lhsT=wt_[:], rhs=x_tok[:, j, el * 64:(el + 1) * 64], start=True, stop=True),
                          [wtres, "s.xtok"], [pyres])
                pint, pintres = proj_ps.next()
                A('pe', lambda e, pint=pint, js=js, g=g: e.matmul(pint[:], lhsT=CT[:, js], rhs=state_bf[:, g, :], start=True, stop=True), ["s.CT", sres + "b"], [pintres])
                ta_, tares = ta[0], "s.ta0"
                tb_, tbres = tb[0], "s.tb0"
                for el in range(8):
                    eh = g * 8 + el
                    A('dve', lambda e, pint=pint, ta_=ta_, el=el, j=j, eh=eh: e.tensor_scalar_mul(out=ta_[:, el * 64:(el + 1) * 64], in0=pint[:, el * 64:(el + 1) * 64],
                                                                                          scalar1=ecum[:, j, eh:eh + 1]), [pintres, "s.ecum"], [tares])
                A('dve', lambda e, py=py, ta_=ta_: e.tensor_tensor(out=ta_[:], in0=ta_[:], in1=py[:], op=ALU.add), [tares, pyres], [tares])
                A('dve', lambda e, tb_=tb_, j=j: e.tensor_tensor(out=tb_[:], in0=x_tok[:, j, :], in1=ssd_ngd_sb[:, 1, :], op=ALU.mult), ["s.xtok", "ssd_ngd"], [tbres])
                A('dve', lambda e, ta_=ta_, tb_=tb_: e.tensor_tensor(out=ta_[:], in0=ta_[:], in1=tb_[:], op=ALU.add), [tares, tbres], [tares])
                A('dve', lambda e, ta_=ta_, j=j: e.tensor_tensor(out=ta_[:], in0=ta_[:], in1=sz[:, j, :], op=ALU.mult), [tares, "s.sz"], [tares])
                s_, s_res = st[q], "s.st%d" % q
                A('dve', lambda e, ta_=ta_, s_=s_: e.bn_stats(out=s_[:, 0:6], in_=ta_[:]), [tares], [s_res])
                A('dve', lambda e, s_=s_: e.bn_aggr(out=s_[:, 6:8], in_=s_[:, 0:6]), [s_res], [s_res])
                A('dve', lambda e, s_=s_: e.scalar_tensor_tensor(out=s_[:, 8:9], in0=s_[:, 6:7], scalar=s_[:, 6:7], in1=s_[:, 7:8], op0=ALU.mult, op1=ALU.add), [s_res], [s_res])
                rstd_from_var(s_[:, 8:9], s_[:, 9:10], s_res, 1e-6)
                A('dve', lambda e, ta_=ta_, s_=s_, q=q: e.scalar_tensor_tensor(out=yb[0][:], in0=ta_[:], scalar=s_[:, 9:10], in1=ssd_ngd_sb[:, 0, :], op0=ALU.mult, op1=ALU.mult),
                  [tares, s_res, "ssd_ngd"], ["s.yb0"])
                transposes(yb[0], "s.yb0", 4, lambda b0, n, j=j, g=g: actT[:, 4 * g + b0:4 * g + b0 + n, j * 128:(j + 1) * 128], "actT")
                xs_, xsres = xs[0], "s.xs0"
                for el in range(8):
                    eh = g * 8 + el
                    A('dve', lambda e, xs_=xs_, el=el, j=j, eh=eh: e.tensor_scalar_mul(out=xs_[:, el * 64:(el + 1) * 64], in0=x_tok[:, j, el * 64:(el + 1) * 64],
                                                                                scalar1=toend[:, j, eh:eh + 1]), ["s.xtok", "s.toend"], [xsres])
                psu, psures = proj_ps.next()
                A('pe', lambda e, psu=psu, xs_=xs_, j=j: e.matmul(psu[:], lhsT=B_tok[:, j, :], rhs=xs_[:], start=True, stop=True), ["s.Btok", xsres], [psures])
                for el in range(8):
                    eh = g * 8 + el
                    A('dve', lambda e, psu=psu, el=el, j=j, eh=eh, g=g: e.scalar_tensor_tensor(out=state[:, g, el * 64:(el + 1) * 64], in0=state[:, g, el * 64:(el + 1) * 64],
                                                                                        scalar=declast[:, j, eh:eh + 1], in1=psu[:, el * 64:(el + 1) * 64],
                                                                                        op0=ALU.mult, op1=ALU.add), [psures, "s.declast", sres], [sres])
                A('act', lambda e, g=g: e.copy(out=state_bf[:, g, :], in_=state[:, g, :]), [sres], [sres + "b"])
        P.fence()
        out_proj(16, ("mix_out", l), l * 3 + 1, 1.0)

    for sq in range(n_seq):
        for tt in range(ntile_seq):
            t0 = sq * seq_len + tt * T
            for j in range(J):
                P.add('sp', lambda e, j=j, t0=t0: e.dma_start(out=h[:, j, :], in_=x_d[t0 + j * 128:t0 + (j + 1) * 128, :]),
                      writes=["h%d" % j], chan="xin%d" % j)
                hbt, hbres = hb_r.next()
                P.add('act', lambda e, hbt=hbt, j=j: e.copy(out=hbt[:], in_=h[:, j, :]), reads=["h%d" % j], writes=[hbres])
                transpose_to_xT(hbt, hbres, j)
            for l in layers:
                for stg in stages:
                    if stg == 'ffn1':
                        ffn(l, 0)
                    elif stg == 'ffn2':
                        ffn(l, 1)
                    elif MIX[l] == 'gla':
                        gla(l, tt == 0)
                    elif MIX[l] == 'ret':
                        ret(l, tt == 0, tt * T)
                    else:
                        ssd(l, tt == 0)
            for j in range(J):
                P.add('sp', lambda e, j=j, t0=t0: e.dma_start(out=y_d[t0 + j * 128:t0 + (j + 1) * 128, :], in_=h[:, j, :]),
                      reads=["h%d" % j], chan="yout%d" % j)
                P.out_chans.add("yout%d" % j)

    P.emit(nc)
    es.close()
    return nc, P


def host_inputs(inputs, n_seq, seq_len):
    f = lambda a: np.ascontiguousarray(np.asarray(a, dtype=np.float32))
    ln_g, ln_b = f(inputs['ln_g']), f(inputs['ln_b'])
    ln_gb = np.empty((DEPTH * 3, 2, 128, D), np.float32)
    for i in range(DEPTH):
        for k in range(3):
            ln_gb[i * 3 + k, 0] = ln_g[i, k][None, :]
            ln_gb[i * 3 + k, 1] = ln_b[i, k][None, :]
    gla_wg = np.concatenate([f(inputs['gla_w_gate']), f(inputs['gla_b_gate'])[:, None, :]], axis=1)
    gla_ng = np.ascontiguousarray(np.broadcast_to(f(inputs['gla_norm_g'])[:, None, :], (2, 128, 256)))
    ret_ng = np.ascontiguousarray(np.broadcast_to(f(inputs['ret_norm_g'])[0][None, :], (128, 512)))
    cw = f(inputs['ssd_conv_w'])[0]
    ssd_cw = np.ascontiguousarray(cw.T.reshape(24, 128, 4).transpose(1, 0, 2))
    ssd_cb = np.ascontiguousarray(f(inputs['ssd_conv_b'])[0].reshape(24, 128).T)
    vec = np.stack([f(inputs['ssd_dt_bias'])[0], f(inputs['ssd_a_log'])[0], f(inputs['ssd_d'])[0]], 0)
    ssd_vec = np.ascontiguousarray(np.broadcast_to(vec[None], (128, 3, 32)))
    ng = f(inputs['ssd_norm_g'])[0].reshape(4, 512)
    dsk = np.repeat(f(inputs['ssd_d'])[0], 64).reshape(4, 512)
    ngd = np.stack([ng, dsk], 1)
    ssd_ngd = np.ascontiguousarray(np.broadcast_to(ngd[:, None], (4, 128, 2, 512)))
    rc, rs_ = make_rope(seq_len)
    return dict(
        ffn_w_in=f(inputs['ffn_w_in']), ffn_w_out=f(inputs['ffn_w_out']), ln_gb=ln_gb, cst=make_consts(),
        gla_w_in=f(inputs['gla_w_in']), gla_wg=np.ascontiguousarray(gla_wg), gla_ng=gla_ng, gla_w_out=f(inputs['gla_w_out']),
        ret_w_in=f(inputs['ret_w_in']), ret_ng=ret_ng, ret_w_out=f(inputs['ret_w_out']), rope_cos=rc, rope_sin=rs_,
        ssd_w_in=f(inputs['ssd_w_in']), ssd_cw=ssd_cw, ssd_cb=ssd_cb, ssd_vec=ssd_vec, ssd_ngd=ssd_ngd, ssd_w_out=f(inputs['ssd_w_out']))


def kernel(**inputs):
    x = np.asarray(inputs['x'], dtype=np.float32)
    B, L, _ = x.shape
    n_seq = B // NCORES
    nc, _ = build_program(n_seq, L)
    shared = host_inputs(inputs, n_seq, L)
    in_maps = []
    for c in range(NCORES):
        m = dict(shared)
        m['x'] = np.ascontiguousarray(x[c * n_seq:(c + 1) * n_seq].reshape(n_seq * L, D))
        in_maps.append(m)
    res = run_bass_kernel_spmd(nc, in_maps, core_ids=list(range(NCORES)))
    out = np.concatenate([np.asarray(r["y"]).reshape(n_seq, L, D) for r in res.results], axis=0)
    return out.astype(np.float32)
```

```python
import math
import numpy as np
import concourse.bass as bass
import concourse.mybir as mybir
from concourse.bass_utils import run_bass_kernel_spmd

F32 = mybir.dt.float32
BF16 = mybir.dt.bfloat16
AF = mybir.ActivationFunctionType
ALU = mybir.AluOpType

D = 1024
DEPTH = 4
DFF = 2816
NFC = DFF // 128
T = 512
J = T // 128
ALPHA = (2 * DEPTH) ** 0.25
LN_EPS_EFF = 1e-5 / (ALPHA * ALPHA)
NCORES = 8

COMPUTE = ('pe', 'act', 'dve', 'pool')


class Prog:
    def __init__(self):
        self.ops = []
        self.last_w = {}
        self.readers = {}
        self.chan_count = {}
        self.fence_last = {}
        self.fence_pending = set()
        self.last_on = {}
        self.out_chans = set()

    def add(self, eng, fn, reads=(), writes=(), chan=None, after=()):
        idx = len(self.ops)
        deps = {}
        def dep(i):
            o = self.ops[i]
            if o['chan'] is not None:
                deps[i] = self.chan_count[o['chan']]
            else:
                deps[i] = None
        for i in after:
            dep(i)
        for r in reads:
            if r in self.last_w:
                dep(self.last_w[r])
            if r.startswith("pb") or r.startswith("ptr"):
                for i in self.readers.get(r, ()):
                    if self.ops[i]['eng'] != eng:
                        dep(i)
        for w in writes:
            if w in self.last_w:
                dep(self.last_w[w])
            for i in self.readers.get(w, ()):
                dep(i)
        if eng in self.fence_pending:
            self.fence_pending.discard(eng)
            for e, i in self.fence_last.items():
                if e != eng:
                    dep(i)
        for r in reads:
            self.readers.setdefault(r, []).append(idx)
        for w in writes:
            self.last_w[w] = idx
            self.readers[w] = []
        op = dict(eng=eng, fn=fn, deps=deps, chan=chan, mark=False, cnt=0)
        if chan is not None:
            self.chan_count[chan] = self.chan_count.get(chan, 0) + 16
        self.ops.append(op)
        if chan is None:
            self.last_on[eng] = idx
        return idx

    def fence(self):
        self.fence_last = dict(self.last_on)
        self.fence_pending = set(COMPUTE)

    def emit(self, nc):
        ops = self.ops
        for X in ops:
            for i in X['deps']:
                Dp = ops[i]
                if Dp['chan'] is None and not (Dp['eng'] == 'pe' and X['eng'] == 'pe' and X['chan'] is None):
                    Dp['mark'] = True
        cnt = {}
        for X in ops:
            if X['chan'] is None and X['mark']:
                cnt[X['eng']] = cnt.get(X['eng'], 0) + 1
                X['cnt'] = cnt[X['eng']]
        chans = sorted(self.chan_count)
        import contextlib
        with contextlib.ExitStack() as es:
            sems = {}
            for e in ('pe', 'act', 'dve', 'pool'):
                sems[('eng', e)] = es.enter_context(nc.semaphore("s_" + e))
            for c in chans:
                sems[('chan', c)] = es.enter_context(nc.semaphore("c_" + c))
            block = es.enter_context(nc.Block())
            by_eng = {}
            for X in ops:
                by_eng.setdefault(X['eng'], []).append(X)

            def run(E, eng):
                waited = {}
                for X in by_eng.get(E, []):
                    waits = {}
                    for i, cv in X['deps'].items():
                        Dp = ops[i]
                        if Dp['chan'] is not None:
                            key, val = ('chan', Dp['chan']), cv
                        elif Dp['eng'] == 'pe' and E == 'pe' and X['chan'] is None:
                            continue
                        else:
                            key, val = ('eng', Dp['eng']), Dp['cnt']
                        if val > waits.get(key, 0):
                            waits[key] = val
                    for key, val in waits.items():
                        if val > waited.get(key, 0):
                            eng.wait_ge(sems[key], val)
                            waited[key] = val
                    ins = X['fn'](eng)
                    if X['chan'] is not None:
                        ins.then_inc(sems[('chan', X['chan'])], 16)
                    elif X['mark']:
                        ins.then_inc(sems[('eng', E)], 1)
                if E == 'sp':
                    for c in sorted(self.out_chans):
                        eng.wait_ge(sems[('chan', c)], self.chan_count[c])

            @block.tensor
            def _(e):
                run('pe', e)

            @block.scalar
            def _(e):
                run('act', e)

            @block.vector
            def _(e):
                run('dve', e)

            @block.gpsimd
            def _(e):
                run('pool', e)

            @block.sync
            def _(e):
                run('sp', e)


class Rot:
    def __init__(self, items):
        self.items = items
        self.i = 0

    def next(self):
        it = self.items[self.i % len(self.items)]
        self.i += 1
        return it


class WStream:
    def __init__(self, P, name, slots, plan):
        self.P, self.name, self.slots, self.plan = P, name, slots, plan
        self.issued = 0
        self.used = 0
        for _ in range(len(slots)):
            self._issue()

    def _issue(self):
        if self.issued >= len(self.plan):
            return
        n = self.issued
        self.issued += 1
        s = n % len(self.slots)
        buf = self.slots[s]
        res = "%s%d" % (self.name, s)
        g = self.plan[n]
        dst, src = g['load'](buf)
        self.P.add('sp', (lambda e, d=dst, s_=src: e.dma_start(out=d, in_=s_)),
                   writes=[res], chan=res, after=g['after'])

    def get(self, tag):
        n = self.used
        assert self.plan[n]['tag'] == tag, (self.plan[n]['tag'], tag)
        s = n % len(self.slots)
        return self.slots[s], "%s%d" % (self.name, s)

    def done(self):
        self.used += 1
        self._issue()


GLA_H, GLA_DK, GLA_DV = 4, 128, 256
RET_H, RET_DK, RET_DV = 4, 256, 512
SSD_G, SSD_HPG, SSD_P, SSD_N = 4, 8, 64, 128
MIX = ['gla', 'ret', 'ssd', 'gla']
MIXIDX = [0, 0, 0, 1]
RET_GAMMA = [1.0 - 2.0 ** (-5.0 - hh) for hh in range(RET_H)]
C_ID, C_TRI, C_U, C_NEG4, C_ONES, C_DEC, C_RC, C_RC2, C_RDK, C_END = 0, 128, 256, 384, 896, 1024, 1536, 1540, 1544, 1548
NEG = -30000.0


def make_consts():
    c = np.zeros((128, C_END), np.float32)
    r = np.arange(128)[:, None].astype(np.float64)
    i = np.arange(128)[None, :].astype(np.float64)
    c[:, C_ID:C_ID + 128] = np.eye(128)
    c[:, C_TRI:C_TRI + 128] = (r <= i)
    c[:, C_U:C_U + 128] = (r > i)
    for q in range(4):
        c[:, C_NEG4 + q * 128:C_NEG4 + (q + 1) * 128] = np.where(i < r, NEG, 0.0)
    c[:, C_ONES:C_ONES + 128] = 1.0
    for hh, g in enumerate(RET_GAMMA):
        c[:, C_DEC + hh * 128:C_DEC + (hh + 1) * 128] = np.where(i >= r, g ** (-(r + 1.0)), 0.0) / 16.0
        c[:, C_RC + hh] = g ** (np.arange(128) + 1.0)
        c[:, C_RC2 + hh] = (g ** (np.arange(128) + 1.0)) ** 2
        c[:, C_RDK + hh] = g ** (127.0 - np.arange(128)) / 16.0
    return c


def make_rope(seq_len):
    dh = RET_DK
    inv = (np.float32(10000.0) ** (-np.arange(0, dh, 2, dtype=np.float32) / np.float32(dh))).astype(np.float32)
    ang = (np.arange(seq_len, dtype=np.float32)[None, :] * inv[:, None]).astype(np.float32)
    return np.cos(ang).astype(np.float32), np.sin(ang).astype(np.float32)


def build_program(n_seq, seq_len, layers=None, stages=None):
    ntok = n_seq * seq_len
    ntile_seq = seq_len // T
    layers = list(range(DEPTH)) if layers is None else layers
    stages = stages or ['ffn1', 'mix', 'ffn2']
    nc = bass.Bass("TRN2", target_bir_lowering=False)
    P = Prog()

    def din(name, shape, dt=F32):
        return nc.dram_tensor(name, list(shape), dt, kind="ExternalInput").ap()

    x_d = din("x", [ntok, D])
    y_d = nc.dram_tensor("y", [ntok, D], F32, kind="ExternalOutput").ap()
    ffn_w_in = din("ffn_w_in", [DEPTH, 2, D, 2 * DFF])
    ffn_w_out = din("ffn_w_out", [DEPTH, 2, DFF, D])
    ln_gb = din("ln_gb", [DEPTH * 3, 2, 128, D])
    cst_d = din("cst", [128, C_END])
    gla_w_in = din("gla_w_in", [2, D, 3088])
    gla_wg = din("gla_wg", [2, 17, 512])
    gla_ng = din("gla_ng", [2, 128, 256])
    gla_w_out = din("gla_w_out", [2, 1024, D])
    ret_w_in = din("ret_w_in", [1, D, 6144])
    ret_ng = din("ret_ng", [128, 512])
    ret_w_out = din("ret_w_out", [1, 2048, D])
    rope_c = din("rope_cos", [128, seq_len])
    rope_s = din("rope_sin", [128, seq_len])
    ssd_w_in = din("ssd_w_in", [1, D, 5152])
    ssd_cw = din("ssd_cw", [128, 24, 4])
    ssd_cb = din("ssd_cb", [128, 24])
    ssd_vec = din("ssd_vec", [128, 3, 32])
    ssd_ngd = din("ssd_ngd", [4, 128, 2, 512])
    ssd_w_out = din("ssd_w_out", [1, 2048, D])

    import contextlib
    es = contextlib.ExitStack()

    def sb(name, shape, dt):
        return es.enter_context(nc.sbuf_tensor(name, list(shape), dt))

    def ps(name, shape, dt):
        return es.enter_context(nc.psum_tensor(name, list(shape), dt))

    h = sb("h", [128, J, D], F32)
    xT = sb("xT", [128, 8, T], BF16)
    actT = sb("actT", [128, NFC, T], BF16)
    win_slots = [sb("win%d" % i, [128, 8, 768], BF16) for i in range(2)]
    wout_slots = [sb("wout%d" % i, [128, 8, 512], BF16) for i in range(2)]
    lng = sb("lng", [128, 2, D], F32)
    hb = [sb("hb%d" % i, [128, D], BF16) for i in range(2)]
    silu_s = [sb("silu%d" % i, [128, T], F32) for i in range(2)]
    stats = [sb("stats%d" % i, [128, 16], F32) for i in range(2)]
    ep_st = sb("ep_st", [128, J, 12], F32)
    ep_mv = sb("ep_mv", [128, J, 2], F32)
    ep_rs = sb("ep_rs", [128, J], F32)
    cst = sb("cst_sb", [128, C_END], F32)
    ident = sb("ident_b", [128, 128], BF16)
    gla_state = [sb("gla_st%d" % i, [128, GLA_H, GLA_DV], F32) for i in range(2)]
    gla_state_bf = [sb("gla_stb%d" % i, [128, GLA_H, GLA_DV], BF16) for i in range(2)]
    gla_wg_sb = [sb("gla_wg%d" % i, [17, 512], BF16) for i in range(2)]
    gla_ng_sb = [sb("gla_ng%d" % i, [128, 256], F32) for i in range(2)]
    ret_state = sb("ret_st", [128, 8, RET_DV], F32)
    ret_state_bf = sb("ret_stb", [128, 8, RET_DV], BF16)
    ret_ng_sb = sb("ret_ng_sb", [128, 512], F32)
    rope_sb = sb("rope_sb", [128, 2, T], F32)
    ssd_state = sb("ssd_st", [128, SSD_G, 512], F32)
    ssd_state_bf = sb("ssd_stb", [128, SSD_G, 512], BF16)
    ssd_carry = sb("ssd_carry", [128, 24, 4], F32)
    ssd_cw_sb = sb("ssd_cw_sb", [128, 24, 4], F32)
    ssd_cb_sb = sb("ssd_cb_sb", [128, 24], F32)
    ssd_vec_sb = sb("ssd_vec_sb", [128, 3, 32], F32)
    ssd_nega = sb("ssd_nega", [128, 32], F32)
    ssd_ngd_sb = sb("ssd_ngd_sb", [128, 2, 512], F32)
    AR_WORDS = 8704
    arena_t = sb("arena", [128, AR_WORDS], F32)

    class Arena:
        def __init__(self):
            self.off = 0

        def take(self, free, dt, parts=128):
            n = 1
            for f in free:
                n *= f
            words = n if dt == F32 else (n + 1) // 2
            a = arena_t[0:parts, self.off:self.off + words]
            self.off += words
            assert self.off <= AR_WORDS, self.off
            if dt != F32:
                a = a.bitcast(BF16)
            if len(free) == 2:
                a = a.rearrange("p (a b) -> p a b", a=free[0])
            elif len(free) == 3:
                a = a.rearrange("p (a b c) -> p a b c", a=free[0], b=free[1])
            return a

    pbank = [ps("pb%d" % i, [128, 512], F32) for i in range(6)]
    ptr_all = [ps("ptr_all%d" % i, [128, 8, 128], BF16) for i in range(2)]
    ptr = [ptr_all[0][:, 0:4, :], ptr_all[1][:, 0:4, :]]
    proj_ps = Rot([(pbank[i], "pb%d" % i) for i in range(4)])
    acc_ps = [(pbank[i], "pb%d" % i) for i in (4, 5, 2, 3)]
    tr_ps = Rot([(ptr[i], "ptr%d" % i) for i in range(2)])
    hb_r = Rot([(hb[i], "hb%d" % i) for i in range(2)])
    silu_r = Rot([(silu_s[i], "silu%d" % i) for i in range(2)])
    stats_r = Rot([(stats[i], "stats%d" % i) for i in range(2)])

    def cs(c0, n=128):
        return cst[:, c0:c0 + n]

    def colgrp(w2d, pieces, tag):
        dmas = []
        for (d0, s0, n) in pieces:
            src = w2d[:, s0:s0 + n].rearrange("(kc p) n -> p kc n", p=128)
            dmas.append((lambda b, d0=d0, n=n: b[:, :, d0:d0 + n], src))
        return dict(tag=tag, dmas=dmas, ncols=max(d0 + n for (d0, s0, n) in pieces))

    def ffn_in_groups(l, s):
        groups = []
        for g0 in range(0, NFC, 3):
            n = min(3, NFC - g0)
            groups.append(colgrp(ffn_w_in[l, s], [(0, g0 * 128, n * 128), (n * 128, DFF + g0 * 128, n * 128)], ("ffn_in", l, s, g0)))
        return groups

    def out_groups(w2d, nk, tag):
        groups = []
        for hf in range(2):
            src = w2d[:, hf * 512:(hf + 1) * 512].rearrange("(fc p) n -> p fc n", p=128)
            for k0 in range(0, nk, 8):
                n = min(8, nk - k0)
                groups.append(dict(tag=(tag, hf, k0), dmas=[(lambda b, n=n: b[:, 0:n, :], src[:, k0:k0 + n, :])], nk=n))
        return groups

    def mix_in_groups(l):
        kind, mi = MIX[l], MIXIDX[l]
        gs = []
        if kind == 'gla':
            w = gla_w_in[mi]
            gs.append(colgrp(w, [(0, 3072, 16)], ("gla_g", l)))
            for hh in range(GLA_H):
                gs.append(colgrp(w, [(0, hh * 128, 128), (128, 512 + hh * 128, 128), (256, 1024 + hh * 256, 256),
                                     (512, 2048 + hh * 256, 256)], ("gla_h", l, hh)))
        elif kind == 'ret':
            w = ret_w_in[0]
            for hh in range(RET_H):
                gs.append(colgrp(w, [(0, 4096 + hh * 512, 512)], ("ret_g", l, hh)))
                gs.append(colgrp(w, [(0, hh * 256, 256), (256, 1024 + hh * 256, 256)], ("ret_qk", l, hh)))
                gs.append(colgrp(w, [(0, 2048 + hh * 512, 512)], ("ret_v", l, hh)))
        else:
            w = ssd_w_in[0]
            gs.append(colgrp(w, [(0, 5120, 32)], ("ssd_dt", l)))
            for g in range(SSD_G):
                gs.append(colgrp(w, [(0, g * 512, 512)], ("ssd_z", l, g)))
                gs.append(colgrp(w, [(0, 2048 + g * 512, 512), (512, 4096 + g * 128, 128), (640, 4608 + g * 128, 128)], ("ssd_x", l, g)))
        return gs

    def mix_out_groups(l):
        kind, mi = MIX[l], MIXIDX[l]
        if kind == 'gla':
            return out_groups(gla_w_out[mi], 8, ("mix_out", l))
        if kind == 'ret':
            return out_groups(ret_w_out[0], 16, ("mix_out", l))
        return out_groups(ssd_w_out[0], 16, ("mix_out", l))

    STG = {'ffn1': 0, 'mix': 1, 'ffn2': 2}

    def stage_groups(l, st):
        if st == 'ffn1':
            return ffn_in_groups(l, 0), out_groups(ffn_w_out[l, 0], NFC, ("ffn_out", l, 0))
        if st == 'ffn2':
            return ffn_in_groups(l, 1), out_groups(ffn_w_out[l, 1], NFC, ("ffn_out", l, 1))
        return mix_in_groups(l), mix_out_groups(l)

    stage_list = [(l, st) for l in layers for st in stages]
    n_in = sum(len(stage_groups(l, st)[0]) for l, st in stage_list)
    n_out = sum(len(stage_groups(l, st)[1]) for l, st in stage_list)
    scr_in = nc.dram_tensor("scr_in", [n_in, 128, 8 * 768], BF16).ap()
    scr_out = nc.dram_tensor("scr_out", [n_out, 128, 8 * 512], BF16).ap()
    stage_plans = {}
    ii = oi = 0
    for (l, st) in stage_list:
        gi, go = stage_groups(l, st)
        chan = "prep%d" % (l * 3 + STG[st])
        last = None
        pin, pout = [], []
        for g in gi:
            img = scr_in[ii].rearrange("p (kc n) -> p kc n", kc=8)
            ncols = 0
            for (dst_fn, src) in g['dmas']:
                last = P.add('pool', (lambda e, d=dst_fn(img), s_=src: e.dma_start(out=d, in_=s_)), chan=chan)
            ncols = g['ncols']
            pin.append(dict(tag=g['tag'], img=img, ncols=ncols))
            ii += 1
        for g in go:
            img = scr_out[oi].rearrange("p (kc n) -> p kc n", kc=8)
            for (dst_fn, src) in g['dmas']:
                last = P.add('pool', (lambda e, d=dst_fn(img), s_=src: e.dma_start(out=d, in_=s_)), chan=chan)
            pout.append(dict(tag=g['tag'], img=img, nk=g['nk']))
            oi += 1
        for g in pin:
            g['after'] = [last]
            g['load'] = (lambda buf, g=g: (buf[:, :, 0:g['ncols']], g['img'][:, :, 0:g['ncols']]))
        for g in pout:
            g['after'] = [last]
            g['load'] = (lambda buf, g=g: (buf[:, 0:g['nk'], :], g['img'][:, 0:g['nk'], :]))
        stage_plans[(l, st)] = (pin, pout)

    in_plan, out_plan = [], []
    for sq in range(n_seq):
        for tt in range(ntile_seq):
            for (l, st) in stage_list:
                in_plan += stage_plans[(l, st)][0]
                out_plan += stage_plans[(l, st)][1]

    P.add('sp', lambda e: e.dma_start(out=cst[:], in_=cst_d[:, :]), writes=["cst"], chan="k_cst")
    P.add('dve', lambda e: e.tensor_copy(out=ident[:], in_=cst[:, C_ID:C_ID + 128]), reads=["cst"], writes=["ident"])
    for i in range(2):
        P.add('pool', lambda e, i=i: e.dma_start(out=gla_wg_sb[i][:], in_=gla_wg[i]), writes=["gla_wg%d" % i], chan="k_wg%d" % i)
        P.add('sp', lambda e, i=i: e.dma_start(out=gla_ng_sb[i][:], in_=gla_ng[i]), writes=["gla_ng%d" % i], chan="k_ng%d" % i)
    P.add('sp', lambda e: e.dma_start(out=ret_ng_sb[:], in_=ret_ng[:, :]), writes=["ret_ng"], chan="k_rng")
    P.add('sp', lambda e: e.dma_start(out=ssd_cw_sb[:], in_=ssd_cw[:, :, :]), writes=["ssd_cw"], chan="k_cw")
    P.add('sp', lambda e: e.dma_start(out=ssd_cb_sb[:], in_=ssd_cb[:, :]), writes=["ssd_cb"], chan="k_cb")
    P.add('sp', lambda e: e.dma_start(out=ssd_vec_sb[:], in_=ssd_vec[:, :, :]), writes=["ssd_vec"], chan="k_vec")
    P.add('act', lambda e: e.activation(out=ssd_nega[:], in_=ssd_vec_sb[:, 1, :], func=AF.Exp), reads=["ssd_vec"], writes=["ssd_nega"])
    P.add('dve', lambda e: e.tensor_scalar_mul(out=ssd_nega[:], in0=ssd_nega[:], scalar1=-1.0), reads=["ssd_nega"], writes=["ssd_nega"])

    win = WStream(P, "win", win_slots, in_plan)
    wout = WStream(P, "wout", wout_slots, out_plan)

    def transposes(src_bf, src_res, nblk, dst_fn, dst_res):
        for b0 in range(0, nblk, 4):
            n = min(4, nblk - b0)
            pt, pres = tr_ps.next()
            for q in range(n):
                P.add('pe', lambda e, pt=pt, q=q, b=b0 + q: e.transpose(pt[:, q, :], src_bf[:, b * 128:(b + 1) * 128], ident[:]),
                      reads=[src_res, "ident"], writes=[pres])
            P.add('act', lambda e, pt=pt, b0=b0, n=n: e.copy(out=dst_fn(b0, n), in_=pt[:, 0:n, :]),
                  reads=[pres], writes=[dst_res])

    def transpose_to_xT(src_bf, src_res, j):
        transposes(src_bf, src_res, 8, lambda b0, n: xT[:, b0:b0 + n, j * 128:(j + 1) * 128], "xT%d" % j)

    def load_ln(idx):
        P.add('sp', lambda e: e.dma_start(out=lng[:], in_=ln_gb[idx].rearrange("a p d -> p a d")),
              writes=["lng"], chan="lng")

    def rstd_from_var(var_ap, out_ap, res, eps):
        P.add('dve', lambda e: e.tensor_scalar_add(out=out_ap, in0=var_ap, scalar1=eps), reads=[res], writes=[res])
        P.add('act', lambda e: e.sqrt(out=out_ap, in_=out_ap), reads=[res], writes=[res])
        P.add('dve', lambda e: e.reciprocal(out=out_ap, in_=out_ap), reads=[res], writes=[res])

    def epi_stats(j):
        hres = "h%d" % j
        for hf in range(2):
            P.add('dve', lambda e, hf=hf: e.bn_stats(out=ep_st[:, j, hf * 6:(hf + 1) * 6], in_=h[:, j, hf * 512:(hf + 1) * 512]),
                  reads=[hres], writes=["ep_st%d" % j])
        P.add('dve', lambda e: e.bn_aggr(out=ep_mv[:, j, :], in_=ep_st[:, j, :]), reads=["ep_st%d" % j], writes=["ep_mv"])

    def epi_finish():
        P.add('dve', lambda e: e.tensor_scalar_add(out=ep_rs[:], in0=ep_mv[:, :, 1], scalar1=float(LN_EPS_EFF)), reads=["ep_mv"], writes=["ep_rs"])
        P.add('act', lambda e: e.sqrt(out=ep_rs[:], in_=ep_rs[:]), reads=["ep_rs"], writes=["ep_rs"])
        P.add('dve', lambda e: e.reciprocal(out=ep_rs[:], in_=ep_rs[:]), reads=["ep_rs"], writes=["ep_rs"])
        for jp in range(0, J, 2):
            js_ = (jp, jp + 1)
            for j in js_:
                P.add('dve', lambda e, j=j: e.tensor_scalar(out=h[:, j, :], in0=h[:, j, :], scalar1=ep_mv[:, j, 0:1], scalar2=ep_rs[:, j:j + 1],
                                                           op0=ALU.subtract, op1=ALU.mult), reads=["ep_mv", "ep_rs", "h%d" % j], writes=["h%d" % j])
            for j in js_:
                P.add('dve', lambda e, j=j: e.tensor_tensor(out=h[:, j, :], in0=h[:, j, :], in1=lng[:, 0, :], op=ALU.mult),
                      reads=["h%d" % j, "lng"], writes=["h%d" % j])
            for j in js_:
                P.add('dve', lambda e, j=j: e.tensor_tensor(out=h[:, j, :], in0=h[:, j, :], in1=lng[:, 1, :], op=ALU.add),
                      reads=["h%d" % j, "lng"], writes=["h%d" % j])
            for j in js_:
                hbt, hbres = hb_r.next()
                P.add('act', lambda e, hbt=hbt, j=j: e.copy(out=hbt[:], in_=h[:, j, :]), reads=["h%d" % j], writes=[hbres])
                transpose_to_xT(hbt, hbres, j)

    XT_ALL = ["xT%d" % j for j in range(J)]

    def out_proj(nk, tag, lnidx, coef):
        load_ln(lnidx)
        for hf in range(2):
            k0s = list(range(0, nk, 8))
            for k0 in k0s:
                n = min(8, nk - k0)
                wb, wres = wout.get((tag, hf, k0))
                for j in range(J):
                    mp, mres = acc_ps[j]
                    for kk in range(n):
                        kc = k0 + kk
                        P.add('pe', lambda e, mp=mp, kc=kc, kk=kk, j=j, wb=wb: e.matmul(
                            mp[:], lhsT=actT[:, kc, j * 128:(j + 1) * 128], rhs=wb[:, kk, :], start=(kc == 0), stop=(kc == nk - 1)),
                            reads=["actT", wres], writes=[mres])
                wout.done()
            for j in range(J):
                mp, mres = acc_ps[j]
                P.add('dve', lambda e, mp=mp, j=j, hf=hf: e.scalar_tensor_tensor(
                    out=h[:, j, hf * 512:(hf + 1) * 512], in0=mp[:], scalar=float(coef / ALPHA),
                    in1=h[:, j, hf * 512:(hf + 1) * 512], op0=ALU.mult, op1=ALU.add),
                    reads=[mres, "h%d" % j], writes=["h%d" % j])
                if hf == 1:
                    epi_stats(j)
        epi_finish()

    def proj_fm(pp, pres, wb, wres, c0, m=128):
        for kc in range(8):
            P.add('pe', lambda e, kc=kc: e.matmul(pp[0:m, :], lhsT=wb[:, kc, c0:c0 + m], rhs=xT[:, kc, :], start=(kc == 0), stop=(kc == 7)),
                  reads=[wres] + XT_ALL, writes=[pres])

    def proj_tm(pp, pres, wb, wres, j, c0, n):
        for kc in range(8):
            P.add('pe', lambda e, kc=kc: e.matmul(pp[:, 0:n], lhsT=xT[:, kc, j * 128:(j + 1) * 128], rhs=wb[:, kc, c0:c0 + n],
                                                  start=(kc == 0), stop=(kc == 7)),
                  reads=[wres, "xT%d" % j], writes=[pres])

    def ffn(l, s):
        for g0 in range(0, NFC, 3):
            n = min(3, NFC - g0)
            wb, wres = win.get(("ffn_in", l, s, g0))
            for c in range(n):
                pg, gres = proj_ps.next()
                pu, ures = proj_ps.next()
                proj_fm(pg, gres, wb, wres, c * 128)
                proj_fm(pu, ures, wb, wres, n * 128 + c * 128)
                sl, slres = silu_r.next()
                P.add('act', lambda e, sl=sl, pg=pg: e.activation(out=sl[:], in_=pg[:], func=AF.Silu), reads=[gres], writes=[slres])
                fc = g0 + c
                P.add('dve', lambda e, sl=sl, pu=pu, fc=fc: e.tensor_tensor(out=actT[:, fc, :], in0=sl[:], in1=pu[:], op=ALU.mult),
                      reads=[slres, ures], writes=["actT"])
            win.done()
        out_proj(NFC, ("ffn_out", l, s), l * 3 + (0 if s == 0 else 2), 0.5)

    def A(eng, fn, reads, writes):
        P.add(eng, fn, reads=reads, writes=writes)

    def gla(l, first_tile):
        mi = MIXIDX[l]
        P.fence()
        ar = Arena()
        Lb = ar.take([J, 512], F32)
        E3 = ar.take([J, 512], F32)
        E1 = ar.take([T], F32)
        E2 = ar.take([T], F32)
        gl_aug = ar.take([T], BF16, parts=17)
        qd = ar.take([T], BF16)
        ki = ar.take([T], BF16)
        kend = [ar.take([128], BF16) for _ in range(2)]
        vb = [ar.take([256], BF16) for _ in range(2)]
        rs = [ar.take([256], F32) for _ in range(2)]
        sTm = [ar.take([128], BF16) for _ in range(2)]
        ytmp = [ar.take([256], F32) for _ in range(2)]
        yb = [ar.take([256], BF16) for _ in range(2)]
        st = [ar.take([16], F32) for _ in range(2)]
        state, state_bf = gla_state[mi], gla_state_bf[mi]
        sres = "gla_st%d" % mi
        if first_tile:
            A('dve', lambda e: e.memset(state[:], 0.0), [], [sres])
            A('dve', lambda e: e.memset(state_bf[:], 0.0), [], [sres + "b"])
        A('dve', lambda e: e.memset(gl_aug[:], 1.0), [], ["g.gl"])
        wb, wres = win.get(("gla_g", l))
        pp, pres = proj_ps.next()
        proj_fm(pp, pres, wb, wres, 0, m=16)
        A('act', lambda e, pp=pp: e.copy(out=gl_aug[0:16, :], in_=pp[0:16, :]), [pres], ["g.gl"])
        win.done()
        for j in range(J):
            pp, pres = proj_ps.next()
            A('pe', lambda e, pp=pp, j=j: e.matmul(pp[:], lhsT=gl_aug[0:17, j * 128:(j + 1) * 128], rhs=gla_wg_sb[mi][0:17, :], start=True, stop=True),
              ["g.gl", "gla_wg%d" % mi], [pres])
            A('act', lambda e, pp=pp, j=j: e.activation(out=Lb[:, j, :], in_=pp[:], func=AF.Exp, scale=-1.0), [pres], ["g.L%d" % j])
            A('act', lambda e, j=j: e.activation(out=Lb[:, j, :], in_=Lb[:, j, :], func=AF.Ln, bias=cst[:, C_ONES:C_ONES + 1]), ["g.L%d" % j, "cst"], ["g.L%d" % j])
        for j in range(J):
            pp, pres = proj_ps.next()
            A('pe', lambda e, pp=pp, j=j: e.matmul(pp[:], lhsT=cs(C_U), rhs=Lb[:, j, :], start=True, stop=True), ["cst", "g.L%d" % j], [pres])
            A('act', lambda e, pp=pp, j=j: e.activation(out=E3[:, j, :], in_=pp[:], func=AF.Exp, scale=-1.0 / 16.0), [pres], ["g.E3%d" % j])
        LALL = ["g.L%d" % j for j in range(J)]
        for hh in range(GLA_H):
            wb, wres = win.get(("gla_h", l, hh))
            pc, pcres = proj_ps.next()
            for j in range(J):
                A('pe', lambda e, pc=pc, j=j, hh=hh: e.matmul(pc[:, j * 128:(j + 1) * 128], lhsT=Lb[:, j, hh * 128:(hh + 1) * 128], rhs=cs(C_TRI),
                                                         start=True, stop=True), ["cst"] + LALL, [pcres])
            A('act', lambda e, pc=pc: e.activation(out=E1[:], in_=pc[:], func=AF.Exp, scale=-1.0 / 16.0), [pcres], ["g.E1"])
            A('act', lambda e, pc=pc: e.activation(out=E2[:], in_=pc[:], func=AF.Exp, scale=1.0 / 16.0), [pcres], ["g.E2"])
            pq, pqres = proj_ps.next()
            proj_fm(pq, pqres, wb, wres, 0)
            A('dve', lambda e, pq=pq: e.scalar_tensor_tensor(out=qd[:], in0=pq[:], scalar=float(GLA_DK ** -0.5), in1=E1[:], op0=ALU.mult, op1=ALU.mult),
              [pqres, "g.E1"], ["g.qd"])
            pk, pkres = proj_ps.next()
            proj_fm(pk, pkres, wb, wres, 128)
            A('dve', lambda e, pk=pk: e.tensor_tensor(out=ki[:], in0=pk[:], in1=E2[:], op=ALU.mult), [pkres, "g.E2"], ["g.ki"])
            for j in range(J):
                q = j % 2
                js = slice(j * 128, (j + 1) * 128)
                pkv, pkvres = proj_ps.next()
                proj_tm(pkv, pkvres, wb, wres, j, 128, 384)
                A('dve', lambda e, pkv=pkv, q=q, j=j, hh=hh: e.tensor_tensor(out=kend[q][:], in0=pkv[:, 0:128], in1=E3[:, j, hh * 128:(hh + 1) * 128], op=ALU.mult),
                  [pkvres, "g.E3%d" % j], ["g.kend%d" % q])
                A('act', lambda e, pkv=pkv, q=q: e.copy(out=vb[q][:], in_=pkv[:, 128:384]), [pkvres], ["g.vb%d" % q])
                pr, prres = proj_ps.next()
                proj_tm(pr, prres, wb, wres, j, 512, 256)
                A('act', lambda e, pr=pr, q=q: e.activation(out=rs[q][:], in_=pr[:, 0:256], func=AF.Silu), [prres], ["g.rs%d" % q])
                psT, psTres = proj_ps.next()
                A('pe', lambda e, psT=psT, js=js: e.matmul(psT[:, 0:128], lhsT=ki[:, js], rhs=qd[:, js], start=True, stop=True), ["g.ki", "g.qd"], [psTres])
                A('dve', lambda e, psT=psT, q=q: e.tensor_tensor(out=sTm[q][:], in0=psT[:, 0:128], in1=cs(C_TRI), op=ALU.mult), [psTres, "cst"], ["g.sTm%d" % q])
                po, pores = proj_ps.next()
                A('pe', lambda e, po=po, q=q: e.matmul(po[:, 0:256], lhsT=sTm[q][:], rhs=vb[q][:], start=True, stop=False), ["g.sTm%d" % q, "g.vb%d" % q], [pores])
                A('pe', lambda e, po=po, js=js, hh=hh: e.matmul(po[:, 0:256], lhsT=qd[:, js], rhs=state_bf[:, hh, :], start=False, stop=True), ["g.qd", sres + "b"], [pores])
                psu, psures = proj_ps.next()
                A('pe', lambda e, psu=psu, q=q: e.matmul(psu[:, 0:256], lhsT=kend[q][:], rhs=vb[q][:], start=True, stop=True), ["g.kend%d" % q, "g.vb%d" % q], [psures])
                A('dve', lambda e, psu=psu, j=j, hh=hh: e.scalar_tensor_tensor(out=state[:, hh, :], in0=state[:, hh, :], scalar=E1[:, j * 128 + 127:j * 128 + 128],
                                                                           in1=psu[:, 0:256], op0=ALU.mult, op1=ALU.add), [psures, "g.E1", sres], [sres])
                A('act', lambda e, hh=hh: e.copy(out=state_bf[:, hh, :], in_=state[:, hh, :]), [sres], [sres + "b"])
                s_, s_res = st[q], "g.st%d" % q
                A('dve', lambda e, po=po, s_=s_: e.bn_stats(out=s_[:, 0:6], in_=po[:, 0:256]), [pores], [s_res])
                A('dve', lambda e, s_=s_: e.bn_aggr(out=s_[:, 6:8], in_=s_[:, 0:6]), [s_res], [s_res])
                A('dve', lambda e, s_=s_: e.scalar_tensor_tensor(out=s_[:, 8:9], in0=s_[:, 6:7], scalar=s_[:, 6:7], in1=s_[:, 7:8], op0=ALU.mult, op1=ALU.add), [s_res], [s_res])
                rstd_from_var(s_[:, 8:9], s_[:, 9:10], s_res, 1e-6)
                A('dve', lambda e, po=po, s_=s_, q=q: e.scalar_tensor_tensor(out=ytmp[q][:], in0=po[:, 0:256], scalar=s_[:, 9:10], in1=gla_ng_sb[mi][:],
                                                                      op0=ALU.mult, op1=ALU.mult), [pores, s_res, "gla_ng%d" % mi], ["g.ytmp%d" % q])
                A('dve', lambda e, q=q: e.tensor_tensor(out=yb[q][:], in0=ytmp[q][:], in1=rs[q][:], op=ALU.mult), ["g.ytmp%d" % q, "g.rs%d" % q], ["g.yb%d" % q])
                transposes(yb[q], "g.yb%d" % q, 2, lambda b0, n, j=j, hh=hh: actT[:, 2 * hh + b0:2 * hh + b0 + n, j * 128:(j + 1) * 128], "actT")
            win.done()
        P.fence()
        out_proj(8, ("mix_out", l), l * 3 + 1, 1.0)

    def ret(l, first_tile, pos0):
        P.fence()
        ar = Arena()
        sg = ar.take([J, 512], BF16)
        qr = ar.take([2, T], BF16)
        kr = ar.take([2, T], BF16)
        kd = ar.take([J, 256], BF16)
        vb = ar.take([J, 512], BF16)
        t1 = ar.take([T], F32)
        t2 = ar.take([T], F32)
        sTm = [ar.take([128], BF16) for _ in range(2)]
        ytmp = [ar.take([512], F32) for _ in range(2)]
        yb = [ar.take([512], BF16) for _ in range(2)]
        st = [ar.take([16], F32) for _ in range(2)]
        state, state_bf, sres = ret_state, ret_state_bf, "ret_st"
        if first_tile:
            A('dve', lambda e: e.memset(state[:], 0.0), [], [sres])
            A('dve', lambda e: e.memset(state_bf[:], 0.0), [], [sres + "b"])
        P.add('sp', lambda e: e.dma_start(out=rope_sb[:, 0, :], in_=rope_c[:, pos0:pos0 + T]), writes=["rope"], chan="rope")
        P.add('sp', lambda e: e.dma_start(out=rope_sb[:, 1, :], in_=rope_s[:, pos0:pos0 + T]), writes=["rope"], chan="rope")
        cosb, sinb = rope_sb[:, 0, :], rope_sb[:, 1, :]

        def rotary(pa, pares, pb_, pbres, dst, dres):
            A('dve', lambda e: e.tensor_tensor(out=t1[:], in0=pa[:], in1=cosb, op=ALU.mult), [pares, "rope"], ["r.t1"])
            A('dve', lambda e: e.tensor_tensor(out=t2[:], in0=pb_[:], in1=sinb, op=ALU.mult), [pbres, "rope"], ["r.t2"])
            A('dve', lambda e: e.tensor_tensor(out=dst[:, 0, :], in0=t1[:], in1=t2[:], op=ALU.subtract), ["r.t1", "r.t2"], [dres])
            A('dve', lambda e: e.tensor_tensor(out=t1[:], in0=pa[:], in1=sinb, op=ALU.mult), [pares, "rope", dres], ["r.t1"])
            A('dve', lambda e: e.tensor_tensor(out=t2[:], in0=pb_[:], in1=cosb, op=ALU.mult), [pbres, "rope", dres], ["r.t2"])
            A('dve', lambda e: e.tensor_tensor(out=dst[:, 1, :], in0=t1[:], in1=t2[:], op=ALU.add), ["r.t1", "r.t2"], [dres])

        for hh in range(RET_H):
            wb, wres = win.get(("ret_g", l, hh))
            for j in range(J):
                pp, pres = proj_ps.next()
                proj_tm(pp, pres, wb, wres, j, 0, 512)
                A('act', lambda e, pp=pp, j=j: e.activation(out=sg[:, j, :], in_=pp[:], func=AF.Silu), [pres], ["r.sg"])
            win.done()
            wb, wres = win.get(("ret_qk", l, hh))
            pa, pares = proj_ps.next(); proj_fm(pa, pares, wb, wres, 0)
            pb_, pbres = proj_ps.next(); proj_fm(pb_, pbres, wb, wres, 128)
            rotary(pa, pares, pb_, pbres, qr, "r.qr")
            pa, pares = proj_ps.next(); proj_fm(pa, pares, wb, wres, 256)
            pb_, pbres = proj_ps.next(); proj_fm(pb_, pbres, wb, wres, 384)
            rotary(pa, pares, pb_, pbres, kr, "r.kr")
            win.done()
            wb, wres = win.get(("ret_v", l, hh))
            for j in range(J):
                pt, ptres = tr_ps.next()
                for dc in range(2):
                    A('pe', lambda e, pt=pt, dc=dc, j=j: e.transpose(pt[:, dc, :], kr[:, dc, j * 128:(j + 1) * 128], ident[:]), ["r.kr", "ident"], [ptres])
                A('dve', lambda e, pt=pt, j=j, hh=hh: e.tensor_scalar_mul(out=kd[:, j, :].rearrange("p (a b) -> p a b", a=2), in0=pt[:, 0:2, :],
                                                                    scalar1=cst[:, C_RDK + hh:C_RDK + hh + 1]), [ptres, "cst"], ["r.kd"])
                pp, pres = proj_ps.next()
                proj_tm(pp, pres, wb, wres, j, 0, 512)
                A('act', lambda e, pp=pp, j=j: e.copy(out=vb[:, j, :], in_=pp[:]), [pres], ["r.vb"])
            for j in range(J):
                q = j % 2
                js = slice(j * 128, (j + 1) * 128)
                psT, psTres = proj_ps.next()
                for dc in range(2):
                    A('pe', lambda e, psT=psT, dc=dc, js=js: e.matmul(psT[:, 0:128], lhsT=kr[:, dc, js], rhs=qr[:, dc, js], start=(dc == 0), stop=(dc == 1)),
                      ["r.kr", "r.qr"], [psTres])
                A('dve', lambda e, psT=psT, q=q, hh=hh: e.tensor_tensor(out=sTm[q][:], in0=psT[:, 0:128], in1=cs(C_DEC + hh * 128), op=ALU.mult),
                  [psTres, "cst"], ["r.sTm%d" % q])
                po, pores = proj_ps.next()
                A('pe', lambda e, po=po, q=q, j=j: e.matmul(po[:], lhsT=sTm[q][:], rhs=vb[:, j, :], start=True, stop=False), ["r.sTm%d" % q, "r.vb"], [pores])
                for dc in range(2):
                    A('pe', lambda e, po=po, dc=dc, js=js, hh=hh: e.matmul(po[:], lhsT=qr[:, dc, js], rhs=state_bf[:, hh * 2 + dc, :], start=False, stop=(dc == 1)),
                      ["r.qr", sres + "b"], [pores])
                for dc in range(2):
                    psu, psures = proj_ps.next()
                    A('pe', lambda e, psu=psu, dc=dc, j=j: e.matmul(psu[:], lhsT=kd[:, j, dc * 128:(dc + 1) * 128], rhs=vb[:, j, :], start=True, stop=True),
                      ["r.kd", "r.vb"], [psures])
                    A('dve', lambda e, psu=psu, dc=dc, hh=hh: e.scalar_tensor_tensor(out=state[:, hh * 2 + dc, :], in0=state[:, hh * 2 + dc, :],
                                                                               scalar=float(RET_GAMMA[hh] ** 128), in1=psu[:], op0=ALU.mult, op1=ALU.add),
                      [psures, sres], [sres])
                    A('act', lambda e, dc=dc, hh=hh: e.copy(out=state_bf[:, hh * 2 + dc, :], in_=state[:, hh * 2 + dc, :]), [sres], [sres + "b"])
                s_, s_res = st[q], "r.st%d" % q
                A('dve', lambda e, po=po, s_=s_: e.bn_stats(out=s_[:, 0:6], in_=po[:]), [pores], [s_res])
                A('dve', lambda e, s_=s_: e.bn_aggr(out=s_[:, 6:8], in_=s_[:, 0:6]), [s_res], [s_res])
                A('dve', lambda e, s_=s_, hh=hh: e.tensor_tensor(out=s_[:, 8:9], in0=s_[:, 7:8], in1=cst[:, C_RC2 + hh:C_RC2 + hh + 1], op=ALU.mult), [s_res, "cst"], [s_res])
                rstd_from_var(s_[:, 8:9], s_[:, 9:10], s_res, 1e-6)
                A('dve', lambda e, s_=s_, hh=hh: e.tensor_tensor(out=s_[:, 10:11], in0=s_[:, 9:10], in1=cst[:, C_RC + hh:C_RC + hh + 1], op=ALU.mult), [s_res, "cst"], [s_res])
                A('dve', lambda e, po=po, s_=s_, q=q: e.tensor_scalar(out=ytmp[q][:], in0=po[:], scalar1=s_[:, 6:7], scalar2=s_[:, 10:11],
                                                               op0=ALU.subtract, op1=ALU.mult), [pores, s_res], ["r.ytmp%d" % q])
                A('dve', lambda e, q=q: e.tensor_tensor(out=ytmp[q][:], in0=ytmp[q][:], in1=ret_ng_sb[:], op=ALU.mult), ["r.ytmp%d" % q, "ret_ng"], ["r.ytmp%d" % q])
                A('dve', lambda e, q=q, j=j: e.tensor_tensor(out=yb[q][:], in0=ytmp[q][:], in1=sg[:, j, :], op=ALU.mult), ["r.ytmp%d" % q, "r.sg"], ["r.yb%d" % q])
                transposes(yb[q], "r.yb%d" % q, 4, lambda b0, n, j=j, hh=hh: actT[:, 4 * hh + b0:4 * hh + b0 + n, j * 128:(j + 1) * 128], "actT")
            win.done()
        P.fence()
        out_proj(16, ("mix_out", l), l * 3 + 1, 1.0)

    def ssd(l, first_tile):
        P.fence()
        ar = Arena()
        dt = ar.take([J, 32], F32)
        dtA = ar.take([J, 32], F32)
        negcum = ar.take([J, 32], F32)
        ecum = ar.take([J, 32], F32)
        declast = ar.take([J, 32], F32)
        toend = ar.take([J, 32], F32)
        sz = ar.take([J, 512], BF16)
        stage = [ar.take([T + 4], F32) for _ in range(1)]
        acc = [ar.take([T], F32) for _ in range(1)]
        xfm = [ar.take([T], BF16) for _ in range(2)]
        x_tok = ar.take([J, 512], BF16)
        BT = ar.take([T], BF16)
        CT = ar.take([T], BF16)
        B_tok = ar.take([J, 128], BF16)
        cbm = [ar.take([128], F32) for _ in range(2)]
        wseg = [ar.take([128], F32) for _ in range(4)]
        wT = [ar.take([128], BF16) for _ in range(4)]
        dbc = [ar.take([128], F32) for _ in range(4)]
        ta = [ar.take([512], F32) for _ in range(1)]
        tb = [ar.take([512], F32) for _ in range(1)]
        xs = [ar.take([512], BF16) for _ in range(1)]
        yb = [ar.take([512], BF16) for _ in range(1)]
        st = [ar.take([16], F32) for _ in range(2)]
        state, state_bf, sres = ssd_state, ssd_state_bf, "ssd_st"
        if first_tile:
            A('dve', lambda e: e.memset(state[:], 0.0), [], [sres])
            A('dve', lambda e: e.memset(state_bf[:], 0.0), [], [sres + "b"])
            A('dve', lambda e: e.memset(ssd_carry[:], 0.0), [], ["ssd_carry"])
        wb, wres = win.get(("ssd_dt", l))
        for j in range(J):
            pp, pres = proj_ps.next()
            proj_tm(pp, pres, wb, wres, j, 0, 32)
            A('dve', lambda e, pp=pp, j=j: e.tensor_tensor(out=dt[:, j, :], in0=pp[:, 0:32], in1=ssd_vec_sb[:, 0, :], op=ALU.add), [pres, "ssd_vec"], ["s.dt"])
        win.done()
        A('act', lambda e: e.activation(out=dt[:], in_=dt[:], func=AF.Exp), ["s.dt"], ["s.dt"])
        A('act', lambda e: e.activation(out=dt[:], in_=dt[:], func=AF.Ln, bias=cst[:, C_ONES:C_ONES + 1]), ["s.dt", "cst"], ["s.dt"])
        for j in range(J):
            A('dve', lambda e, j=j: e.tensor_tensor(out=dtA[:, j, :], in0=dt[:, j, :], in1=ssd_nega[:], op=ALU.mult), ["s.dt", "ssd_nega"], ["s.dtA"])
        for j in range(J):
            pp, pres = proj_ps.next()
            A('pe', lambda e, pp=pp, j=j: e.matmul(pp[:, 0:32], lhsT=cs(C_TRI), rhs=dtA[:, j, :], start=True, stop=True), ["cst", "s.dtA"], [pres])
            A('pe', lambda e, pp=pp, j=j: e.matmul(pp[:, 32:64], lhsT=cs(C_ONES), rhs=dtA[:, j, :], start=True, stop=True), ["cst", "s.dtA"], [pres])
            A('pe', lambda e, pp=pp, j=j: e.matmul(pp[:, 64:96], lhsT=cs(C_U), rhs=dtA[:, j, :], start=True, stop=True), ["cst", "s.dtA"], [pres])
            A('act', lambda e, pp=pp, j=j: e.activation(out=ecum[:, j, :], in_=pp[:, 0:32], func=AF.Exp), [pres], ["s.ecum"])
            A('dve', lambda e, pp=pp, j=j: e.tensor_scalar_mul(out=negcum[:, j, :], in0=pp[:, 0:32], scalar1=-1.0), [pres], ["s.negcum"])
            A('act', lambda e, pp=pp, j=j: e.activation(out=declast[:, j, :], in_=pp[:, 32:64], func=AF.Exp), [pres], ["s.declast"])
            A('act', lambda e, pp=pp, j=j: e.activation(out=toend[:, j, :], in_=pp[:, 64:96], func=AF.Exp), [pres], ["s.toend"])
            A('dve', lambda e, j=j: e.tensor_tensor(out=toend[:, j, :], in0=toend[:, j, :], in1=dt[:, j, :], op=ALU.mult), ["s.toend", "s.dt"], ["s.toend"])
        for g in range(SSD_G):
            P.add('sp', lambda e, g=g: e.dma_start(out=ssd_ngd_sb[:], in_=ssd_ngd[g]), writes=["ssd_ngd"], chan="ssd_ngd")
            wb, wres = win.get(("ssd_z", l, g))
            for j in range(J):
                pp, pres = proj_ps.next()
                proj_tm(pp, pres, wb, wres, j, 0, 512)
                A('act', lambda e, pp=pp, j=j: e.activation(out=sz[:, j, :], in_=pp[:], func=AF.Silu), [pres], ["s.sz"])
            win.done()
            wb, wres = win.get(("ssd_x", l, g))
            for ci_loc in range(6):
                ci = (g * 4 + ci_loc) if ci_loc < 4 else (16 + g if ci_loc == 4 else 20 + g)
                q = ci_loc % 2
                sg_, sgres = stage[0], "s.stage0"
                ac, acres = acc[0], "s.acc0"
                pp, pres = proj_ps.next()
                proj_fm(pp, pres, wb, wres, ci_loc * 128)
                A('dve', lambda e, sg_=sg_, ci=ci: e.tensor_copy(out=sg_[:, 0:3], in_=ssd_carry[:, ci, 0:3]), ["ssd_carry"], [sgres])
                A('act', lambda e, sg_=sg_, pp=pp: e.copy(out=sg_[:, 3:3 + T], in_=pp[:]), [pres], [sgres])
                A('dve', lambda e, sg_=sg_, ci=ci: e.tensor_copy(out=ssd_carry[:, ci, 0:3], in_=sg_[:, T:T + 3]), [sgres], ["ssd_carry"])
                A('dve', lambda e, sg_=sg_, ac=ac, ci=ci: e.tensor_scalar(out=ac[:], in0=sg_[:, 3:3 + T], scalar1=ssd_cw_sb[:, ci, 3:4], scalar2=ssd_cb_sb[:, ci:ci + 1],
                                                                  op0=ALU.mult, op1=ALU.add), [sgres, "ssd_cw", "ssd_cb"], [acres])
                for k in range(3):
                    A('dve', lambda e, sg_=sg_, ac=ac, ci=ci, k=k: e.scalar_tensor_tensor(out=ac[:], in0=sg_[:, k:k + T], scalar=ssd_cw_sb[:, ci, k:k + 1], in1=ac[:],
                                                                                   op0=ALU.mult, op1=ALU.add), [sgres, "ssd_cw", acres], [acres])
                if ci_loc < 4:
                    xf, xfres = xfm[q], "s.xfm%d" % q
                    A('act', lambda e, xf=xf, ac=ac: e.activation(out=xf[:], in_=ac[:], func=AF.Silu), [acres], [xfres])
                    for j in range(J):
                        pt, ptres = tr_ps.next()
                        A('pe', lambda e, pt=pt, xf=xf, j=j: e.transpose(pt[:, 0, :], xf[:, j * 128:(j + 1) * 128], ident[:]), [xfres, "ident"], [ptres])
                        A('act', lambda e, pt=pt, j=j, c=ci_loc: e.copy(out=x_tok[:, j, c * 128:(c + 1) * 128], in_=pt[:, 0, :]), [ptres], ["s.xtok"])
                elif ci_loc == 4:
                    A('act', lambda e, ac=ac: e.activation(out=BT[:], in_=ac[:], func=AF.Silu), [acres], ["s.BT"])
                    for j in range(J):
                        pt, ptres = tr_ps.next()
                        A('pe', lambda e, pt=pt, j=j: e.transpose(pt[:, 0, :], BT[:, j * 128:(j + 1) * 128], ident[:]), ["s.BT", "ident"], [ptres])
                        A('act', lambda e, pt=pt, j=j: e.copy(out=B_tok[:, j, :], in_=pt[:, 0, :]), [ptres], ["s.Btok"])
                else:
                    A('act', lambda e, ac=ac: e.activation(out=CT[:], in_=ac[:], func=AF.Silu), [acres], ["s.CT"])
            win.done()
            for j in range(J):
                q = j % 2
                js = slice(j * 128, (j + 1) * 128)
                pcb, pcbres = proj_ps.next()
                A('pe', lambda e, pcb=pcb, js=js: e.matmul(pcb[:, 0:128], lhsT=BT[:, js], rhs=CT[:, js], start=True, stop=True), ["s.BT", "s.CT"], [pcbres])
                A('dve', lambda e, pcb=pcb, q=q: e.tensor_tensor(out=cbm[q][:], in0=pcb[:, 0:128], in1=cs(C_TRI), op=ALU.mult), [pcbres, "cst"], ["s.cbm%d" % q])
                py, pyres = proj_ps.next()
                for half in range(2):
                    pseg, psegres = proj_ps.next()
                    for e4 in range(4):
                        eh = g * 8 + half * 4 + e4
                        w_, wres_ = dbc[e4 % 4], "s.dbc%d" % (e4 % 4)
                        A('dve', lambda e, w_=w_, j=j, eh=eh: e.tensor_scalar_mul(out=w_[:], in0=cs(C_ONES), scalar1=dtA[:, j, eh:eh + 1]), ["cst", "s.dtA"], [wres_])
                        A('pe', lambda e, pseg=pseg, e4=e4, w_=w_: e.matmul(pseg[:, e4 * 128:(e4 + 1) * 128], lhsT=w_[:], rhs=cs(C_TRI), start=True, stop=False),
                          [wres_, "cst"], [psegres])
                        A('pe', lambda e, pseg=pseg, e4=e4: e.matmul(pseg[:, e4 * 128:(e4 + 1) * 128], lhsT=cs(C_ID), rhs=cs(C_NEG4), start=False, stop=True),
                          ["cst"], [psegres])
                    for e4 in range(4):
                        eh = g * 8 + half * 4 + e4
                        el = half * 4 + e4
                        ws_, wsres = wseg[e4 % 4], "s.wseg%d" % (e4 % 4)
                        wt_, wtres = wT[e4 % 4], "s.wT%d" % (e4 % 4)
                        A('act', lambda e, pseg=pseg, e4=e4, ws_=ws_, j=j, eh=eh: e.activation(out=ws_[:], in_=pseg[:, e4 * 128:(e4 + 1) * 128], func=AF.Exp,
                                                                                       bias=negcum[:, j, eh:eh + 1], scale=1.0), [psegres, "s.negcum"], [wsres])
                        A('dve', lambda e, ws_=ws_, wt_=wt_, j=j, eh=eh, q=q: e.scalar_tensor_tensor(out=wt_[:], in0=ws_[:], scalar=dt[:, j, eh:eh + 1], in1=cbm[q][:],
                                                                                             op0=ALU.mult, op1=ALU.mult), [wsres, "s.dt", "s.cbm%d" % q], [wtres])
                        A('pe', lambda e, py=py, wt_=wt_, j=j, el=el: e.matmul(py[:, el * 64:(el + 1) * 64], lhsT=wt_[:], rhs=x_tok[:, j, el * 64:(el + 1) * 64], start=True, stop=True),
                          [wtres, "s.xtok"], [pyres])
                pint, pintres = proj_ps.next()
                A('pe', lambda e, pint=pint, js=js, g=g: e.matmul(pint[:], lhsT=CT[:, js], rhs=state_bf[:, g, :], start=True, stop=True), ["s.CT", sres + "b"], [pintres])
                ta_, tares = ta[0], "s.ta0"
                tb_, tbres = tb[0], "s.tb0"
                for el in range(8):
                    eh = g * 8 + el
                    A('dve', lambda e, pint=pint, ta_=ta_, el=el, j=j, eh=eh: e.tensor_scalar_mul(out=ta_[:, el * 64:(el + 1) * 64], in0=pint[:, el * 64:(el + 1) * 64],
                                                                                          scalar1=ecum[:, j, eh:eh + 1]), [pintres, "s.ecum"], [tares])
                A('dve', lambda e, py=py, ta_=ta_: e.tensor_tensor(out=ta_[:], in0=ta_[:], in1=py[:], op=ALU.add), [tares, pyres], [tares])
                A('dve', lambda e, tb_=tb_, j=j: e.tensor_tensor(out=tb_[:], in0=x_tok[:, j, :], in1=ssd_ngd_sb[:, 1, :], op=ALU.mult), ["s.xtok", "ssd_ngd"], [tbres])
                A('dve', lambda e, ta_=ta_, tb_=tb_: e.tensor_tensor(out=ta_[:], in0=ta_[:], in1=tb_[:], op=ALU.add), [tares, tbres], [tares])
                A('dve', lambda e, ta_=ta_, j=j: e.tensor_tensor(out=ta_[:], in0=ta_[:], in1=sz[:, j, :], op=ALU.mult), [tares, "s.sz"], [tares])
                s_, s_res = st[q], "s.st%d" % q
                A('dve', lambda e, ta_=ta_, s_=s_: e.bn_stats(out=s_[:, 0:6], in_=ta_[:]), [tares], [s_res])
                A('dve', lambda e, s_=s_: e.bn_aggr(out=s_[:, 6:8], in_=s_[:, 0:6]), [s_res], [s_res])
                A('dve', lambda e, s_=s_: e.scalar_tensor_tensor(out=s_[:, 8:9], in0=s_[:, 6:7], scalar=s_[:, 6:7], in1=s_[:, 7:8], op0=ALU.mult, op1=ALU.add), [s_res], [s_res])
                rstd_from_var(s_[:, 8:9], s_[:, 9:10], s_res, 1e-6)
                A('dve', lambda e, ta_=ta_, s_=s_, q=q: e.scalar_tensor_tensor(out=yb[0][:], in0=ta_[:], scalar=s_[:, 9:10], in1=ssd_ngd_sb[:, 0, :], op0=ALU.mult, op1=ALU.mult),
                  [tares, s_res, "ssd_ngd"], ["s.yb0"])
                transposes(yb[0], "s.yb0", 4, lambda b0, n, j=j, g=g: actT[:, 4 * g + b0:4 * g + b0 + n, j * 128:(j + 1) * 128], "actT")
                xs_, xsres = xs[0], "s.xs0"
                for el in range(8):
                    eh = g * 8 + el
                    A('dve', lambda e, xs_=xs_, el=el, j=j, eh=eh: e.tensor_scalar_mul(out=xs_[:, el * 64:(el + 1) * 64], in0=x_tok[:, j, el * 64:(el + 1) * 64],
                                                                                scalar1=toend[:, j, eh:eh + 1]), ["s.xtok", "s.toend"], [xsres])
                psu, psures = proj_ps.next()
                A('pe', lambda e, psu=psu, xs_=xs_, j=j: e.matmul(psu[:], lhsT=B_tok[:, j, :], rhs=xs_[:], start=True, stop=True), ["s.Btok", xsres], [psures])
                for el in range(8):
                    eh = g * 8 + el
                    A('dve', lambda e, psu=psu, el=el, j=j, eh=eh, g=g: e.scalar_tensor_tensor(out=state[:, g, el * 64:(el + 1) * 64], in0=state[:, g, el * 64:(el + 1) * 64],
                                                                                        scalar=declast[:, j, eh:eh + 1], in1=psu[:, el * 64:(el + 1) * 64],
                                                                                        op0=ALU.mult, op1=ALU.add), [psures, "s.declast", sres], [sres])
                A('act', lambda e, g=g: e.copy(out=state_bf[:, g, :], in_=state[:, g, :]), [sres], [sres + "b"])
        P.fence()
        out_proj(16, ("mix_out", l), l * 3 + 1, 1.0)

    for sq in range(n_seq):
        for tt in range(ntile_seq):
            t0 = sq * seq_len + tt * T
            for j in range(J):
                P.add('sp', lambda e, j=j, t0=t0: e.dma_start(out=h[:, j, :], in_=x_d[t0 + j * 128:t0 + (j + 1) * 128, :]),
                      writes=["h%d" % j], chan="xin%d" % j)
                hbt, hbres = hb_r.next()
                P.add('act', lambda e, hbt=hbt, j=j: e.copy(out=hbt[:], in_=h[:, j, :]), reads=["h%d" % j], writes=[hbres])
                transpose_to_xT(hbt, hbres, j)
            for l in layers:
                for stg in stages:
                    if stg == 'ffn1':
                        ffn(l, 0)
                    elif stg == 'ffn2':
                        ffn(l, 1)
                    elif MIX[l] == 'gla':
                        gla(l, tt == 0)
                    elif MIX[l] == 'ret':
                        ret(l, tt == 0, tt * T)
                    else:
                        ssd(l, tt == 0)
            for j in range(J):
                P.add('sp', lambda e, j=j, t0=t0: e.dma_start(out=y_d[t0 + j * 128:t0 + (j + 1) * 128, :], in_=h[:, j, :]),
                      reads=["h%d" % j], chan="yout%d" % j)
                P.out_chans.add("yout%d" % j)

    P.emit(nc)
    es.close()
    return nc, P


def host_inputs(inputs, n_seq, seq_len):
    f = lambda a: np.ascontiguousarray(np.asarray(a, dtype=np.float32))
    ln_g, ln_b = f(inputs['ln_g']), f(inputs['ln_b'])
    ln_gb = np.empty((DEPTH * 3, 2, 128, D), np.float32)
    for i in range(DEPTH):
        for k in range(3):
            ln_gb[i * 3 + k, 0] = ln_g[i, k][None, :]
            ln_gb[i * 3 + k, 1] = ln_b[i, k][None, :]
    gla_wg = np.concatenate([f(inputs['gla_w_gate']), f(inputs['gla_b_gate'])[:, None, :]], axis=1)
    gla_ng = np.ascontiguousarray(np.broadcast_to(f(inputs['gla_norm_g'])[:, None, :], (2, 128, 256)))
    ret_ng = np.ascontiguousarray(np.broadcast_to(f(inputs['ret_norm_g'])[0][None, :], (128, 512)))
    cw = f(inputs['ssd_conv_w'])[0]
    ssd_cw = np.ascontiguousarray(cw.T.reshape(24, 128, 4).transpose(1, 0, 2))
    ssd_cb = np.ascontiguousarray(f(inputs['ssd_conv_b'])[0].reshape(24, 128).T)
    vec = np.stack([f(inputs['ssd_dt_bias'])[0], f(inputs['ssd_a_log'])[0], f(inputs['ssd_d'])[0]], 0)
    ssd_vec = np.ascontiguousarray(np.broadcast_to(vec[None], (128, 3, 32)))
    ng = f(inputs['ssd_norm_g'])[0].reshape(4, 512)
    dsk = np.repeat(f(inputs['ssd_d'])[0], 64).reshape(4, 512)
    ngd = np.stack([ng, dsk], 1)
    ssd_ngd = np.ascontiguousarray(np.broadcast_to(ngd[:, None], (4, 128, 2, 512)))
    rc, rs_ = make_rope(seq_len)
    return dict(
        ffn_w_in=f(inputs['ffn_w_in']), ffn_w_out=f(inputs['ffn_w_out']), ln_gb=ln_gb, cst=make_consts(),
        gla_w_in=f(inputs['gla_w_in']), gla_wg=np.ascontiguousarray(gla_wg), gla_ng=gla_ng, gla_w_out=f(inputs['gla_w_out']),
        ret_w_in=f(inputs['ret_w_in']), ret_ng=ret_ng, ret_w_out=f(inputs['ret_w_out']), rope_cos=rc, rope_sin=rs_,
        ssd_w_in=f(inputs['ssd_w_in']), ssd_cw=ssd_cw, ssd_cb=ssd_cb, ssd_vec=ssd_vec, ssd_ngd=ssd_ngd, ssd_w_out=f(inputs['ssd_w_out']))


def kernel(**inputs):
    x = np.asarray(inputs['x'], dtype=np.float32)
    B, L, _ = x.shape
    n_seq = B // NCORES
    nc, _ = build_program(n_seq, L)
    shared = host_inputs(inputs, n_seq, L)
    in_maps = []
    for c in range(NCORES):
        m = dict(shared)
        m['x'] = np.ascontiguousarray(x[c * n_seq:(c + 1) * n_seq].reshape(n_seq * L, D))
        in_maps.append(m)
    res = run_bass_kernel_spmd(nc, in_maps, core_ids=list(range(NCORES)))
    out = np.concatenate([np.asarray(r["y"]).reshape(n_seq, L, D) for r in res.results], axis=0)
    return out.astype(np.float32)
```
